# Optimizing a Trainium2 kernel written in Bass

```python
import math
import jax, jax.numpy as jnp
from jax import lax
import numpy as np

D_MODEL = 4096
BATCH = 2
SEQ = 4096
DEPTH = 2

N_EVEN = (DEPTH + 1) // 2
N_ODD = DEPTH // 2
ROPE_THETA = 10000.0
NORM_EPS = 1e-6
MOBA_HEADS = 16
MOBA_HEAD_DIM = 128
MOBA_BLOCK = 256
MOBA_TOPK = 3
MOBA_Q_CHUNK = 32
RET_HEADS = 8
RET_KEY_DIM = 256
RET_VAL_DIM = 512
RET_CHUNK = 128
CONV_CH = D_MODEL // 2
CONV_WIDTH = 31
RWKV_HEAD_DIM = 64
RWKV_DIM = D_MODEL // 2
RWKV_HEADS = RWKV_DIM // RWKV_HEAD_DIM
DECAY_LORA = max(32, int(round(1.8 * RWKV_DIM ** 0.5 / 32)) * 32)
ICLR_LORA = max(32, int(round(1.8 * RWKV_DIM ** 0.5 / 32)) * 32)
GATE_LORA = max(32, int(round(0.6 * RWKV_DIM ** 0.8 / 32)) * 32)
RWKV_LNX_EPS = 64e-5
FFN_HIDDEN = -(-8 * D_MODEL // (3 * 256)) * 256
EVEN_IN = 3 * MOBA_HEADS * MOBA_HEAD_DIM + 2 * RET_HEADS * RET_KEY_DIM + 2 * RET_HEADS * RET_VAL_DIM
EVEN_OUT = MOBA_HEADS * MOBA_HEAD_DIM + RET_HEADS * RET_VAL_DIM
RWKV_IN = 3 * RWKV_DIM + DECAY_LORA + ICLR_LORA + GATE_LORA
ODD_IN = 2 * CONV_CH + RWKV_IN
ODD_OUT = CONV_CH + RWKV_DIM

kernel_name = 'hybrid_moba_retention_conformer_rwkv7_block'


def split_cols(t, sizes):
    offs = np.cumsum(sizes)[:-1].tolist()
    return jnp.split(t, offs, axis=-1)


def rms_norm(x, g):
    xf = x.astype(jnp.float32)
    y = xf * lax.rsqrt(jnp.mean(xf * xf, axis=-1, keepdims=True) + NORM_EPS)
    return (y * g.astype(jnp.float32)).astype(x.dtype)


def layer_norm(x, g, b, eps):
    xf = x.astype(jnp.float32)
    mu = jnp.mean(xf, axis=-1, keepdims=True)
    var = jnp.mean(jnp.square(xf - mu), axis=-1, keepdims=True)
    return (xf - mu) * lax.rsqrt(var + eps) * g.astype(jnp.float32) + b.astype(jnp.float32)


def modulation(c, w_ada, b_ada):
    m = jax.nn.silu(c) @ w_ada + b_ada
    return [t[:, None, :] for t in jnp.split(m, 6, axis=-1)]


def rope_tables(seq, dim):
    inv = 1.0 / (ROPE_THETA ** (jnp.arange(0, dim, 2, dtype=jnp.float32) / dim))
    ang = jnp.arange(seq, dtype=jnp.float32)[:, None] * inv[None, :]
    return jnp.cos(ang), jnp.sin(ang)


def apply_rope(x, cos, sin):
    half = x.shape[-1] // 2
    x1, x2 = x[..., :half], x[..., half:]
    return jnp.concatenate([x1 * cos - x2 * sin, x2 * cos + x1 * sin], axis=-1).astype(x.dtype)


def to_heads(t, n_heads):
    B, S, _ = t.shape
    return t.reshape(B, S, n_heads, -1).transpose(0, 2, 1, 3)


def from_heads(t):
    B, H, S, Dh = t.shape
    return t.transpose(0, 2, 1, 3).reshape(B, S, H * Dh)


def moba_attention(q, k, v):
    B, H, S, Dh = q.shape
    nb = -(-S // MOBA_BLOCK)
    pad = nb * MOBA_BLOCK - S
    kp = jnp.pad(k, ((0, 0), (0, 0), (0, pad), (0, 0)))
    vp = jnp.pad(v, ((0, 0), (0, 0), (0, pad), (0, 0)))
    k_blocks = kp.reshape(B, H, nb, MOBA_BLOCK, Dh)
    v_blocks = vp.reshape(B, H, nb, MOBA_BLOCK, Dh)
    k_mean = jnp.mean(k_blocks.astype(jnp.float32), axis=3)
    topk = min(MOBA_TOPK, nb)
    n_sel = topk * MOBA_BLOCK
    scale = Dh ** -0.5
    n_chunks = S // MOBA_Q_CHUNK
    q_chunks = q.reshape(B, H, n_chunks, MOBA_Q_CHUNK, Dh).transpose(2, 0, 1, 3, 4)
    gather = jax.vmap(jax.vmap(lambda blocks, idx: blocks[idx]))

    def attend_chunk(args):
        ci, qc = args
        start = ci * MOBA_Q_CHUNK
        own = start // MOBA_BLOCK
        q_pos = start + jnp.arange(MOBA_Q_CHUNK)
        gate = jnp.einsum('bhqd,bhnd->bhqn', qc.astype(jnp.float32), k_mean)
        gate = jnp.where(jnp.arange(nb) < own, gate, -jnp.inf)
        g_val, g_idx = lax.top_k(gate, topk)
        sel_ok = jnp.isfinite(g_val)
        k_sel = gather(k_blocks, g_idx)
        v_sel = gather(v_blocks, g_idx)
        s_sel = jnp.einsum('bhqd,bhqtkd->bhqtk', qc, k_sel).astype(jnp.float32) * scale
        s_sel = jnp.where(sel_ok[..., None], s_sel, -jnp.inf)
        k_own = lax.dynamic_slice_in_dim(kp, own * MOBA_BLOCK, MOBA_BLOCK, axis=2)
        v_own = lax.dynamic_slice_in_dim(vp, own * MOBA_BLOCK, MOBA_BLOCK, axis=2)
        s_own = jnp.einsum('bhqd,bhkd->bhqk', qc, k_own).astype(jnp.float32) * scale
        own_pos = own * MOBA_BLOCK + jnp.arange(MOBA_BLOCK)
        s_own = jnp.where(own_pos[None, :] <= q_pos[:, None], s_own, -jnp.inf)
        scores = jnp.concatenate([s_sel.reshape(B, H, MOBA_Q_CHUNK, n_sel), s_own], axis=-1)
        p = jax.nn.softmax(scores, axis=-1)
        p_sel = p[..., :n_sel].reshape(B, H, MOBA_Q_CHUNK, topk, MOBA_BLOCK)
        out = (jnp.einsum('bhqtk,bhqtkd->bhqd', p_sel, v_sel.astype(jnp.float32))
               + jnp.einsum('bhqk,bhkd->bhqd', p[..., n_sel:], v_own.astype(jnp.float32)))
        return out.astype(q.dtype)

    out = lax.map(attend_chunk, (jnp.arange(n_chunks), q_chunks))
    return out.transpose(1, 2, 0, 3, 4).reshape(B, H, S, Dh)


def retention(q, k, v):
    B, H, S, Dk = q.shape
    Dv = v.shape[-1]
    C = RET_CHUNK
    nc = S // C
    log_g = jnp.log1p(-jnp.exp2(-5.0 - jnp.arange(H, dtype=jnp.float32)))
    idx = jnp.arange(C, dtype=jnp.float32)
    diff = idx[:, None] - idx[None, :]
    decay_mask = jnp.where(diff >= 0, jnp.exp(jnp.maximum(diff, 0.0) * log_g[:, None, None]), 0.0)
    q_decay = jnp.exp((idx + 1.0) * log_g[:, None])[..., None]
    k_decay = jnp.exp((C - 1.0 - idx) * log_g[:, None])[..., None]
    chunk_decay = jnp.exp(C * log_g)[:, None, None]
    qf = q.astype(jnp.float32)
    kf = k.astype(jnp.float32) * (Dk ** -0.5)
    vf = v.astype(jnp.float32)
    chunks = lambda t: t.reshape(B, H, nc, C, t.shape[-1]).transpose(2, 0, 1, 3, 4)

    def step(state, xs):
        qc, kc, vc = xs
        inner = jnp.einsum('bhid,bhjd->bhij', qc, kc) * decay_mask
        o = (jnp.einsum('bhij,bhjv->bhiv', inner, vc)
             + jnp.einsum('bhid,bhdv->bhiv', qc, state) * q_decay)
        state = state * chunk_decay + jnp.einsum('bhjd,bhjv->bhdv', kc * k_decay, vc)
        return state, o

    state0 = jnp.zeros((B, H, Dk, Dv), jnp.float32)
    _, o = lax.scan(step, state0, (chunks(qf), chunks(kf), chunks(vf)))
    return o.transpose(1, 2, 0, 3, 4).reshape(B, H, S, Dv)


def even_mixer(h, w_in, w_out, rope_m, rope_r):
    dm = MOBA_HEADS * MOBA_HEAD_DIM
    dk = RET_HEADS * RET_KEY_DIM
    dv = RET_HEADS * RET_VAL_DIM
    q_m, k_m, v_m, q_r, k_r, v_r, g_r = split_cols(h @ w_in, [dm, dm, dm, dk, dk, dv, dv])
    o_m = moba_attention(apply_rope(to_heads(q_m, MOBA_HEADS), *rope_m),
                         apply_rope(to_heads(k_m, MOBA_HEADS), *rope_m),
                         to_heads(v_m, MOBA_HEADS))
    o_r = retention(apply_rope(to_heads(q_r, RET_HEADS), *rope_r),
                    apply_rope(to_heads(k_r, RET_HEADS), *rope_r),
                    to_heads(v_r, RET_HEADS))
    o_r = o_r * lax.rsqrt(jnp.mean(o_r * o_r, axis=-1, keepdims=True) + NORM_EPS)
    o_r = from_heads(o_r) * jax.nn.silu(g_r.astype(jnp.float32))
    merged = jnp.concatenate([from_heads(o_m).astype(jnp.float32), o_r], axis=-1)
    return (merged.astype(h.dtype) @ w_out).astype(h.dtype)


def causal_depthwise_conv(u, w, b):
    K, C = w.shape
    out = lax.conv_general_dilated(u, w[:, None, :].astype(u.dtype), window_strides=(1,),
                                   padding=[(K - 1, 0)], dimension_numbers=('NWC', 'WIO', 'NWC'),
                                   feature_group_count=C)
    return out + b


def token_shift(t, mu):
    prev = jnp.pad(t, ((0, 0), (1, 0), (0, 0)))[:, :-1]
    return t + (prev - t) * mu


def rwkv7_scan(r, w, k, v, a, b):
    B, S, H, N = r.shape
    xs = tuple(t.astype(jnp.float32).transpose(1, 0, 2, 3) for t in (r, w, k, v, a, b))

    def step(state, inp):
        r_t, w_t, k_t, v_t, a_t, b_t = inp
        sa = jnp.einsum('bhvk,bhk->bhv', state, a_t)
        state = (state * w_t[:, :, None, :] + sa[..., None] * b_t[:, :, None, :]
                 + v_t[..., None] * k_t[:, :, None, :])
        return state, jnp.einsum('bhvk,bhk->bhv', state, r_t)

    state0 = jnp.zeros((B, H, N, N), jnp.float32)
    _, y = lax.scan(step, state0, xs)
    return y.transpose(1, 0, 2, 3)


def odd_mixer(h, w_in, w_out, conv_w, conv_b, conv_ln_g, conv_ln_b, rwkv_mu, rwkv_w0, rwkv_w_up,
              rwkv_a0, rwkv_a_up, rwkv_g_up, rwkv_k_k, rwkv_k_a, rwkv_r_k, rwkv_lnx_g, rwkv_lnx_b):
    B, S, _ = h.shape
    conv_a, conv_gate, rw = split_cols(h @ w_in, [CONV_CH, CONV_CH, RWKV_IN])
    u = conv_a * jax.nn.sigmoid(conv_gate)
    u = causal_depthwise_conv(u, conv_w, conv_b)
    u = jax.nn.silu(layer_norm(u, conv_ln_g, conv_ln_b, 1e-5))
    rw = token_shift(rw, rwkv_mu)
    r, k, v, xw, xa, xg = split_cols(rw, [RWKV_DIM, RWKV_DIM, RWKV_DIM, DECAY_LORA, ICLR_LORA, GATE_LORA])
    w_log = -jax.nn.softplus(-(rwkv_w0 + jnp.tanh(xw) @ rwkv_w_up)) - 0.5
    decay = jnp.exp(-jnp.exp(w_log.astype(jnp.float32)))
    a = jax.nn.sigmoid(rwkv_a0 + xa @ rwkv_a_up)
    g = jax.nn.sigmoid(xg) @ rwkv_g_up
    heads = lambda t: t.reshape(B, S, RWKV_HEADS, RWKV_HEAD_DIM)
    kk = heads((k * rwkv_k_k).astype(jnp.float32))
    kk = kk / jnp.maximum(jnp.sqrt(jnp.sum(kk * kk, axis=-1, keepdims=True)), 1e-12)
    k = k * (1.0 + (a - 1.0) * rwkv_k_a)
    rh, kh, vh, ah = heads(r), heads(k), heads(v), heads(a)
    y = rwkv7_scan(rh, heads(decay), kh, vh, -kk, kk * ah)
    y = layer_norm(y, rwkv_lnx_g.reshape(RWKV_HEADS, RWKV_HEAD_DIM),
                   rwkv_lnx_b.reshape(RWKV_HEADS, RWKV_HEAD_DIM), RWKV_LNX_EPS)
    bonus = jnp.sum((rh * kh * rwkv_r_k).astype(jnp.float32), axis=-1, keepdims=True) * vh
    y = (y + bonus).reshape(B, S, RWKV_DIM) * g
    merged = jnp.concatenate([u, y.astype(jnp.float32)], axis=-1)
    return (merged.astype(h.dtype) @ w_out).astype(h.dtype)


def swiglu(h, w_in, w_out):
    gate, up = jnp.split(h @ w_in, 2, axis=-1)
    return (jax.nn.silu(gate) * up) @ w_out


def setup_inputs(seed: int = 0) -> dict:
    key = jax.random.key(seed)
    keys = iter(jax.random.split(key, 32))
    f32 = jnp.float32

    def nrm(shape, scale):
        return jax.random.normal(next(keys), shape, f32) * scale

    def unif(shape, lo, hi):
        return jax.random.uniform(next(keys), shape, f32, lo, hi)

    D = D_MODEL
    return {
        'x': nrm((BATCH, SEQ, D), 1.0),
        'c': nrm((BATCH, D), 1.0),
        'w_ada': nrm((DEPTH, D, 6 * D), 0.5 * D ** -0.5),
        'b_ada': nrm((DEPTH, 6 * D), 0.02),
        'norm_g': 1.0 + nrm((DEPTH, 4, D), 0.02),
        'w_ffn_in': nrm((DEPTH, D, 2 * FFN_HIDDEN), D ** -0.5),
        'w_ffn_out': nrm((DEPTH, FFN_HIDDEN, D), FFN_HIDDEN ** -0.5),
        'even_w_in': nrm((N_EVEN, D, EVEN_IN), D ** -0.5),
        'even_w_out': nrm((N_EVEN, EVEN_OUT, D), EVEN_OUT ** -0.5),
        'odd_w_in': nrm((N_ODD, D, ODD_IN), D ** -0.5),
        'odd_w_out': nrm((N_ODD, ODD_OUT, D), ODD_OUT ** -0.5),
        'conv_w': nrm((N_ODD, CONV_WIDTH, CONV_CH), CONV_WIDTH ** -0.5),
        'conv_b': nrm((N_ODD, CONV_CH), 0.02),
        'conv_ln_g': 1.0 + nrm((N_ODD, CONV_CH), 0.02),
        'conv_ln_b': nrm((N_ODD, CONV_CH), 0.02),
        'rwkv_mu': unif((N_ODD, RWKV_IN), 0.0, 1.0),
        'rwkv_w0': unif((N_ODD, RWKV_DIM), -6.0, -0.5),
        'rwkv_w_up': nrm((N_ODD, DECAY_LORA, RWKV_DIM), 0.5 * DECAY_LORA ** -0.5),
        'rwkv_a0': nrm((N_ODD, RWKV_DIM), 0.5),
        'rwkv_a_up': nrm((N_ODD, ICLR_LORA, RWKV_DIM), ICLR_LORA ** -0.5),
        'rwkv_g_up': nrm((N_ODD, GATE_LORA, RWKV_DIM), GATE_LORA ** -0.5),
        'rwkv_k_k': 0.85 + nrm((N_ODD, RWKV_DIM), 0.05),
        'rwkv_k_a': 1.0 + nrm((N_ODD, RWKV_DIM), 0.05),
        'rwkv_r_k': nrm((N_ODD, RWKV_HEADS, RWKV_HEAD_DIM), 0.1),
        'rwkv_lnx_g': 1.0 + nrm((N_ODD, RWKV_DIM), 0.02),
        'rwkv_lnx_b': nrm((N_ODD, RWKV_DIM), 0.02),
    }


def reference(x, c, w_ada, b_ada, norm_g, w_ffn_in, w_ffn_out, even_w_in, even_w_out, odd_w_in,
              odd_w_out, conv_w, conv_b, conv_ln_g, conv_ln_b, rwkv_mu, rwkv_w0, rwkv_w_up, rwkv_a0,
              rwkv_a_up, rwkv_g_up, rwkv_k_k, rwkv_k_a, rwkv_r_k, rwkv_lnx_g, rwkv_lnx_b):
    S = x.shape[1]
    rope_m = rope_tables(S, MOBA_HEAD_DIM)
    rope_r = rope_tables(S, RET_KEY_DIM)
    for layer in range(DEPTH):
        sh_m, sc_m, g_m, sh_f, sc_f, g_f = modulation(c, w_ada[layer], b_ada[layer])
        h = rms_norm(x, norm_g[layer, 0]) * (1.0 + sc_m) + sh_m
        j = layer // 2
        if layer % 2 == 0:
            o = even_mixer(h, even_w_in[j], even_w_out[j], rope_m, rope_r)
        else:
            o = odd_mixer(h, odd_w_in[j], odd_w_out[j], conv_w[j], conv_b[j], conv_ln_g[j],
                          conv_ln_b[j], rwkv_mu[j], rwkv_w0[j], rwkv_w_up[j], rwkv_a0[j],
                          rwkv_a_up[j], rwkv_g_up[j], rwkv_k_k[j], rwkv_k_a[j], rwkv_r_k[j],
                          rwkv_lnx_g[j], rwkv_lnx_b[j])
        x = x + g_m * rms_norm(o, norm_g[layer, 1])
        h = rms_norm(x, norm_g[layer, 2]) * (1.0 + sc_f) + sh_f
        x = x + g_f * rms_norm(swiglu(h, w_ffn_in[layer], w_ffn_out[layer]), norm_g[layer, 3])
    return x
```

```python
from contextlib import ExitStack
import numpy as np
import concourse.bass as bass
import concourse.mybir as mybir
from concourse.bass_utils import run_bass_kernel_spmd

F32 = mybir.dt.float32
BF16 = mybir.dt.bfloat16
AF = mybir.ActivationFunctionType
ALU = mybir.AluOpType
AX = mybir.AxisListType

ENG_ATTR = {'pe': 'tensor', 'act': 'scalar', 'dve': 'vector', 'pool': 'gpsimd', 'sp': 'sync'}


class Prog:
    def __init__(self, nc, same_engine_sync=False):
        self.nc = nc
        self.es = ExitStack()
        self.ops = {e: [] for e in ENG_ATTR}
        self.sem = {}
        self.cnt = {}
        self.waited = {e: {} for e in ENG_ATTR}
        self.last_w = {}
        self.readers = {}
        self.same = same_engine_sync
        for e in ENG_ATTR:
            self.sem[e] = self.es.enter_context(nc.semaphore('s_' + e))
            self.cnt[e] = 0
        self.n_ops = 0
        self.pending = {e: [] for e in ENG_ATTR}
        self.phase_es = None

    def sb(self, name, shape, dt):
        es = self.phase_es if self.phase_es is not None else self.es
        self.n_sb = getattr(self, 'n_sb', 0) + 1
        return es.enter_context(self.nc.sbuf_tensor('%s_u%d' % (name, self.n_sb), shape, dt))

    def barrier(self):
        cur = [(s, c) for s, c in self.cnt.items() if c > 0]
        for e in ENG_ATTR:
            self.pending[e] = list(cur)

    def phase_begin(self):
        self.phase_es = ExitStack()

    def phase_end(self):
        self.barrier()
        self.phase_es.close()
        self.phase_es = None
        for lane in self.__dict__.get('phase_lanes', []):
            k = self.lane_map.pop(lane, None)
            if k is not None:
                self.lane_free.append(k)
        self.phase_lanes = []

    def ps(self, name, shape, dt=F32):
        return self.es.enter_context(self.nc.psum_tensor(name, shape, dt))

    def _lane(self, lane):
        m = self.__dict__.setdefault('lane_map', {})
        if lane in m:
            return m[lane]
        free = self.__dict__.setdefault('lane_free', [])
        if self.phase_es is not None and free:
            k = free.pop()
        else:
            k = 'L_%d' % len([x for x in self.sem if x.startswith('L_')])
            self.sem[k] = self.__dict__.setdefault('sem_es', ExitStack()).enter_context(self.nc.semaphore(k))
            self.cnt[k] = 0
        m[lane] = k
        if self.phase_es is not None:
            self.__dict__.setdefault('phase_lanes', []).append(lane)
        return k

    def op(self, eng, fn, reads=(), writes=(), lane=None, force=False, lane_inc=16):
        deps = []
        for r in reads:
            if r in self.last_w:
                s_, v_, e_ = self.last_w[r]
                deps.append((s_, v_, e_ if eng == 'pe' else None))
        for w in writes:
            if w in self.last_w:
                deps.append(self.last_w[w])
            deps.extend(self.readers.get(w, {}).values())
        waits = []
        wd = self.waited[eng]
        for (s, v) in self.pending[eng]:
            if wd.get(s, 0) < v:
                wd[s] = v
                waits.append((s, v))
        self.pending[eng] = []
        for (s, v, e2) in deps:
            if e2 == eng and lane is None and not self.same and not force and not s.startswith('L_'):
                continue
            if wd.get(s, 0) >= v:
                continue
            wd[s] = v
            waits.append((s, v))
        if lane is not None:
            s = self._lane(lane)
            self.cnt[s] += lane_inc
            tok = (s, self.cnt[s], eng)
            inc = (s, lane_inc)
        else:
            self.cnt[eng] += 1
            tok = (eng, self.cnt[eng], eng)
            inc = (eng, 1)
        self.ops[eng].append((waits, fn, inc))
        for w in writes:
            self.last_w[w] = tok
            self.readers[w] = {}
        for r in reads:
            self.readers.setdefault(r, {})[tok[0]] = tok
        self.n_ops += 1
        return tok

    def finish(self):
        nc = self.nc
        final = [(s, c) for s, c in self.cnt.items() if c > 0]
        with nc.Block() as block:
            def mk(eng):
                def body(e):
                    for waits, fn, inc in self.ops[eng]:
                        for s, v in waits:
                            e.wait_ge(self.sem[s], v)
                        ins = fn(e)
                        ins.then_inc(self.sem[inc[0]], inc[1])
                    if eng == 'sp':
                        for s, c in final:
                            e.wait_ge(self.sem[s], c)
                return body
            for eng, attr in ENG_ATTR.items():
                getattr(block, attr)(mk(eng))
        self.es.close()
        if 'sem_es' in self.__dict__:
            self.sem_es.close()


def chunked(v, nchunk):
    return np.ascontiguousarray(np.asarray(v, np.float32).reshape(nchunk, 128).T)


class Common:
    def __init__(self, P, D, TT):
        self.P = P
        self.KC = D // 128
        self.TT = TT
        self.D = D
        nc = P.nc
        self.ones_f = P.sb('ones_f', [128, 128], F32)
        self.ones_b = P.sb('ones_b', [128, 128], BF16)
        P.op('dve', lambda e: e.memset(self.ones_f[:], 1.0), writes=['ones_f'])
        P.op('dve', lambda e: e.memset(self.ones_b[:], 1.0), writes=['ones_b'])
        self.psum_all = P.ps('psum_all', [128, 4096], F32)
        self.psum = [self.psum_all[:, i * 512:(i + 1) * 512] for i in range(8)]
        self.ps_i = 0

    def bank(self):
        i = self.ps_i
        self.ps_i = (self.ps_i + 1) % 8
        return i


def emit_norm_mod(P, C, x_dram, tok0, h, hkey, gs, sh, xs, eps, sq, rstd, tagp):
    KC, TT = C.KC, C.TT
    nh = TT // 512
    banks = [C.bank() for _ in range(nh)]
    xget = x_dram if callable(x_dram) else (lambda k, t0, n: x_dram[k, :, t0:t0 + n])
    for k in range(KC):
        xt = xs[k % 2]
        xk = 'xs%d' % (k % 2)
        P.op('sp', lambda e, xt=xt, k=k: e.dma_start(out=xt[:], in_=xget(k, tok0, TT)),
             writes=[xk], lane=xk)
        P.op('act', lambda e, xt=xt: e.activation(out=sq[:], in_=xt[:], func=AF.Square),
             reads=[xk], writes=['sq'])
        for hh in range(nh):
            P.op('pe', lambda e, hh=hh, k=k: e.matmul(C.psum[banks[hh]][:], C.ones_f[:], sq[:, hh * 512:(hh + 1) * 512],
                                                     start=(k == 0), stop=(k == KC - 1)),
                 reads=['sq', 'ones_f'], writes=['ps%d' % banks[hh]])
    for hh in range(nh):
        sl = slice(hh * 512, (hh + 1) * 512)
        P.op('act', lambda e, hh=hh, sl=sl: e.activation(out=rstd[:, sl], in_=C.psum[banks[hh]][:], func=AF.Sqrt,
                                                        scale=1.0 / C.D, bias=eps[:, 0:1]),
             reads=['ps%d' % banks[hh], tagp + 'eps'], writes=['rstd'])
    P.op('dve', lambda e: e.reciprocal(out=rstd[:], in_=rstd[:]), reads=['rstd'], writes=['rstd'])
    for k in range(KC):
        xt = xs[k % 2]
        xk = 'xs%d' % (k % 2)
        P.op('sp', lambda e, xt=xt, k=k: e.dma_start(out=xt[:], in_=xget(k, tok0, TT)),
             writes=[xk], lane=xk)
        P.op('dve', lambda e, xt=xt: e.tensor_tensor(out=xt[:], in0=xt[:], in1=rstd[:], op=ALU.mult),
             reads=[xk, 'rstd'], writes=[xk])
        P.op('pool', lambda e, xt=xt, k=k: e.tensor_scalar(out=h[:, k, :], in0=xt[:], scalar1=gs[:, k:k + 1],
                                                          scalar2=sh[:, k:k + 1], op0=ALU.mult, op1=ALU.add),
             reads=[xk, tagp + 'gs', tagp + 'sh'], writes=[hkey])


class WStream:
    def __init__(self, P, name, nslot, width):
        self.P = P
        self.name = name
        self.nslot = nslot
        self.slots = [P.sb('%s_w%d' % (name, i), [128, width], BF16) for i in range(nslot)]
        self.reqs = []
        self.issued = 0

    def plan(self, reqs):
        self.reqs = self.reqs + list(reqs)

    def _issue(self, i):
        ap, ncols = self.reqs[i]
        sl = self.slots[i % self.nslot]
        key = '%s_w%d' % (self.name, i % self.nslot)
        self.P.op('pool', lambda e: e.dma_start(out=sl[:, 0:ncols], in_=ap), writes=[key], lane=key)

    def get(self, i, oldest=None):
        if oldest is None:
            oldest = i
        while self.issued < min(len(self.reqs), oldest + self.nslot):
            self._issue(self.issued)
            self.issued += 1
        return self.slots[i % self.nslot], '%s_w%d' % (self.name, i % self.nslot)


def load_vec(P, name, dram, ncol, eng='sp'):
    t = P.sb(name, [128, ncol], F32)
    P.op(eng, lambda e: e.dma_start(out=t[:], in_=dram), writes=[name], lane=name)
    return t


def build_ffn(D, S, HC, TT=1024):
    nc = bass.Bass("TRN2", target_bir_lowering=False)
    KC = D // 128
    x = nc.dram_tensor("x", [KC, 128, S], F32, kind="ExternalInput").ap()
    ng = nc.dram_tensor("ng", [128, KC], F32, kind="ExternalInput").ap()
    sc = nc.dram_tensor("sc", [128, KC], F32, kind="ExternalInput").ap()
    shf = nc.dram_tensor("shf", [128, KC], F32, kind="ExternalInput").ap()
    w1t = nc.dram_tensor("w1t", [2 * HC, 128, KC * 128], F32, kind="ExternalInput").ap()
    w2t = nc.dram_tensor("w2t", [KC, 128, HC * 128], F32, kind="ExternalInput").ap()
    y = nc.dram_tensor("y", [KC, 128, S], F32, kind="ExternalOutput").ap()
    P = Prog(nc)
    C = Common(P, D, TT)
    dbg = None
    if DEBUG:
        dbg = (nc.dram_tensor("dbg_h", [KC, 128, S], F32, kind="ExternalOutput").ap(),
               nc.dram_tensor("dbg_a", [HC, 128, S], F32, kind="ExternalOutput").ap())
    emit_ffn(P, C, x, ng, sc, shf, w1t, w2t, y, S, HC, dbg=dbg)
    P.finish()
    return nc


def emit_gs(P, ngt, sct, KC, name):
    gs = P.sb(name, [128, KC], F32)
    P.op('dve', lambda e: e.tensor_scalar(out=gs[:], in0=sct[:], scalar1=1.0, scalar2=None, op0=ALU.add),
         reads=[name + '_sc'], writes=[name])
    P.op('dve', lambda e: e.tensor_tensor(out=gs[:], in0=gs[:], in1=ngt[:], op=ALU.mult),
         reads=[name + '_ng'], writes=[name])
    return gs


DEBUG = False


def emit_ffn(P, C, x, ng, sc, shf, w1t, w2t, y, S, HC, pfx='f', dbg=None):
    KC, TT = C.KC, C.TT
    nh = TT // 512
    ngt = load_vec(P, pfx + 'gs_ng', ng, KC)
    sct = load_vec(P, pfx + 'gs_sc', sc, KC)
    sht = load_vec(P, pfx + 'sh', shf, KC)
    gs = emit_gs(P, ngt, sct, KC, pfx + 'gs')
    eps = P.sb(pfx + 'eps', [128, 1], F32)
    P.op('dve', lambda e: e.memset(eps[:], 1e-6), writes=[pfx + 'eps'])
    h = P.sb(pfx + 'h', [128, KC, TT], BF16)
    actb = P.sb(pfx + 'actb', [128, HC, TT], BF16)
    xs = [P.sb(pfx + 'xs%d' % i, [128, TT], F32) for i in range(2)]
    sq = P.sb(pfx + 'sq', [128, TT], F32)
    rstd = P.sb(pfx + 'rstd', [128, TT], F32)
    sg = [P.sb(pfx + 'sg%d' % i, [128, 512], F32) for i in range(2)]
    ost = [P.sb(pfx + 'ost%d' % i, [128, 512], F32) for i in range(4)]
    ws = WStream(P, pfx + 'ws', 4, max(KC, HC) * 128)
    ntile = S // TT
    reqs = []
    for t in range(ntile):
        for i in range(2 * HC):
            reqs.append((w1t[i], KC * 128))
        for m in range(KC):
            reqs.append((w2t[m], HC * 128))
    ws.plan(reqs)
    wi = 0
    sgi = 0
    oi = 0
    for t in range(ntile):
        tok0 = t * TT
        emit_norm_mod(P, C, x, tok0, h, pfx + 'h', gs, sht, xs, eps, sq, rstd, pfx)
        for hc in range(HC):
            wg, wgk = ws.get(wi)
            wu, wuk = ws.get(wi + 1, oldest=wi)
            wi += 2
            for hh in range(nh):
                sl = slice(hh * 512, (hh + 1) * 512)
                bg = C.bank()
                bu = C.bank()
                for k in range(KC):
                    P.op('pe', lambda e, k=k, sl=sl, bg=bg, wg=wg: e.matmul(
                        C.psum[bg][:], wg[:, k * 128:(k + 1) * 128], h[:, k, sl], start=(k == 0), stop=(k == KC - 1)),
                        reads=[wgk, pfx + 'h'], writes=['ps%d' % bg])
                for k in range(KC):
                    P.op('pe', lambda e, k=k, sl=sl, bu=bu, wu=wu: e.matmul(
                        C.psum[bu][:], wu[:, k * 128:(k + 1) * 128], h[:, k, sl], start=(k == 0), stop=(k == KC - 1)),
                        reads=[wuk, pfx + 'h'], writes=['ps%d' % bu])
                sgt = sg[sgi % 2]
                sgk = pfx + 'sg%d' % (sgi % 2)
                sgi += 1
                P.op('act', lambda e, sgt=sgt, bg=bg: e.activation(out=sgt[:], in_=C.psum[bg][:], func=AF.Silu),
                     reads=['ps%d' % bg], writes=[sgk])
                P.op('dve', lambda e, sgt=sgt, bu=bu, hc=hc, sl=sl: e.tensor_tensor(
                    out=actb[:, hc, sl], in0=sgt[:], in1=C.psum[bu][:], op=ALU.mult),
                    reads=[sgk, 'ps%d' % bu], writes=[pfx + 'actb'])
        if dbg is not None:
            for k in range(KC):
                P.op('pool', lambda e, k=k, tok0=tok0: e.dma_start(out=dbg[0][k, :, tok0:tok0 + TT], in_=h[:, k, :]),
                     reads=[pfx + 'h'], lane='dbgh')
            for hc in range(HC):
                P.op('pool', lambda e, hc=hc, tok0=tok0: e.dma_start(out=dbg[1][hc, :, tok0:tok0 + TT], in_=actb[:, hc, :]),
                     reads=[pfx + 'actb'], lane='dbga')
        for m in range(KC):
            w2, w2k = ws.get(wi)
            wi += 1
            for hh in range(nh):
                sl = slice(hh * 512, (hh + 1) * 512)
                bo = C.bank()
                for hc in range(HC):
                    P.op('pe', lambda e, hc=hc, sl=sl, bo=bo, w2=w2: e.matmul(
                        C.psum[bo][:], w2[:, hc * 128:(hc + 1) * 128], actb[:, hc, sl], start=(hc == 0), stop=(hc == HC - 1)),
                        reads=[w2k, pfx + 'actb'], writes=['ps%d' % bo])
                o = ost[oi % 4]
                ok = pfx + 'ost%d' % (oi % 4)
                oi += 1
                P.op('act', lambda e, o=o, bo=bo: e.copy(out=o[:], in_=C.psum[bo][:]),
                     reads=['ps%d' % bo], writes=[ok])
                yput = y if callable(y) else (lambda m_, t0_, n_: y[m_, :, t0_:t0_ + n_])
                P.op('sp', lambda e, o=o, m=m, hh=hh, tok0=tok0, yput=yput: e.dma_start(
                    out=yput(m, tok0 + hh * 512, 512), in_=o[:]),
                    reads=[ok], lane=ok + '_st')


def build_reduce(D, TS, G):
    nc = bass.Bass("TRN2", target_bir_lowering=False)
    KC = D // 128
    part = nc.dram_tensor("part", [G, KC, 128, TS], F32, kind="ExternalInput").ap()
    x = nc.dram_tensor("x", [KC, 128, TS], F32, kind="ExternalInput").ap()
    ng = nc.dram_tensor("ng", [128, KC], F32, kind="ExternalInput").ap()
    gate = nc.dram_tensor("gate", [128, KC], F32, kind="ExternalInput").ap()
    xo = nc.dram_tensor("xo", [KC, 128, TS], F32, kind="ExternalOutput").ap()
    P = Prog(nc)
    C = Common(P, D, 512)
    emit_reduce(P, C, part, x, ng, gate, xo, TS, G)
    P.finish()
    return nc


def emit_reduce(P, C, part, x, ng, gate, xo, TS, G, pfx='r'):
    KC = C.KC
    ngt = load_vec(P, pfx + 'ng', ng, KC)
    gt = load_vec(P, pfx + 'gate', gate, KC)
    gg = P.sb(pfx + 'gg', [128, KC], F32)
    P.op('dve', lambda e: e.tensor_tensor(out=gg[:], in0=ngt[:], in1=gt[:], op=ALU.mult),
         reads=[pfx + 'ng', pfx + 'gate'], writes=[pfx + 'gg'])
    eps = P.sb(pfx + 'eps', [128, 1], F32)
    P.op('dve', lambda e: e.memset(eps[:], 1e-6), writes=[pfx + 'eps'])
    osum = P.sb(pfx + 'osum', [128, KC, 512], F32)
    pst = [P.sb(pfx + 'pst%d' % i, [128, 512], F32) for i in range(4)]
    xst = [P.sb(pfx + 'xst%d' % i, [128, 512], F32) for i in range(2)]
    sq = P.sb(pfx + 'sq', [128, 512], F32)
    rstd = P.sb(pfx + 'rstd', [128, 512], F32)
    pi = 0
    for hh in range(TS // 512):
        sl = slice(hh * 512, (hh + 1) * 512)
        bk = C.bank()
        for k in range(KC):
            for g in range(G):
                if g == 0:
                    P.op('sp', lambda e, k=k, g=g, sl=sl: e.dma_start(out=osum[:, k, :], in_=part[g, k, :, sl]),
                         writes=[pfx + 'osum%d' % k], lane=pfx + 'osum%d' % k)
                else:
                    st = pst[pi % 4]
                    sk = pfx + 'pst%d' % (pi % 4)
                    pi += 1
                    P.op('sp', lambda e, k=k, g=g, st=st, sl=sl: e.dma_start(out=st[:], in_=part[g, k, :, sl]),
                         writes=[sk], lane=sk)
                    P.op('dve', lambda e, k=k, st=st: e.tensor_tensor(out=osum[:, k, :], in0=osum[:, k, :], in1=st[:], op=ALU.add),
                         reads=[sk], writes=[pfx + 'osum%d' % k])
            P.op('act', lambda e, k=k: e.activation(out=sq[:], in_=osum[:, k, :], func=AF.Square),
                 reads=[pfx + 'osum%d' % k], writes=[pfx + 'sq'])
            P.op('pe', lambda e, k=k, bk=bk: e.matmul(C.psum[bk][:], C.ones_f[:], sq[:], start=(k == 0), stop=(k == KC - 1)),
                 reads=[pfx + 'sq', 'ones_f'], writes=['ps%d' % bk])
        P.op('act', lambda e, bk=bk: e.activation(out=rstd[:], in_=C.psum[bk][:], func=AF.Sqrt, scale=1.0 / C.D, bias=eps[:, 0:1]),
             reads=['ps%d' % bk, pfx + 'eps'], writes=[pfx + 'rstd'])
        P.op('dve', lambda e: e.reciprocal(out=rstd[:], in_=rstd[:]), reads=[pfx + 'rstd'], writes=[pfx + 'rstd'])
        for k in range(KC):
            xt = xst[k % 2]
            xk = pfx + 'xst%d' % (k % 2)
            P.op('sp', lambda e, k=k, xt=xt, sl=sl: e.dma_start(out=xt[:], in_=x[k, :, sl]), writes=[xk], lane=xk)
            P.op('pool', lambda e, k=k: e.tensor_tensor(out=osum[:, k, :], in0=osum[:, k, :], in1=rstd[:], op=ALU.mult),
                 reads=[pfx + 'rstd'], writes=[pfx + 'osum%d' % k])
            P.op('dve', lambda e, k=k, xt=xt: e.scalar_tensor_tensor(out=xt[:], in0=osum[:, k, :], scalar=gg[:, k:k + 1],
                                                                    in1=xt[:], op0=ALU.mult, op1=ALU.add),
                 reads=[pfx + 'osum%d' % k, xk, pfx + 'gg'], writes=[xk], force=(k == 0))
            P.op('sp', lambda e, k=k, xt=xt, sl=sl: e.dma_start(out=xo[k, :, sl], in_=xt[:]), reads=[xk], lane=xk + '_st')


def emit_inproj(P, C, x, ng, sc, shf, wint, S, groups, tile_begin=None, pfx='i'):
    KC, TT = C.KC, C.TT
    nh = TT // 512
    ngt = load_vec(P, pfx + 'gs_ng', ng, KC)
    sct = load_vec(P, pfx + 'gs_sc', sc, KC)
    sht = load_vec(P, pfx + 'sh', shf, KC)
    gs = emit_gs(P, ngt, sct, KC, pfx + 'gs')
    eps = P.sb(pfx + 'eps', [128, 1], F32)
    P.op('dve', lambda e: e.memset(eps[:], 1e-6), writes=[pfx + 'eps'])
    h = P.sb(pfx + 'h', [128, KC, TT], BF16)
    xs = [P.sb(pfx + 'xs%d' % i, [128, TT], F32) for i in range(2)]
    sq = P.sb(pfx + 'sq', [128, TT], F32)
    rstd = P.sb(pfx + 'rstd', [128, TT], F32)
    ws = WStream(P, pfx + 'ws', 4, KC * 128)
    ntile = S // TT
    nchunk = sum(g[0] for g in groups)
    reqs = []
    for t in range(ntile):
        for i in range(nchunk):
            reqs.append((wint[i], KC * 128))
    ws.plan(reqs)
    wi = 0
    for t in range(ntile):
        tok0 = t * TT
        emit_norm_mod(P, C, x, tok0, h, pfx + 'h', gs, sht, xs, eps, sq, rstd, pfx)
        if tile_begin is not None:
            tile_begin(tok0)
        for (n, epi) in groups:
            tl = [ws.get(wi + i, oldest=wi) for i in range(n)]
            wi += n
            for hh in range(nh):
                sl = slice(hh * 512, (hh + 1) * 512)
                banks = []
                for (wt, wk) in tl:
                    b = C.bank()
                    banks.append(b)
                    for k in range(KC):
                        P.op('pe', lambda e, k=k, sl=sl, b=b, wt=wt: e.matmul(
                            C.psum[b][:], wt[:, k * 128:(k + 1) * 128], h[:, k, sl], start=(k == 0), stop=(k == KC - 1)),
                            reads=[wk, pfx + 'h'], writes=['ps%d' % b])
                epi(banks, tok0 + hh * 512, hh)


class Stager:
    def __init__(self, P, name, n, dt, width=512):
        self.P = P
        self.name = name
        self.tiles = [P.sb('%s%d' % (name, i), [128, width], dt) for i in range(n)]
        self.i = 0

    def get(self):
        j = self.i % len(self.tiles)
        self.i += 1
        return self.tiles[j], '%s%d' % (self.name, j)


def st_dma(P, dst_ap, src_ap, key, eng='sp'):
    P.op(eng, lambda e: e.dma_start(out=dst_ap, in_=src_ap), reads=[key], lane=key + '_st')


def make_rope_epi(P, C, stg, cos, sin, ckey, dstA, dstB):
    def epi(banks, tok0, hh):
        bA, bB = banks
        sl = slice(hh * 512, (hh + 1) * 512)
        a, ak = stg.get()
        b, bk = stg.get()
        c, ck = stg.get()
        d, dk = stg.get()
        pa, pb = C.psum[bA], C.psum[bB]
        P.op('dve', lambda e: e.tensor_tensor(out=a[:], in0=pa[:], in1=cos[:, sl], op=ALU.mult), reads=['ps%d' % bA] + ckey, writes=[ak])
        P.op('dve', lambda e: e.tensor_tensor(out=b[:], in0=pb[:], in1=sin[:, sl], op=ALU.mult), reads=['ps%d' % bB] + ckey, writes=[bk])
        P.op('dve', lambda e: e.tensor_tensor(out=c[:], in0=pb[:], in1=cos[:, sl], op=ALU.mult), reads=['ps%d' % bB] + ckey, writes=[ck])
        P.op('dve', lambda e: e.tensor_tensor(out=d[:], in0=pa[:], in1=sin[:, sl], op=ALU.mult), reads=['ps%d' % bA] + ckey, writes=[dk])
        P.op('pool', lambda e: e.tensor_tensor(out=a[:], in0=a[:], in1=b[:], op=ALU.subtract), reads=[ak, bk], writes=[ak])
        P.op('pool', lambda e: e.tensor_tensor(out=c[:], in0=c[:], in1=d[:], op=ALU.add), reads=[ck, dk], writes=[ck])
        dstA(a, ak, tok0)
        dstB(c, ck, tok0)
    return epi


def make_act_epi(P, C, stg, func, dst):
    def epi(banks, tok0, hh):
        (b,) = banks
        o, ok = stg.get()
        P.op('act', lambda e: e.activation(out=o[:], in_=C.psum[b][:], func=func), reads=['ps%d' % b], writes=[ok])
        dst(o, ok, tok0)
    return epi


NEG = -30000.0
NOMASK = False


def emit_moba_head(P, C, K, qsrc, ksrc, vsrc, odst, S, pfx='m'):
    NB = S // 256
    QT = S // 128
    qf = P.sb(pfx + 'qf', [128, S], F32)
    kf = P.sb(pfx + 'kf', [128, S], F32)
    vf = P.sb(pfx + 'vf', [128, S], F32)
    qb = P.sb(pfx + 'qb', [128, S], BF16)
    kb = P.sb(pfx + 'kb', [128, S], BF16)
    vtm = P.sb(pfx + 'vtm', [128, QT, 128], BF16)
    maskT = P.sb(pfx + 'maskT', [16, S], BF16)
    kmean = P.sb(pfx + 'kmean', [128, 16], F32)
    gm = P.sb(pfx + 'gm', [128, 16], F32)
    m8 = P.sb(pfx + 'm8', [128, 8], F32)
    mv = P.sb(pfx + 'mv', [128, 16], F32)
    rden = P.sb(pfx + 'rden', [128, 256], F32)
    pst = Stager(P, pfx + 'pT', 3, BF16, 256)
    ost = Stager(P, pfx + 'o', 2, F32, 256)
    P.op('sp', lambda e: e.dma_start(out=qf[:], in_=qsrc), writes=[pfx + 'qf'], lane=pfx + 'qf')
    P.op('sp', lambda e: e.dma_start(out=kf[:], in_=ksrc), writes=[pfx + 'kf'], lane=pfx + 'kf')
    P.op('sp', lambda e: e.dma_start(out=vf[:], in_=vsrc), writes=[pfx + 'vf'], lane=pfx + 'vf')
    P.op('act', lambda e: e.copy(out=qb[:], in_=qf[:]), reads=[pfx + 'qf'], writes=[pfx + 'qb'])
    P.op('pool', lambda e: e.tensor_copy(out=kb[:], in_=kf[:]), reads=[pfx + 'kf'], writes=[pfx + 'kb'])
    P.op('dve', lambda e: e.memset(kmean[:], 0.0), writes=[pfx + 'kmean'])
    P.op('dve', lambda e: e.tensor_reduce(out=kmean[:, 0:NB], in_=kf[:].rearrange("p (n l) -> p n l", l=256), axis=AX.X, op=ALU.add),
         reads=[pfx + 'kf'], writes=[pfx + 'kmean'])
    for t in range(QT):
        P.op('pe', lambda e, t=t: e.transpose(out=C.psum[7][:, 0:128], in_=vf[:, t * 128:(t + 1) * 128], identity=K['ident_f'][:]),
             reads=[pfx + 'vf', 'ident_f'], writes=['ps7'])
        P.op('act', lambda e, t=t: e.copy(out=vtm[:, t, :], in_=C.psum[7][:, 0:128]), reads=['ps7'], writes=[pfx + 'vtm'])
    for t in range(QT):
        ob = t // 2
        P.op('pe', lambda e, t=t: e.matmul(C.psum[7][:, 256:272], qf[:, t * 128:(t + 1) * 128], kmean[:, 0:16], start=True, stop=True),
             reads=[pfx + 'qf', pfx + 'kmean'], writes=['ps7'])
        P.op('dve', lambda e, ob=ob: e.tensor_tensor(out=gm[:], in0=C.psum[7][:, 256:272], in1=K['addmask'][:, ob, :], op=ALU.add),
             reads=['ps7', 'addmask'], writes=[pfx + 'gm'])
        P.op('dve', lambda e: e.max(out=m8[:], in_=gm[:]), reads=[pfx + 'gm'], writes=[pfx + 'm8'], force=True)
        P.op('dve', lambda e: e.tensor_scalar(out=mv[:], in0=gm[:], scalar1=m8[:, 2:3], scalar2=None, op0=ALU.is_ge),
             reads=[pfx + 'gm', pfx + 'm8'], writes=[pfx + 'mv'], force=True)
        P.op('dve', lambda e: e.tensor_scalar(out=mv[:], in0=mv[:], scalar1=-NEG, scalar2=NEG, op0=ALU.mult, op1=ALU.add),
             reads=[pfx + 'mv'], writes=[pfx + 'mv'], force=True)
        P.op('pe', lambda e: e.transpose(out=C.psum[6][0:16, 0:128], in_=mv[:], identity=K['ident_f'][:]),
             reads=[pfx + 'mv', 'ident_f'], writes=['ps6'])
        P.op('act', lambda e, t=t: e.copy(out=maskT[0:16, t * 128:(t + 1) * 128], in_=C.psum[6][0:16, 0:128]),
             reads=['ps6'], writes=[pfx + 'maskT'])
    sbi = 0
    scale = 128.0 ** -0.5
    for Q in range(NB):
        acc = 3 + (Q % 2)
        nk = 2 * (Q + 1)
        qs = slice(Q * 256, (Q + 1) * 256)
        for kt in range(nk):
            blk = kt // 2
            sb_ = sbi % 3
            sbi += 1
            P.op('pe', lambda e, kt=kt, sb_=sb_, qs=qs: e.matmul(C.psum[sb_][:, 0:256], kb[:, kt * 128:(kt + 1) * 128], qb[:, qs], start=True, stop=(NOMASK and blk < Q)),
                 reads=[pfx + 'kb', pfx + 'qb'], writes=['ps%d' % sb_])
            if blk < Q and not NOMASK:
                P.op('pe', lambda e, blk=blk, sb_=sb_, qs=qs: e.matmul(C.psum[sb_][:, 0:256], K['Eall'][0:16, blk * 128:(blk + 1) * 128], maskT[0:16, qs], start=False, stop=True),
                     reads=['Eall', pfx + 'maskT'], writes=['ps%d' % sb_])
            elif blk == Q:
                P.op('pe', lambda e, kt=kt, sb_=sb_: e.matmul(C.psum[sb_][:, 0:256], K['ident_b'][:], K['caus'][:, kt % 2, :], start=False, stop=True),
                     reads=['ident_b', 'caus'], writes=['ps%d' % sb_])
            pT, pk = pst.get()
            P.op('act', lambda e, pT=pT, sb_=sb_: e.activation(out=pT[:], in_=C.psum[sb_][:, 0:256], func=AF.Exp, scale=scale),
                 reads=['ps%d' % sb_], writes=[pk])
            P.op('pe', lambda e, kt=kt, pT=pT, acc=acc, nk=nk: e.matmul(C.psum[acc][:, 0:256], vtm[:, kt, :], pT[:], start=(kt == 0), stop=(kt == nk - 1)),
                 reads=[pfx + 'vtm', pk], writes=['ps%d' % acc])
            P.op('pe', lambda e, kt=kt, pT=pT, acc=acc, nk=nk: e.matmul(C.psum[acc + 2][:, 0:256], C.ones_b[:], pT[:], start=(kt == 0), stop=(kt == nk - 1)),
                 reads=['ones_b', pk], writes=['ps%d' % (acc + 2)])
        P.op('dve', lambda e, acc=acc: e.reciprocal(out=rden[:], in_=C.psum[acc + 2][:, 0:256]), reads=['ps%d' % (acc + 2)], writes=[pfx + 'rden'])
        o, ok = ost.get()
        P.op('dve', lambda e, acc=acc, o=o: e.tensor_tensor(out=o[:], in0=C.psum[acc][:, 0:256], in1=rden[:], op=ALU.mult),
             reads=['ps%d' % acc, pfx + 'rden'], writes=[ok])
        st_dma(P, odst[:, qs], o[:], ok)


def load_consts(P, names_dram):
    K = {}
    for name, (ap, shape, dt) in names_dram.items():
        t = P.sb('k_%s_%d' % (name, P.n_ops), shape, dt)
        eng = 'pool' if dt == BF16 else 'sp'
        P.op(eng, lambda e, t=t, ap=ap: e.dma_start(out=t[:], in_=ap), writes=[name], lane=name)
        K[name] = t
    return K


def moba_const_arrays(S):
    NBP = 16
    ident = np.eye(128, dtype=np.float32)
    Eall = np.zeros((16, 16, 128), np.float32)
    for b in range(16):
        Eall[b, b, :] = 1.0
    Eall = Eall.reshape(16, 16 * 128)
    j = np.arange(128)[:, None]
    i = np.arange(256)[None, :]
    caus = np.stack([np.where(p * 128 + j <= i, 0.0, NEG) for p in range(2)], axis=1).astype(np.float32)
    addmask = np.zeros((128, NBP, 16), np.float32)
    for ob in range(NBP):
        addmask[:, ob, ob:] = -1e30
    return dict(ident=ident, Eall=Eall, caus=np.ascontiguousarray(caus), addmask=addmask)


def emit_ret_head(P, C, K, hr, qsrc, ksrc, vsrc, gsrc, odst, S, pfx='t'):
    BT = min(S, 1024)
    ncb = BT // 128
    qb = [P.sb(pfx + 'qb%d' % i, [128, BT], BF16) for i in range(2)]
    kb = [P.sb(pfx + 'kb%d' % i, [128, BT], BF16) for i in range(2)]
    qd = [P.sb(pfx + 'qd%d' % i, [128, BT], BF16) for i in range(2)]
    kf = [P.sb(pfx + 'kf%d' % i, [128, BT], F32) for i in range(2)]
    vf = [P.sb(pfx + 'vf%d' % i, [128, BT], F32) for i in range(4)]
    gf = [P.sb(pfx + 'gf%d' % i, [128, BT], F32) for i in range(4)]
    ktm = P.sb(pfx + 'ktm', [128, 256], BF16)
    vtm = P.sb(pfx + 'vtm', [128, 512], BF16)
    inn = P.sb(pfx + 'inn', [128, 128], BF16)
    o_sb = P.sb(pfx + 'o_sb', [128, 512], F32)
    osq = P.sb(pfx + 'osq', [128, 512], F32)
    rs = P.sb(pfx + 'rs', [128, 128], F32)
    eps = P.sb(pfx + 'eps', [128, 1], F32)
    st_f = [P.sb(pfx + 'stf%d' % i, [128, 512], F32) for i in range(2)]
    st_b = [P.sb(pfx + 'stb%d' % i, [128, 512], BF16) for i in range(2)]
    ost = Stager(P, pfx + 'om', 4, F32, 128)
    P.op('dve', lambda e: e.memset(eps[:], 1e-6), writes=[pfx + 'eps'])
    for dc in range(2):
        P.op('dve', lambda e, dc=dc: e.memset(st_f[dc][:], 0.0), writes=[pfx + 'stf%d' % dc])
    nblk = S // BT
    obi = 0
    for blk in range(nblk):
        bs = slice(blk * BT, (blk + 1) * BT)
        for dc in range(2):
            P.op('pool', lambda e, dc=dc, bs=bs: e.dma_start(out=qb[dc][:], in_=qsrc[dc][:, bs]), writes=[pfx + 'qb%d' % dc], lane=pfx + 'qb%d' % dc)
            P.op('pool', lambda e, dc=dc, bs=bs: e.dma_start(out=kb[dc][:], in_=ksrc[dc][:, bs]), writes=[pfx + 'kb%d' % dc], lane=pfx + 'kb%d' % dc)
            P.op('sp', lambda e, dc=dc, bs=bs: e.dma_start(out=kf[dc][:], in_=ksrc[dc][:, bs]), writes=[pfx + 'kf%d' % dc], lane=pfx + 'kf%d' % dc)
        for vc in range(4):
            P.op('sp', lambda e, vc=vc, bs=bs: e.dma_start(out=vf[vc][:], in_=vsrc[vc][:, bs]), writes=[pfx + 'vf%d' % vc], lane=pfx + 'vf%d' % vc)
            P.op('sp', lambda e, vc=vc, bs=bs: e.dma_start(out=gf[vc][:], in_=gsrc[vc][:, bs]), writes=[pfx + 'gf%d' % vc], lane=pfx + 'gf%d' % vc)
        for dc in range(2):
            for c in range(ncb):
                cs = slice(c * 128, (c + 1) * 128)
                P.op('pool', lambda e, dc=dc, cs=cs: e.tensor_tensor(out=qd[dc][:, cs], in0=qb[dc][:, cs], in1=K['qdec'][:, hr, :], op=ALU.mult),
                     reads=[pfx + 'qb%d' % dc, 'qdec'], writes=[pfx + 'qd%d' % dc])
        for c in range(ncb):
            cg = blk * ncb + c
            cs = slice(c * 128, (c + 1) * 128)
            gsl = slice(cg * 128, (cg + 1) * 128)
            for dc in range(2):
                P.op('pe', lambda e, dc=dc, cs=cs: e.transpose(out=C.psum[7][:, dc * 128:(dc + 1) * 128], in_=kf[dc][:, cs], identity=K['ident_f'][:]),
                     reads=[pfx + 'kf%d' % dc, 'ident_f'], writes=['ps7'])
            P.op('dve', lambda e: e.tensor_scalar(out=ktm[:], in0=C.psum[7][:, 0:256], scalar1=K['kdec'][:, hr:hr + 1], scalar2=None, op0=ALU.mult),
                 reads=['ps7', 'kdec'], writes=[pfx + 'ktm'])
            for vc in range(4):
                P.op('pe', lambda e, vc=vc, cs=cs: e.transpose(out=C.psum[6][:, vc * 128:(vc + 1) * 128], in_=vf[vc][:, cs], identity=K['ident_f'][:]),
                     reads=[pfx + 'vf%d' % vc, 'ident_f'], writes=['ps6'])
            P.op('act', lambda e: e.copy(out=vtm[:], in_=C.psum[6][:, 0:512]), reads=['ps6'], writes=[pfx + 'vtm'])
            for dc in range(2):
                P.op('pe', lambda e, dc=dc, cs=cs: e.matmul(C.psum[5][:, 0:128], kb[dc][:, cs], qb[dc][:, cs], start=(dc == 0), stop=(dc == 1)),
                     reads=[pfx + 'kb%d' % dc, pfx + 'qb%d' % dc], writes=['ps5'])
            P.op('dve', lambda e: e.tensor_tensor(out=inn[:], in0=C.psum[5][:, 0:128], in1=K['dmT'][:, hr, :], op=ALU.mult),
                 reads=['ps5', 'dmT'], writes=[pfx + 'inn'])
            ob = obi % 2
            obi += 1
            for vc in range(4):
                vs = slice(vc * 128, (vc + 1) * 128)
                P.op('pe', lambda e, vs=vs, ob=ob, cg=cg: e.matmul(C.psum[ob][:, vs], vtm[:, vs], inn[:], start=True, stop=(cg == 0)),
                     reads=[pfx + 'vtm', pfx + 'inn'], writes=['ps%d' % ob])
                if cg > 0:
                    for dc in range(2):
                        P.op('pe', lambda e, vs=vs, ob=ob, dc=dc, cs=cs: e.matmul(C.psum[ob][:, vs], st_b[dc][:, vs], qd[dc][:, cs], start=False, stop=(dc == 1)),
                             reads=[pfx + 'stb%d' % dc, pfx + 'qd%d' % dc], writes=['ps%d' % ob])
            P.op('act', lambda e, ob=ob: e.copy(out=o_sb[:], in_=C.psum[ob][:]), reads=['ps%d' % ob], writes=[pfx + 'o_sb'])
            P.op('act', lambda e, ob=ob: e.activation(out=osq[:], in_=C.psum[ob][:], func=AF.Square), reads=['ps%d' % ob], writes=[pfx + 'osq'])
            for vc in range(4):
                P.op('pe', lambda e, vc=vc: e.matmul(C.psum[2][:, 0:128], C.ones_f[:], osq[:, vc * 128:(vc + 1) * 128], start=(vc == 0), stop=(vc == 3)),
                     reads=['ones_f', pfx + 'osq'], writes=['ps2'])
            P.op('act', lambda e: e.activation(out=rs[:], in_=C.psum[2][:, 0:128], func=AF.Sqrt, scale=1.0 / 512, bias=eps[:, 0:1]),
                 reads=['ps2', pfx + 'eps'], writes=[pfx + 'rs'])
            P.op('dve', lambda e: e.reciprocal(out=rs[:], in_=rs[:]), reads=[pfx + 'rs'], writes=[pfx + 'rs'])
            for vc in range(4):
                vs = slice(vc * 128, (vc + 1) * 128)
                om, omk = ost.get()
                P.op('dve', lambda e, vs=vs, om=om: e.tensor_tensor(out=om[:], in0=o_sb[:, vs], in1=rs[:], op=ALU.mult),
                     reads=[pfx + 'o_sb', pfx + 'rs'], writes=[omk])
                P.op('pool', lambda e, vc=vc, om=om, cs=cs: e.tensor_tensor(out=om[:], in0=om[:], in1=gf[vc][:, cs], op=ALU.mult),
                     reads=[omk, pfx + 'gf%d' % vc], writes=[omk])
                st_dma(P, odst[vc][:, gsl], om[:], omk)
            for dc in range(2):
                P.op('pe', lambda e, dc=dc: e.matmul(C.psum[3 + dc][:], ktm[:, dc * 128:(dc + 1) * 128], vtm[:], start=True, stop=True),
                     reads=[pfx + 'ktm', pfx + 'vtm'], writes=['ps%d' % (3 + dc)])
                P.op('dve', lambda e, dc=dc: e.scalar_tensor_tensor(out=st_f[dc][:], in0=st_f[dc][:], scalar=K['gC'][:, hr:hr + 1], in1=C.psum[3 + dc][:],
                                                                   op0=ALU.mult, op1=ALU.add),
                     reads=['ps%d' % (3 + dc), 'gC'], writes=[pfx + 'stf%d' % dc])
                P.op('act', lambda e, dc=dc: e.copy(out=st_b[dc][:], in_=st_f[dc][:]), reads=[pfx + 'stf%d' % dc], writes=[pfx + 'stb%d' % dc])


def ret_const_arrays(heads):
    C = 128
    dmT = np.zeros((128, len(heads), 128), np.float32)
    qdec = np.zeros((128, len(heads), 128), np.float32)
    kdec = np.zeros((128, len(heads)), np.float32)
    gC = np.zeros((128, len(heads)), np.float32)
    idx = np.arange(C, dtype=np.float32)
    for n, hd in enumerate(heads):
        log_g = np.log1p(-np.exp2(np.float32(-5.0 - hd))).astype(np.float32)
        diff = idx[None, :] - idx[:, None]
        dmT[:, n, :] = np.where(diff >= 0, np.exp(np.maximum(diff, 0) * log_g), 0.0) * (256 ** -0.5)
        qdec[:, n, :] = np.exp((idx + 1.0) * log_g)[None, :]
        kdec[:, n] = np.exp((C - 1.0 - idx) * log_g) * (256 ** -0.5)
        gC[:, n] = np.exp(C * log_g)
    return dict(dmT=dmT, qdec=qdec, kdec=kdec, gC=gC)


def emit_outproj(P, C, merged, woutt, part, S, NKC, pfx='o'):
    KC, TT = C.KC, C.TT
    nh = TT // 512
    mt = P.sb(pfx + 'mt', [128, NKC, TT], BF16)
    ws = WStream(P, pfx + 'ws', 4, NKC * 128)
    ost = Stager(P, pfx + 'ost', 4, F32, 512)
    ntile = S // TT
    ws.plan([(woutt[m], NKC * 128) for t in range(ntile) for m in range(KC)])
    wi = 0
    for t in range(ntile):
        tok0 = t * TT
        for kc in range(NKC):
            P.op('pool', lambda e, kc=kc, tok0=tok0: e.dma_start(out=mt[:, kc, :], in_=(merged[kc][:, tok0:tok0 + TT] if isinstance(merged, list) else merged[kc, :, tok0:tok0 + TT])),
                 writes=[pfx + 'mt%d' % kc], lane=pfx + 'mt%d' % kc)
        for m in range(KC):
            w, wk = ws.get(wi)
            wi += 1
            for hh in range(nh):
                sl = slice(hh * 512, (hh + 1) * 512)
                b = C.bank()
                for kc in range(NKC):
                    P.op('pe', lambda e, kc=kc, sl=sl, b=b, w=w: e.matmul(C.psum[b][:], w[:, kc * 128:(kc + 1) * 128], mt[:, kc, sl],
                                                                         start=(kc == 0), stop=(kc == NKC - 1)),
                         reads=[wk, pfx + 'mt%d' % kc], writes=['ps%d' % b])
                o, ok = ost.get()
                P.op('act', lambda e, o=o, b=b: e.copy(out=o[:], in_=C.psum[b][:]), reads=['ps%d' % b], writes=[ok])
                pput = part if callable(part) else (lambda m_, t0_, n_: part[m_, :, t0_:t0_ + n_])
                st_dma(P, pput(m, tok0 + hh * 512, 512), o[:], ok)


def decl_mix0(nc, D, S, sfx=''):
    KC = D // 128
    di = lambda n, s: nc.dram_tensor(n + sfx, s, F32, kind="ExternalInput").ap()
    T = dict(wint=di("wint", [36, 128, KC * 128]), woutt=di("woutt", [KC, 128, 12 * 128]), rope=di("rope", [4, 128, S]))
    T['cd'] = dict(ident=di("ident", [128, 128]), Eall=di("Eall", [16, 2048]), caus=di("caus", [128, 2, 256]),
                   addmask=di("addmask", [128, 16, 16]), dmT=di("dmT", [128, 2, 128]), qdec=di("qdec", [128, 2, 128]),
                   kdec=di("kdec", [128, 2]), gC=di("gC", [128, 2]))
    return T


def build_mix0(D, S, TT=1024):
    nc = bass.Bass("TRN2", target_bir_lowering=False)
    KC = D // 128
    TT = min(TT, S)
    di = lambda n, s: nc.dram_tensor(n, s, F32, kind="ExternalInput").ap()
    x = di("x", [KC, 128, S])
    ng, sc, shf = di("ng", [128, KC]), di("sc", [128, KC]), di("shf", [128, KC])
    T = decl_mix0(nc, D, S)
    part = nc.dram_tensor("part", [KC, 128, S], F32, kind="ExternalOutput").ap()
    P = Prog(nc)
    C = Common(P, D, TT)
    stage_mix0(P, C, nc, T, x, ng, sc, shf, part, S)
    P.finish()
    return nc


def stage_mix0(P, C, nc, T, x, ng, sc, shf, part, S, sfx=''):
    wint, woutt, rope, cd = T['wint'], T['woutt'], T['rope'], T['cd']
    TT = C.TT
    scr = lambda n, s: nc.dram_tensor(n + sfx, s, F32, kind="Internal").ap()
    qm, km, vm = scr("qm", [4, 128, S]), scr("km", [4, 128, S]), scr("vm", [4, 128, S])
    qr, kr = scr("qr", [2, 2, 128, S]), scr("kr", [2, 2, 128, S])
    vr, gr = scr("vr", [2, 4, 128, S]), scr("gr", [2, 4, 128, S])
    merged = scr("merged", [12, 128, S]) if not DEBUG else nc.dram_tensor("merged", [12, 128, S], F32, kind="ExternalOutput").ap()
    P.phase_begin()
    tabs = [P.sb('rope%d' % i, [128, TT], F32) for i in range(4)]

    def tile_begin(tok0):
        for i in range(4):
            P.op('sp', lambda e, i=i, tok0=tok0: e.dma_start(out=tabs[i][:], in_=rope[i, :, tok0:tok0 + TT]), writes=['rope%d' % i], lane='rope%d' % i)
    stg = Stager(P, 'stg', 8, F32)
    stp = Stager(P, 'stp', 4, F32)

    def halves(T, r0, h0, h1):
        def f(t, k, tok0):
            st_dma(P, T[h0][r0:r0 + 64, tok0:tok0 + 512], t[0:64, :], k)
            P.op('sp', lambda e: e.dma_start(out=T[h1][r0:r0 + 64, tok0:tok0 + 512], in_=t[64:128, :]), reads=[k], lane=k + '_st2')
        return f

    def whole(dst):
        def f(t, k, tok0):
            st_dma(P, dst[:, tok0:tok0 + 512], t[:], k)
        return f
    groups = []
    for (dstT, pr) in ((qm, 0), (qm, 1), (km, 0), (km, 1)):
        groups.append((2, make_rope_epi(P, C, stg, tabs[0], tabs[1], ['rope0', 'rope1'], halves(dstT, 0, 2 * pr, 2 * pr + 1), halves(dstT, 64, 2 * pr, 2 * pr + 1))))
    for hd in range(4):
        groups.append((1, make_act_epi(P, C, stp, AF.Copy, whole(vm[hd]))))
    for hr in range(2):
        for dstT in (qr, kr):
            groups.append((2, make_rope_epi(P, C, stg, tabs[2], tabs[3], ['rope2', 'rope3'], whole(dstT[hr, 0]), whole(dstT[hr, 1]))))
    for hr in range(2):
        for vc in range(4):
            groups.append((1, make_act_epi(P, C, stp, AF.Copy, whole(vr[hr, vc]))))
    for hr in range(2):
        for vc in range(4):
            groups.append((1, make_act_epi(P, C, stp, AF.Silu, whole(gr[hr, vc]))))
    emit_inproj(P, C, x, ng, sc, shf, wint, S, groups, tile_begin=tile_begin)
    P.phase_end()
    P.phase_begin()
    K = load_consts(P, dict(ident_f=(cd['ident'], [128, 128], F32), ident_b=(cd['ident'], [128, 128], BF16),
                            Eall=(cd['Eall'], [16, 2048], BF16), caus=(cd['caus'], [128, 2, 256], BF16),
                            addmask=(cd['addmask'], [128, 16, 16], F32)))
    P.es, es_keep = P.phase_es, P.es
    for hd in range(4):
        P.phase_begin()
        emit_moba_head(P, C, K, qm[hd], km[hd], vm[hd], merged[hd], S, pfx='m')
        P.phase_end()
    P.phase_es, P.es = P.es, es_keep
    P.phase_end()
    P.phase_begin()
    K = load_consts(P, dict(ident_f=(cd['ident'], [128, 128], F32), dmT=(cd['dmT'], [128, 2, 128], F32),
                            qdec=(cd['qdec'], [128, 2, 128], BF16), kdec=(cd['kdec'], [128, 2], F32), gC=(cd['gC'], [128, 2], F32)))
    P.es, es_keep = P.phase_es, P.es
    for hr in range(2):
        P.phase_begin()
        emit_ret_head(P, C, K, hr, [qr[hr, 0], qr[hr, 1]], [kr[hr, 0], kr[hr, 1]], [vr[hr, i] for i in range(4)],
                      [gr[hr, i] for i in range(4)], [merged[4 + hr * 4 + i] for i in range(4)], S, pfx='t')
        P.phase_end()
    P.phase_es, P.es = P.es, es_keep
    P.phase_end()
    P.phase_begin()
    emit_outproj(P, C, merged, woutt, part, S, 12)
    P.phase_end()


def tile_w(w, KCk):
    K, N = w.shape
    return np.ascontiguousarray(w.reshape(KCk, 128, N // 128, 128).transpose(2, 1, 0, 3).reshape(N // 128, 128, KCk * 128))


def fm(a, KC):
    return np.ascontiguousarray(a.T.reshape(KC, 128, -1))


def rope_np(S, dim):
    inv = (1.0 / (10000.0 ** (np.arange(0, dim, 2, dtype=np.float32) / np.float32(dim)))).astype(np.float32)
    ang = np.arange(S, dtype=np.float32)[:, None] * inv[None, :]
    return np.cos(ang).astype(np.float32), np.sin(ang).astype(np.float32)


def mix0_cols(j):
    dm, dk, dv = 2048, 2048, 4096
    cols = []
    offq, offk, offv = 0, dm, 2 * dm
    for off in (offq, offk):
        for pr in range(2):
            h0, h1 = 4 * j + 2 * pr, 4 * j + 2 * pr + 1
            A = np.concatenate([off + h0 * 128 + np.arange(64), off + h1 * 128 + np.arange(64)])
            B = np.concatenate([off + h0 * 128 + 64 + np.arange(64), off + h1 * 128 + 64 + np.arange(64)])
            cols += [A, B]
    for hd in range(4):
        cols.append(offv + (4 * j + hd) * 128 + np.arange(128))
    oqr, okr, ovr, ogr = 3 * dm, 3 * dm + dk, 3 * dm + 2 * dk, 3 * dm + 2 * dk + dv
    for hr in range(2):
        H = 2 * j + hr
        for off in (oqr, okr):
            cols.append(off + H * 256 + np.arange(128))
            cols.append(off + H * 256 + 128 + np.arange(128))
    for off in (ovr, ogr):
        for hr in range(2):
            H = 2 * j + hr
            for vc in range(4):
                cols.append(off + H * 512 + vc * 128 + np.arange(128))
    return np.concatenate(cols)


def mix0_rows(j):
    rows = []
    for hd in range(4):
        rows.append((4 * j + hd) * 128 + np.arange(128))
    for hr in range(2):
        H = 2 * j + hr
        for vc in range(4):
            rows.append(2048 + H * 512 + vc * 128 + np.arange(128))
    return np.concatenate(rows)


def mix0_inputs(x_b, ng, sc, shf, w_in, w_out, j, S):
    D = x_b.shape[1]
    KC = D // 128
    cm, sm = rope_np(S, 128)
    cr, sr = rope_np(S, 256)
    rope = np.stack([np.concatenate([cm.T, cm.T]), np.concatenate([-sm.T, -sm.T]) * -1.0 if False else np.concatenate([sm.T, sm.T]), cr.T, sr.T]).astype(np.float32)
    im = dict(x=fm(x_b, KC), ng=chunked(ng, KC), sc=chunked(sc, KC), shf=chunked(shf, KC),
              wint=tile_w(w_in[:, mix0_cols(j)], KC), woutt=tile_w(w_out[mix0_rows(j), :], 12), rope=np.ascontiguousarray(rope))
    im.update(moba_const_arrays(S))
    im.update(ret_const_arrays([2 * j, 2 * j + 1]))
    return im


CW = 0.6065306597126334


def rwkv_const_arrays():
    m = np.zeros((64, 320), np.float32)
    s = np.arange(64)[:, None]
    t = np.arange(64)[None, :]
    m[:, 0:64] = (s < t)
    m[:, 64:128] = (t < s)
    m[:, 128:192] = (s < t)
    m[:, 192:256] = (s <= t)
    m[:, 256:320] = (s <= t)
    blockones = np.zeros((128, 128), np.float32)
    blockones[:64, :64] = 1.0
    blockones[64:, 64:] = 1.0
    return dict(amask=m, blockones=blockones, identrep=np.tile(np.eye(64, dtype=np.float32), (1, 8)))


def emit_rwkv(P, C, K, src, yout, S, BT=512):
    BT = min(BT, S)
    NCH = BT // 64
    f32t = lambda n, shp=None: P.sb(n, shp or [128, BT], F32)
    xin = {n: f32t('x_' + n, [128, BT + 1]) for n in ('xw', 'xa', 'xg0', 'xg1')}
    txw, xas, sxg0, sxg1 = f32t('txw'), f32t('xas'), f32t('sxg0'), f32t('sxg1')
    dtmp = f32t('dtmp')
    STh = [[P.sb('ST%d_%d' % (pp, h), [64, 64], F32) for h in range(2)] for pp in range(4)]
    for pp in range(4):
        for h in range(2):
            P.op('dve', lambda e, pp=pp, h=h: e.memset(STh[pp][h][:], 0.0), writes=['ST%d_%d' % (pp, h)])
    eps_ln = P.sb('eps_ln', [128, 1], F32)
    P.op('dve', lambda e: e.memset(eps_ln[:], 64e-5), writes=['eps_ln'])
    SL = []
    for sl in range(1):
        d = {}
        for n in ('rin', 'kin', 'vin'):
            d[n] = f32t('%s%d' % (n, sl), [128, BT + 1])
        for n in ('rs', 'ks', 'vs', 'sgw', 'asig', 'g', 'kk', 'kkn', 'k2', 'bvec', 'bonus', 'Lp', 'Lm', 'eL', 'enL', 'eLm1', 'eTL',
                  'ah', 'bh', 'kh', 'rh', 'bt', 'kt', 'yT', 'tmp', 'tmp2'):
            d[n] = f32t('%s%d' % (n, sl))
        for n in ('ah', 'bh', 'kh', 'rh'):
            d[n + '1'] = P.sb('%s1_%d' % (n, sl), [64, BT], F32)
        d['PC1'] = P.sb('PC1_%d' % sl, [64, NCH], F32)
        d['ntot'] = P.sb('ntot%d' % sl, [128, NCH], F32)
        d['PC'] = P.sb('PC%d' % sl, [128, NCH], F32)
        d['tm'] = P.sb('tm%d' % sl, [64, NCH, 384], F32)
        for h in range(2):
            d['Am%d' % h] = P.sb('Am%d_%d' % (h, sl), [64, NCH, 320], F32)
            d['Tt%d' % h] = P.sb('Tt%d_%d' % (h, sl), [64, NCH, 64], F32)
            for n in ('Mi', 'Ni'):
                d['%s%d' % (n, h)] = [P.sb('%s%d_%d_%d' % (n, h, sl, q), [64, NCH, 64], F32) for q in range(2)]
            d['Zs%d' % h] = P.sb('Zs%d_%d' % (h, sl), [64, 64], F32)
            d['Us%d' % h] = P.sb('Us%d_%d' % (h, sl), [64, 64], F32)
        SL.append(d)

    def k_(sl, n):
        return 'w%d_%s' % (sl, n)

    def shift(eng_sub, xin_t, xin_key, mu_ap, out_t, out_key, tmp_t, tmp_key):
        P.op('pool', lambda e: e.tensor_tensor(out=tmp_t[:], in0=xin_t[:, 0:BT], in1=xin_t[:, 1:BT + 1], op=ALU.subtract),
             reads=[xin_key], writes=[tmp_key])
        P.op('dve', lambda e: e.scalar_tensor_tensor(out=out_t[:], in0=tmp_t[:], scalar=mu_ap, in1=xin_t[:, 1:BT + 1], op0=ALU.mult, op1=ALU.add),
             reads=[tmp_key, xin_key, 'kc'], writes=[out_key])

    def load_prev(t, key, dram_row, t0):
        if t0 == 0:
            P.op('dve', lambda e: e.memset(t[:, 0:1], 0.0), writes=[key])
            P.op('sp', lambda e: e.dma_start(out=t[:, 1:BT + 1], in_=dram_row[:, 0:BT]), writes=[key], lane=key)
        else:
            P.op('sp', lambda e: e.dma_start(out=t[:], in_=dram_row[:, t0 - 1:t0 + BT]), writes=[key], lane=key)

    ones64 = K['blockones']
    import os as _os
    RWS = float(_os.environ.get('RWS', '9'))
    for blk in range(S // BT):
        t0 = blk * BT
        load_prev(xin['xw'], 'x_xw', src['xw'], t0)
        load_prev(xin['xa'], 'x_xa', src['xa'], t0)
        load_prev(xin['xg0'], 'x_xg0', src['xg'][0], t0)
        load_prev(xin['xg1'], 'x_xg1', src['xg'][1], t0)
        shift(None, xin['xw'], 'x_xw', K['mux'][:, 0:1], txw, 'txw', dtmp, 'dtmp')
        P.op('act', lambda e: e.activation(out=txw[:], in_=txw[:], func=AF.Tanh), reads=['txw'], writes=['txw'])
        shift(None, xin['xa'], 'x_xa', K['mux'][:, 1:2], xas, 'xas', dtmp, 'dtmp')
        shift(None, xin['xg0'], 'x_xg0', K['mux'][:, 2:3], sxg0, 'sxg0', dtmp, 'dtmp')
        P.op('act', lambda e: e.activation(out=sxg0[:], in_=sxg0[:], func=AF.Sigmoid), reads=['sxg0'], writes=['sxg0'])
        shift(None, xin['xg1'], 'x_xg1', K['mux'][:, 3:4], sxg1, 'sxg1', dtmp, 'dtmp')
        P.op('act', lambda e: e.activation(out=sxg1[:], in_=sxg1[:], func=AF.Sigmoid), reads=['sxg1'], writes=['sxg1'])
        for half in range(4):
            pairs = [half]
            for sl, pp in enumerate(pairs):
                d = SL[sl]
                kk_ = lambda n, sl=sl: k_(sl, n)
                pc = slice(pp * 128, (pp + 1) * 128)
                for n, nm, mi in (('rin', 'r', 0), ('kin', 'k', 1), ('vin', 'v', 2)):
                    load_prev(d[n], kk_(n), src[nm][pp], t0)
                shift(None, d['rin'], kk_('rin'), K['mu3'][:, pp:pp + 1], d['rs'], kk_('rs'), d['tmp'], kk_('tmp'))
                shift(None, d['kin'], kk_('kin'), K['mu3'][:, 4 + pp:5 + pp], d['ks'], kk_('ks'), d['tmp'], kk_('tmp'))
                shift(None, d['vin'], kk_('vin'), K['mu3'][:, 8 + pp:9 + pp], d['vs'], kk_('vs'), d['tmp'], kk_('tmp'))

                def T(out, fn, reads, eng='dve', force=False, d=d, kk_=kk_):
                    P.op(eng, fn, reads=[kk_(r) if not r.startswith('!') else r[1:] for r in reads], writes=[kk_(out)], force=force)
                P.op('pe', lambda e, pc=pc: e.matmul(C.psum[0][:, 0:BT], K['w_up'][:, pc], txw[:], start=True, stop=True), reads=['kc', 'txw'], writes=['ps0'])
                T('sgw', lambda e, d=d, pp=pp: e.activation(out=d['sgw'][:], in_=C.psum[0][:, 0:BT], func=AF.Sigmoid, bias=K['w0'][:, pp:pp + 1]), ['!ps0', '!kc'], 'act')
                P.op('pe', lambda e, pc=pc: e.matmul(C.psum[1][:, 0:BT], K['a_up'][:, pc], xas[:], start=True, stop=True), reads=['kc', 'xas'], writes=['ps1'])
                T('asig', lambda e, d=d, pp=pp: e.activation(out=d['asig'][:], in_=C.psum[1][:, 0:BT], func=AF.Sigmoid, bias=K['a0'][:, pp:pp + 1]), ['!ps1', '!kc'], 'act')
                P.op('pe', lambda e, pc=pc: e.matmul(C.psum[2][:, 0:BT], K['g_up0'][:, pc], sxg0[:], start=True, stop=False), reads=['kc', 'sxg0'], writes=['ps2'])
                P.op('pe', lambda e, pc=pc: e.matmul(C.psum[2][:, 0:BT], K['g_up1'][:, pc], sxg1[:], start=False, stop=True), reads=['kc', 'sxg1'], writes=['ps2'])
                T('g', lambda e, d=d: e.copy(out=d['g'][:], in_=C.psum[2][:, 0:BT]), ['!ps2'], 'act')
                T('kk', lambda e, d=d, pp=pp: e.tensor_scalar(out=d['kk'][:], in0=d['ks'][:], scalar1=K['k_k'][:, pp:pp + 1], scalar2=None, op0=ALU.mult), ['ks', '!kc'])
                T('tmp', lambda e, d=d: e.activation(out=d['tmp'][:], in_=d['kk'][:], func=AF.Square), ['kk'], 'act')
                P.op('pe', lambda e, d=d: e.matmul(C.psum[3][:, 0:BT], ones64[:], d['tmp'][:], start=True, stop=True), reads=['kc', kk_('tmp')], writes=['ps3'])
                T('tmp2', lambda e, d=d: e.activation(out=d['tmp2'][:], in_=C.psum[3][:, 0:BT], func=AF.Sqrt), ['!ps3'], 'act')
                T('tmp2', lambda e, d=d: e.tensor_scalar(out=d['tmp2'][:], in0=d['tmp2'][:], scalar1=1e-12, scalar2=None, op0=ALU.max), ['tmp2'])
                T('tmp2', lambda e, d=d: e.reciprocal(out=d['tmp2'][:], in_=d['tmp2'][:]), ['tmp2'])
                T('kkn', lambda e, d=d: e.tensor_tensor(out=d['kkn'][:], in0=d['kk'][:], in1=d['tmp2'][:], op=ALU.mult), ['kk', 'tmp2'])
                T('tmp', lambda e, d=d, pp=pp: e.tensor_scalar(out=d['tmp'][:], in0=d['asig'][:], scalar1=-1.0, scalar2=K['k_a'][:, pp:pp + 1], op0=ALU.add, op1=ALU.mult), ['asig', '!kc'])
                T('k2', lambda e, d=d: e.scalar_tensor_tensor(out=d['k2'][:], in0=d['tmp'][:], scalar=1.0, in1=d['ks'][:], op0=ALU.add, op1=ALU.mult), ['tmp', 'ks'])
                T('bvec', lambda e, d=d: e.tensor_tensor(out=d['bvec'][:], in0=d['kkn'][:], in1=d['asig'][:], op=ALU.mult), ['kkn', 'asig'], 'pool')
                T('tmp', lambda e, d=d, pp=pp: e.scalar_tensor_tensor(out=d['tmp'][:], in0=d['rs'][:], scalar=K['r_k'][:, pp:pp + 1], in1=d['k2'][:], op0=ALU.mult, op1=ALU.mult), ['rs', 'k2', '!kc'])
                P.op('pe', lambda e, d=d: e.matmul(C.psum[3][:, 0:BT], ones64[:], d['tmp'][:], start=True, stop=True), reads=['kc', kk_('tmp')], writes=['ps3'])
                T('bonus', lambda e, d=d: e.tensor_tensor(out=d['bonus'][:], in0=C.psum[3][:, 0:BT], in1=d['vs'][:], op=ALU.mult), ['!ps3', 'vs'])
                T('Lp', lambda e, d=d: e.tensor_tensor_scan(out=d['Lp'][:], data0=K['rmask'][:, 0:BT], data1=d['sgw'][:], initial=0.0, op0=ALU.mult, op1=ALU.add), ['sgw', '!kc'])
                T('Lm', lambda e, d=d: e.tensor_tensor(out=d['Lm'][:], in0=d['Lp'][:], in1=d['sgw'][:], op=ALU.subtract), ['Lp', 'sgw'], 'pool')
                T('eL', lambda e, d=d: e.activation(out=d['eL'][:], in_=d['Lp'][:], func=AF.Exp, scale=-CW), ['Lp'], 'act')
                T('enL', lambda e, d=d: e.activation(out=d['enL'][:], in_=d['Lp'][:], func=AF.Exp, scale=CW), ['Lp'], 'act')
                T('eLm1', lambda e, d=d: e.activation(out=d['eLm1'][:], in_=d['Lm'][:], func=AF.Exp, scale=-CW), ['Lm'], 'act')
                T('ntot', lambda e, d=d: e.tensor_scalar(out=d['ntot'][:], in0=d['Lp'][:].rearrange("p (c t) -> p c t", t=64)[:, :, 63], scalar1=-CW, scalar2=None, op0=ALU.mult), ['Lp'])
                T('PC', lambda e, d=d: e.activation(out=d['PC'][:], in_=d['ntot'][:], func=AF.Exp), ['ntot'], 'act')
                for cc in range(NCH):
                    cs = slice(cc * 64, (cc + 1) * 64)
                    T('eTL', lambda e, d=d, cs=cs, cc=cc: e.activation(out=d['eTL'][:, cs], in_=d['Lp'][:, cs], func=AF.Exp, scale=CW, bias=d['ntot'][:, cc:cc + 1]), ['Lp', 'ntot'], 'act')
                T('ah', lambda e, d=d: e.scalar_tensor_tensor(out=d['ah'][:], in0=d['kkn'][:], scalar=-1.0, in1=d['eLm1'][:], op0=ALU.mult, op1=ALU.mult), ['kkn', 'eLm1'])
                T('bh', lambda e, d=d: e.tensor_tensor(out=d['bh'][:], in0=d['bvec'][:], in1=d['enL'][:], op=ALU.mult), ['bvec', 'enL'], 'pool')
                T('kh', lambda e, d=d: e.tensor_tensor(out=d['kh'][:], in0=d['k2'][:], in1=d['enL'][:], op=ALU.mult), ['k2', 'enL'])
                T('rh', lambda e, d=d: e.tensor_tensor(out=d['rh'][:], in0=d['rs'][:], in1=d['eL'][:], op=ALU.mult), ['rs', 'eL'], 'pool')
                T('bt', lambda e, d=d: e.tensor_tensor(out=d['bt'][:], in0=d['bvec'][:], in1=d['eTL'][:], op=ALU.mult), ['bvec', 'eTL'])
                T('kt', lambda e, d=d: e.tensor_tensor(out=d['kt'][:], in0=d['k2'][:], in1=d['eTL'][:], op=ALU.mult), ['k2', 'eTL'], 'pool')
                for n in ('ah', 'bh', 'kh', 'rh'):
                    P.op('sp', lambda e, d=d, n=n: e.dma_start(out=d[n + '1'][0:64, :], in_=d[n][64:128, :]), reads=[kk_(n)], writes=[kk_(n + '1')], lane=kk_(n + '1'))
                P.op('sp', lambda e, d=d: e.dma_start(out=d['PC1'][0:64, :], in_=d['PC'][64:128, :]), reads=[kk_('PC')], writes=[kk_('PC1')], lane=kk_('PC1'))
                for cc in range(NCH if RWS >= 2 else 0):
                    cs = slice(cc * 64, (cc + 1) * 64)
                    for qi, n in enumerate(('vs', 'bt', 'kt')):
                        P.op('pe', lambda e, d=d, n=n, cs=cs, qi=qi: e.transpose(out=C.psum[0][0:64, qi * 128:(qi + 1) * 128], in_=d[n][:, cs], identity=K['ident_f'][:]),
                             reads=[kk_(n), 'kc'], writes=['ps0'])
                    T('tm', lambda e, d=d, cc=cc: e.copy(out=d['tm'][:, cc, :], in_=C.psum[0][0:64, 0:384]), ['!ps0'], 'act')
                for h in range(2 if RWS >= 3 else 0):
                    hs = slice(64 * h, 64 * h + 64)
                    sfx = '' if h == 0 else '1'
                    Am = d['Am%d' % h]
                    for cc in range(NCH):
                        cs = slice(cc * 64, (cc + 1) * 64)
                        for qi, (l, r) in enumerate((('bh', 'ah'), ('ah', 'bh'), ('kh', 'ah'), ('bh', 'rh'), ('kh', 'rh'))):
                            P.op('pe', lambda e, d=d, l=l, r=r, sfx=sfx, cs=cs, qi=qi: e.matmul(C.psum[1][0:64, qi * 64:(qi + 1) * 64], d[l + sfx][0:64, cs], d[r + sfx][0:64, cs], start=True, stop=True),
                                 reads=[kk_(l + sfx), kk_(r + sfx)], writes=['ps1'])
                        if RWS >= 3.3:
                          T('Am%d' % h, lambda e, Am=Am, cc=cc: e.tensor_tensor(out=Am[:, cc, :], in0=C.psum[1][0:64, 0:320], in1=K['amask'][:], op=ALU.mult), ['!ps1', '!kc'])
                    Tt = d['Tt%d' % h]
                    Mi, Ni = d['Mi%d' % h], d['Ni%d' % h]
                    if RWS >= 3.6:
                      T('Tt%d' % h, lambda e, Tt=Tt, Am=Am: e.tensor_tensor(out=Tt[:], in0=Am[:, :, 0:64], in1=K['identrep'][:, 0:NCH * 64].rearrange("p (c t) -> p c t", t=64), op=ALU.add), ['Am%d' % h, '!kc'])
                    Mprev = lambda cc, Am=Am: Am[:, cc, 0:64]
                    Nprev = lambda cc, Am=Am: Am[:, cc, 64:128]
                    mk, nk_ = kk_('Am%d' % h), kk_('Am%d' % h)
                    for it in range(1, 6 if RWS >= 4 else 1):
                        q = it % 2
                        for cc in range(NCH):
                            P.op('pe', lambda e, cc=cc, Mprev=Mprev, Nprev=Nprev: e.matmul(C.psum[2][0:64, cc * 64:(cc + 1) * 64], Mprev(cc), Nprev(cc), start=True, stop=True),
                                 reads=[mk, nk_], writes=['ps2'])
                        if it < 5:
                            for cc in range(NCH):
                                P.op('pe', lambda e, cc=cc, Mprev=Mprev, Nprev=Nprev: e.matmul(C.psum[3][0:64, cc * 64:(cc + 1) * 64], Nprev(cc), Mprev(cc), start=True, stop=True),
                                     reads=[mk, nk_], writes=['ps3'])
                        Nn, Mn = Ni[q], Mi[q]
                        T('Ni%d_%d' % (h, q), lambda e, Nn=Nn: e.copy(out=Nn[:], in_=C.psum[2][0:64, 0:NCH * 64].rearrange("p (c t) -> p c t", t=64)), ['!ps2'], 'act')
                        if it < 5:
                            T('Mi%d_%d' % (h, q), lambda e, Mn=Mn: e.tensor_copy(out=Mn[:], in_=C.psum[3][0:64, 0:NCH * 64].rearrange("p (c t) -> p c t", t=64)), ['!ps3'])
                        for cc in range(NCH):
                            P.op('pe', lambda e, cc=cc, Nn=Nn, Tt=Tt: e.matmul(C.psum[0][0:64, cc * 64:(cc + 1) * 64], Nn[:, cc, :], Tt[:, cc, :], start=True, stop=True),
                                 reads=[kk_('Ni%d_%d' % (h, q)), kk_('Tt%d' % h)], writes=['ps0'])
                        T('Tt%d' % h, lambda e, Tt=Tt: e.tensor_tensor(out=Tt[:], in0=Tt[:], in1=C.psum[0][0:64, 0:NCH * 64].rearrange("p (c t) -> p c t", t=64), op=ALU.add), ['Tt%d' % h, '!ps0'])
                        Mprev = lambda cc, Mn=Mn: Mn[:, cc, :]
                        Nprev = lambda cc, Nn=Nn: Nn[:, cc, :]
                        mk, nk_ = kk_('Mi%d_%d' % (h, q)), kk_('Ni%d_%d' % (h, q))
            for cc in range(NCH if RWS >= 5 else 0):
                cs = slice(cc * 64, (cc + 1) * 64)
                for stage in range(4):
                    for sl, pp in enumerate(pairs):
                        d = SL[sl]
                        kk_ = lambda n, sl=sl: k_(sl, n)
                        for h in range(2):
                            hs = slice(64 * h, 64 * h + 64)
                            B = 4 + 2 * sl + h
                            bk = 'ps%d' % B
                            Am, Tt, tm = d['Am%d' % h], d['Tt%d' % h], d['tm']
                            Zs, Us = d['Zs%d' % h], d['Us%d' % h]
                            vtm = tm[:, cc, 64 * h:64 * h + 64]
                            btm = tm[:, cc, 128 + 64 * h:192 + 64 * h]
                            ktm = tm[:, cc, 256 + 64 * h:320 + 64 * h]
                            stk = 'ST%d_%d' % (pp, h)
                            ST = STh[pp][h]
                            sfx = '' if h == 0 else '1'
                            if stage == 0:
                                P.op('pe', lambda e, d=d, sfx=sfx, cs=cs, B=B, ST=ST: e.matmul(C.psum[B][0:64, 0:64], d['ah' + sfx][0:64, cs], ST[:], start=True, stop=False),
                                     reads=[kk_('ah' + sfx), stk], writes=[bk])
                                P.op('pe', lambda e, Am=Am, cc=cc, vtm=vtm, B=B: e.matmul(C.psum[B][0:64, 0:64], Am[:, cc, 128:192], vtm, start=False, stop=True),
                                     reads=[kk_('Am%d' % h), kk_('tm')], writes=[bk])
                                P.op('act', lambda e, Zs=Zs, B=B: e.copy(out=Zs[:], in_=C.psum[B][0:64, 0:64]), reads=[bk], writes=[kk_('Zs%d' % h)])
                            elif stage == 1:
                                P.op('pe', lambda e, Tt=Tt, cc=cc, Zs=Zs, B=B: e.matmul(C.psum[B][0:64, 64:128], Tt[:, cc, :], Zs[:], start=True, stop=True),
                                     reads=[kk_('Tt%d' % h), kk_('Zs%d' % h)], writes=[bk])
                                P.op('dve', lambda e, Us=Us, B=B: e.tensor_copy(out=Us[:], in_=C.psum[B][0:64, 64:128]), reads=[bk], writes=[kk_('Us%d' % h)])
                            elif stage == 2:
                                P.op('pe', lambda e, d=d, sfx=sfx, hs=hs, cs=cs, B=B, ST=ST: e.matmul(C.psum[B][hs, 128:192], ST[:], d['rh' + sfx][0:64, cs], start=True, stop=False),
                                     reads=[stk, kk_('rh' + sfx)], writes=[bk])
                                P.op('pe', lambda e, Us=Us, Am=Am, cc=cc, hs=hs, B=B: e.matmul(C.psum[B][hs, 128:192], Us[:], Am[:, cc, 192:256], start=False, stop=False),
                                     reads=[kk_('Us%d' % h), kk_('Am%d' % h)], writes=[bk])
                                P.op('pe', lambda e, vtm=vtm, Am=Am, cc=cc, hs=hs, B=B: e.matmul(C.psum[B][hs, 128:192], vtm, Am[:, cc, 256:320], start=False, stop=True),
                                     reads=[kk_('tm'), kk_('Am%d' % h)], writes=[bk])
                                P.op('act', lambda e, d=d, hs=hs, cs=cs, B=B: e.copy(out=d['yT'][hs, cs], in_=C.psum[B][hs, 128:192]), reads=[bk], writes=[kk_('yT')])
                            else:
                                P.op('pe', lambda e, btm=btm, Us=Us, B=B: e.matmul(C.psum[B][0:64, 192:256], btm, Us[:], start=True, stop=False),
                                     reads=[kk_('tm'), kk_('Us%d' % h)], writes=[bk])
                                P.op('pe', lambda e, ktm=ktm, vtm=vtm, B=B: e.matmul(C.psum[B][0:64, 192:256], ktm, vtm, start=False, stop=True),
                                     reads=[kk_('tm')], writes=[bk])
                                pcn = 'PC' if h == 0 else 'PC1'
                                P.op('dve', lambda e, d=d, pcn=pcn, cc=cc, B=B, ST=ST: e.scalar_tensor_tensor(out=ST[:], in0=ST[:], scalar=d[pcn][0:64, cc:cc + 1],
                                                                                                        in1=C.psum[B][0:64, 192:256], op0=ALU.mult, op1=ALU.add),
                                     reads=[bk, kk_(pcn), stk], writes=[stk])
            for sl, pp in enumerate(pairs):
                d = SL[sl]
                kk_ = lambda n, sl=sl: k_(sl, n)

                def T(out, fn, reads, eng='dve', d=d, kk_=kk_):
                    P.op(eng, fn, reads=[kk_(r) if not r.startswith('!') else r[1:] for r in reads], writes=[kk_(out)])
                P.op('pe', lambda e, d=d: e.matmul(C.psum[0][:, 0:BT], ones64[:], d['yT'][:], start=True, stop=True), reads=['kc', kk_('yT')], writes=['ps0'])
                T('tmp', lambda e, d=d: e.scalar_tensor_tensor(out=d['tmp'][:], in0=C.psum[0][:, 0:BT], scalar=-1.0 / 64, in1=d['yT'][:], op0=ALU.mult, op1=ALU.add), ['!ps0', 'yT'])
                T('tmp2', lambda e, d=d: e.activation(out=d['tmp2'][:], in_=d['tmp'][:], func=AF.Square), ['tmp'], 'act')
                P.op('pe', lambda e, d=d: e.matmul(C.psum[1][:, 0:BT], ones64[:], d['tmp2'][:], start=True, stop=True), reads=['kc', kk_('tmp2')], writes=['ps1'])
                T('tmp2', lambda e, d=d: e.activation(out=d['tmp2'][:], in_=C.psum[1][:, 0:BT], func=AF.Sqrt, scale=1.0 / 64, bias=eps_ln[:, 0:1]), ['!ps1', '!eps_ln'], 'act')
                T('tmp2', lambda e, d=d: e.reciprocal(out=d['tmp2'][:], in_=d['tmp2'][:]), ['tmp2'])
                T('tmp', lambda e, d=d: e.tensor_tensor(out=d['tmp'][:], in0=d['tmp'][:], in1=d['tmp2'][:], op=ALU.mult), ['tmp', 'tmp2'])
                T('tmp', lambda e, d=d, pp=pp: e.tensor_scalar(out=d['tmp'][:], in0=d['tmp'][:], scalar1=K['lng'][:, pp:pp + 1], scalar2=K['lnb'][:, pp:pp + 1], op0=ALU.mult, op1=ALU.add), ['tmp', '!kc'])
                T('tmp', lambda e, d=d: e.tensor_tensor(out=d['tmp'][:], in0=d['tmp'][:], in1=d['bonus'][:], op=ALU.add), ['tmp', 'bonus'], 'pool')
                T('yT', lambda e, d=d: e.tensor_tensor(out=d['yT'][:], in0=d['tmp'][:], in1=d['g'][:], op=ALU.mult), ['tmp', 'g'])
                P.op('sp', lambda e, d=d, pp=pp, t0=t0: e.dma_start(out=yout[pp, :, t0:t0 + BT], in_=d['yT'][:]), reads=[kk_('yT')], lane=kk_('yT') + '_st')


def decl_mix1a(nc, D, S, sfx=''):
    KC = D // 128
    di = lambda n, s: nc.dram_tensor(n + sfx, s, F32, kind="ExternalInput").ap()
    T = dict(wint=di("wint1", [24, 128, KC * 128]))
    T['cd'] = {n: di(n, shp) for n, shp in dict(convw=[128, 4, 31], convb=[128, 4], mu3=[128, 12], mux=[128, 4], w0=[128, 4], a0=[128, 4],
                                                k_k=[128, 4], k_a=[128, 4], r_k=[128, 4], lng=[128, 4], lnb=[128, 4], w_up=[128, 512],
                                                a_up=[128, 512], g_up0=[128, 512], g_up1=[128, 512], amask=[64, 320], blockones=[128, 128],
                                                identrep=[64, 512], ident1=[128, 128], rmask=[128, 512]).items()}
    return T


def build_mix1a(D, S, TT=1024):
    nc = bass.Bass("TRN2", target_bir_lowering=False)
    KC = D // 128
    TT = min(TT, S)
    di = lambda n, s: nc.dram_tensor(n, s, F32, kind="ExternalInput").ap()
    x = di("x", [KC, 128, S])
    ng, sc, shf = di("ng", [128, KC]), di("sc", [128, KC]), di("shf", [128, KC])
    T = decl_mix1a(nc, D, S)
    convo = nc.dram_tensor("convo", [4, 128, S], F32, kind="ExternalOutput").ap()
    yout = nc.dram_tensor("yout", [4, 128, S], F32, kind="ExternalOutput").ap()
    P = Prog(nc)
    C = Common(P, D, TT)
    stage_mix1a(P, C, nc, T, x, ng, sc, shf, lambda c, t0, n: convo[c, :, t0:t0 + n], yout, S)
    P.finish()
    return nc


def stage_mix1a(P, C, nc, T, x, ng, sc, shf, convo_put, yout, S, sfx=''):
    wint, cd = T['wint'], T['cd']
    TT = C.TT
    BT = min(512, S)
    scr = lambda n, s: nc.dram_tensor(n + sfx, s, F32, kind="Internal").ap()
    u = scr("u_s", [4, 128, S])
    rr, kr_, vr_ = scr("r_s", [4, 128, S]), scr("k_s", [4, 128, S]), scr("v_s", [4, 128, S])
    xw, xa, xg = scr("xw_s", [128, S]), scr("xa_s", [128, S]), scr("xg_s", [2, 128, S])
    P.phase_begin()
    stg = Stager(P, 'stg', 4, F32)
    stp = Stager(P, 'stp', 4, F32)

    def whole(dst):
        def f(t, k, tok0):
            st_dma(P, dst[:, tok0:tok0 + 512], t[:], k)
        return f

    def glu_epi(dst):
        def epi(banks, tok0, hh):
            bA, bB = banks
            s_, sk = stg.get()
            o, ok = stg.get()
            P.op('act', lambda e: e.activation(out=s_[:], in_=C.psum[bB][:], func=AF.Sigmoid), reads=['ps%d' % bB], writes=[sk])
            P.op('dve', lambda e: e.tensor_tensor(out=o[:], in0=C.psum[bA][:], in1=s_[:], op=ALU.mult), reads=['ps%d' % bA, sk], writes=[ok])
            st_dma(P, dst[:, tok0:tok0 + 512], o[:], ok)
        return epi
    groups = [(2, glu_epi(u[c])) for c in range(4)]
    for T_ in (rr, kr_, vr_):
        for c in range(4):
            groups.append((1, make_act_epi(P, C, stp, AF.Copy, whole(T_[c]))))
    for dst in (xw, xa, xg[0], xg[1]):
        groups.append((1, make_act_epi(P, C, stp, AF.Copy, whole(dst))))
    emit_inproj(P, C, x, ng, sc, shf, wint, S, groups)
    P.phase_end()
    import os as _os
    PH = int(_os.environ.get('PH', '7'))
    P.phase_begin()
    cw = load_vec(P, 'cw', cd['convw'].rearrange("p c k -> p (c k)"), 4 * 31)
    cb = load_vec(P, 'cb', cd['convb'], 4)
    P.barrier()
    up = [P.sb('up%d' % i, [128, S + 30], F32) for i in range(2)]
    acc = [P.sb('acc%d' % i, [128, S], F32) for i in range(2)]
    for c in range(4 if PH & 2 else 0):
        ut, uk = up[c % 2], 'up%d' % (c % 2)
        at, ak = acc[c % 2], 'acc%d' % (c % 2)
        P.op('dve', lambda e, ut=ut: e.memset(ut[:, 0:30], 0.0), writes=[uk])
        P.op('sp', lambda e, ut=ut, c=c: e.dma_start(out=ut[:, 30:30 + S], in_=u[c]), writes=[uk], lane=uk)
        P.op('dve', lambda e, ut=ut, at=at, c=c: e.tensor_scalar(out=at[:], in0=ut[:, 0:S], scalar1=cw[:, c * 31:c * 31 + 1], scalar2=cb[:, c:c + 1],
                                                                op0=ALU.mult, op1=ALU.add), reads=[uk], writes=[ak])
        for k in range(1, 31):
            P.op('dve', lambda e, ut=ut, at=at, c=c, k=k: e.scalar_tensor_tensor(out=at[:], in0=ut[:, k:k + S], scalar=cw[:, c * 31 + k:c * 31 + k + 1], in1=at[:],
                                                                                op0=ALU.mult, op1=ALU.add), reads=[uk, ak], writes=[ak])
        for sb_ in range(S // 512):
            P.op('sp', lambda e, at=at, c=c, sb_=sb_: e.dma_start(out=convo_put(c, sb_ * 512, 512), in_=at[:, sb_ * 512:(sb_ + 1) * 512]), reads=[ak], lane=ak + '_st')
    P.phase_end()
    P.phase_begin()
    spec = {n: (cd[n], list(cd[n].shape), F32) for n in ('mu3', 'mux', 'w0', 'a0', 'k_k', 'k_a', 'r_k', 'lng', 'lnb', 'w_up', 'a_up', 'g_up0', 'g_up1',
                                                        'amask', 'blockones', 'identrep', 'rmask')}
    spec['ident_f'] = (cd['ident1'], [128, 128], F32)
    K = load_consts(P, spec)
    P.barrier()
    if PH & 4:
        emit_rwkv(P, C, K, dict(r=rr, k=kr_, v=vr_, xw=xw, xa=xa, xg=xg), yout, S, BT=BT)
    P.phase_end()


def build_mix1b(D, S, TT=1024):
    nc = bass.Bass("TRN2", target_bir_lowering=False)
    KC = D // 128
    TT = min(TT, S)
    di = lambda n, s: nc.dram_tensor(n, s, F32, kind="ExternalInput").ap()
    call = di("call", [16, 128, S])
    own = di("own", [4, 128, S])
    yr = di("yr", [4, 128, S])
    lg, lb = di("lg", [128, 4]), di("lb", [128, 4])
    woutt = di("woutt", [KC, 128, 8 * 128])
    part = nc.dram_tensor("part", [KC, 128, S], F32, kind="ExternalOutput").ap()
    P = Prog(nc)
    C = Common(P, D, TT)
    stage_mix1b(P, C, nc, lambda c, t0, n: call[c, :, t0:t0 + n], lambda c, t0, n: own[c, :, t0:t0 + n], yr, lg, lb, woutt, part, S)
    P.finish()
    return nc


def stage_mix1b(P, C, nc, call_get, own_get, yr, lg, lb, woutt, part, S, sfx=''):
    un = nc.dram_tensor("un_s" + sfx, [4, 128, S], F32, kind="Internal").ap()
    P.phase_begin()
    lgt = load_vec(P, 'lgt', lg, 4)
    lbt = load_vec(P, 'lbt', lb, 4)
    eps = P.sb('eps5', [128, 1], F32)
    P.op('dve', lambda e: e.memset(eps[:], 1e-5), writes=['eps5'])
    P.barrier()
    xs = Stager(P, 'cx', 3, F32)
    sq = Stager(P, 'csq', 2, F32)
    mean, msq, rstd = P.sb('mean', [128, 512], F32), P.sb('msq', [128, 512], F32), P.sb('rstd5', [128, 512], F32)
    ost = Stager(P, 'co', 2, F32)
    for hh in range(S // 512):
        sl = slice(hh * 512, (hh + 1) * 512)
        for c in range(16):
            xt, xk = xs.get()
            st, sk = sq.get()
            P.op('sp', lambda e, xt=xt, c=c, sl=sl: e.dma_start(out=xt[:], in_=call_get(c, sl.start, 512)), writes=[xk], lane=xk)
            P.op('act', lambda e, xt=xt, st=st: e.activation(out=st[:], in_=xt[:], func=AF.Square), reads=[xk], writes=[sk])
            P.op('pe', lambda e, xt=xt, c=c: e.matmul(C.psum[0][:], C.ones_f[:], xt[:], start=(c == 0), stop=(c == 15)), reads=[xk, 'ones_f'], writes=['ps0'])
            P.op('pe', lambda e, st=st, c=c: e.matmul(C.psum[1][:], C.ones_f[:], st[:], start=(c == 0), stop=(c == 15)), reads=[sk, 'ones_f'], writes=['ps1'])
        P.op('act', lambda e: e.activation(out=mean[:], in_=C.psum[0][:], func=AF.Copy, scale=1.0 / 2048), reads=['ps0'], writes=['mean'])
        P.op('act', lambda e: e.activation(out=msq[:], in_=mean[:], func=AF.Square), reads=['mean'], writes=['msq'])
        P.op('dve', lambda e: e.scalar_tensor_tensor(out=rstd[:], in0=C.psum[1][:], scalar=1.0 / 2048, in1=msq[:], op0=ALU.mult, op1=ALU.subtract),
             reads=['ps1', 'msq'], writes=['rstd5'])
        P.op('act', lambda e: e.activation(out=rstd[:], in_=rstd[:], func=AF.Sqrt, bias=eps[:, 0:1]), reads=['rstd5', 'eps5'], writes=['rstd5'])
        P.op('dve', lambda e: e.reciprocal(out=rstd[:], in_=rstd[:]), reads=['rstd5'], writes=['rstd5'])
        for c in range(4):
            xt, xk = xs.get()
            o, ok = ost.get()
            P.op('sp', lambda e, xt=xt, c=c, sl=sl: e.dma_start(out=xt[:], in_=own_get(c, sl.start, 512)), writes=[xk], lane=xk)
            P.op('dve', lambda e, xt=xt: e.tensor_tensor(out=xt[:], in0=xt[:], in1=mean[:], op=ALU.subtract), reads=[xk, 'mean'], writes=[xk])
            P.op('pool', lambda e, xt=xt: e.tensor_tensor(out=xt[:], in0=xt[:], in1=rstd[:], op=ALU.mult), reads=[xk, 'rstd5'], writes=[xk])
            P.op('dve', lambda e, xt=xt, c=c: e.tensor_scalar(out=xt[:], in0=xt[:], scalar1=lgt[:, c:c + 1], scalar2=lbt[:, c:c + 1], op0=ALU.mult, op1=ALU.add),
                 reads=[xk], writes=[xk])
            P.op('act', lambda e, xt=xt, o=o: e.activation(out=o[:], in_=xt[:], func=AF.Silu), reads=[xk], writes=[ok])
            st_dma(P, un[c, :, sl], o[:], ok)
    P.phase_end()
    P.phase_begin()
    emit_outproj(P, C, [un[c] for c in range(4)] + [yr[c] for c in range(4)], woutt, part, S, 8)
    P.phase_end()


def mix1_cols(j):
    CC, RD = 2048, 2048
    cols = []
    for c in range(4):
        ch = j * 512 + c * 128 + np.arange(128)
        cols += [ch, CC + ch]
    base = 2 * CC
    for off in (0, RD, 2 * RD):
        for c in range(4):
            cols.append(base + off + j * 512 + c * 128 + np.arange(128))
    pad = -np.ones(32, np.int64)
    cols.append(np.concatenate([base + 3 * RD + np.arange(96), pad]))
    cols.append(np.concatenate([base + 3 * RD + 96 + np.arange(96), pad]))
    cols.append(base + 3 * RD + 192 + np.arange(128))
    cols.append(base + 3 * RD + 192 + 128 + np.arange(128))
    return np.concatenate(cols)


def take_cols(w, idx):
    out = np.zeros((w.shape[0], len(idx)), np.float32)
    ok = idx >= 0
    out[:, ok] = w[:, idx[ok]]
    return out


def pad_rows(w, n):
    out = np.zeros((n, w.shape[1]), np.float32)
    out[:w.shape[0]] = w
    return out


def mix1a_inputs(x_b, ng, sc, shf, p, j, S):
    D = p['odd_w_in'].shape[0]
    KC = D // 128
    ch = slice(j * 512, (j + 1) * 512)
    mu = p['rwkv_mu']
    pv = lambda v: chunked(v[ch], 4)
    im = dict(x=(fm(x_b, KC) if x_b is not None else None), ng=chunked(ng, KC), sc=chunked(sc, KC), shf=chunked(shf, KC),
              wint1=tile_w(take_cols(p['odd_w_in'], mix1_cols(j)), KC),
              convw=np.ascontiguousarray(p['conv_w'][:, ch].reshape(31, 4, 128).transpose(2, 1, 0)),
              convb=pv(p['conv_b']),
              mu3=np.concatenate([pv(mu[0:2048]), pv(mu[2048:4096]), pv(mu[4096:6144])], axis=1),
              mux=np.stack([np.pad(mu[6144:6240], (0, 32)), np.pad(mu[6240:6336], (0, 32)), mu[6336:6464], mu[6464:6592]], axis=1).astype(np.float32),
              w0=pv(p['rwkv_w0']), a0=pv(p['rwkv_a0']), k_k=pv(p['rwkv_k_k']), k_a=pv(p['rwkv_k_a']), r_k=pv(p['rwkv_r_k'].reshape(-1)),
              lng=pv(p['rwkv_lnx_g']), lnb=pv(p['rwkv_lnx_b']),
              w_up=pad_rows(p['rwkv_w_up'][:, ch], 128), a_up=pad_rows(p['rwkv_a_up'][:, ch], 128),
              g_up0=np.ascontiguousarray(p['rwkv_g_up'][0:128, ch]), g_up1=np.ascontiguousarray(p['rwkv_g_up'][128:256, ch]),
              ident1=np.eye(128, dtype=np.float32))
    rmask = np.ones((128, 512), np.float32)
    rmask[:, 0::64] = 0.0
    im['rmask'] = rmask
    im.update(rwkv_const_arrays())
    return im


def mix1b_inputs(conv_all_fm, own_fm, yr_fm, p, j, D):
    ch = slice(j * 512, (j + 1) * 512)
    rows = np.concatenate([np.arange(j * 512, (j + 1) * 512), 2048 + np.arange(j * 512, (j + 1) * 512)])
    return dict(call=conv_all_fm, own=own_fm, yr=yr_fm, lg=chunked(p['conv_ln_g'][ch], 4), lb=chunked(p['conv_ln_b'][ch], 4),
                woutt=tile_w(p['odd_w_out'][rows, :], 8))


def build_ada(D, NCH):
    nc = bass.Bass("TRN2", target_bir_lowering=False)
    KC = D // 128
    ct = nc.dram_tensor("ct", [128, KC * 2], F32, kind="ExternalInput").ap()
    wt = nc.dram_tensor("wt", [NCH, 128, KC * 128], F32, kind="ExternalInput").ap()
    bt = nc.dram_tensor("bt", [128, NCH], F32, kind="ExternalInput").ap()
    mod = nc.dram_tensor("mod", [128, NCH * 2], F32, kind="ExternalOutput").ap()
    P = Prog(nc)
    C = Common(P, D, 512)
    c_sb = load_vec(P, 'c_sb', ct, KC * 2)
    b_sb = load_vec(P, 'b_sb', bt, NCH)
    P.op('act', lambda e: e.activation(out=c_sb[:], in_=c_sb[:], func=AF.Silu), reads=['c_sb'], writes=['c_sb'])
    res = P.sb('res', [128, NCH * 2], F32)
    slots = [P.sb('aw%d' % i, [128, KC * 128], F32) for i in range(3)]
    for m in range(NCH):
        w, wk = slots[m % 3], 'aw%d' % (m % 3)
        P.op('sp', lambda e, w=w, m=m: e.dma_start(out=w[:], in_=wt[m]), writes=[wk], lane=wk)
        b = m % 8
        for k in range(KC):
            P.op('pe', lambda e, w=w, k=k, b=b: e.matmul(C.psum[b][:, 0:2], w[:, k * 128:(k + 1) * 128], c_sb[:, 2 * k:2 * k + 2], start=(k == 0), stop=(k == KC - 1)),
                 reads=[wk, 'c_sb'], writes=['ps%d' % b])
        P.op('dve', lambda e, m=m, b=b: e.tensor_scalar(out=res[:, 2 * m:2 * m + 2], in0=C.psum[b][:, 0:2], scalar1=b_sb[:, m:m + 1], scalar2=None, op0=ALU.add),
             reads=['ps%d' % b, 'b_sb'], writes=['res'])
    P.op('sp', lambda e: e.dma_start(out=mod, in_=res[:]), reads=['res'], lane='res_st')
    P.finish()
    return nc


NCORE = 8
_progs = {}


def _dbg(name, a):
    import os
    if os.environ.get('KDEBUG'):
        a = np.asarray(a)
        print('[dbg]', name, a.shape, float(np.abs(a).mean()), float(np.abs(a).max()), bool(np.isfinite(a).all()), flush=True)


def _prog(key, builder):
    if key not in _progs:
        _progs[key] = builder()
    return _progs[key]


def _run(nc, in_maps):
    import os
    if os.environ.get('KTRACE'):
        r = run_bass_kernel_spmd(nc, in_maps, core_ids=list(range(NCORE)), trace=True)
        print('[ktrace] exec_time_ns', r.exec_time_ns, flush=True)
        return r.results
    return run_bass_kernel_spmd(nc, in_maps, core_ids=list(range(NCORE))).results


def run_ada(c, w_ada, b_ada):
    depth, D, N6 = w_ada.shape
    B = c.shape[0]
    KC = D // 128
    per = depth * N6 // NCORE
    nch = per // 128
    nc = build_ada(D, nch)
    ct = np.ascontiguousarray(c.T.reshape(KC, 128, B).transpose(1, 0, 2).reshape(128, KC * B))
    ims = []
    for i in range(NCORE):
        g0 = i * per
        layer, col0 = divmod(g0, N6)
        wsl = w_ada[layer][:, col0:col0 + per]
        ims.append(dict(ct=ct, wt=tile_w(wsl, KC), bt=chunked(b_ada[layer][col0:col0 + per], nch)))
    res = _run(nc, ims)
    flat = np.zeros((depth * N6, B), np.float32)
    for i in range(NCORE):
        m = res[i]["mod"].reshape(128, nch, B)
        flat[i * per:(i + 1) * per] = m.transpose(1, 0, 2).reshape(per, B)
    mods = flat.reshape(depth, 6, D, B).transpose(0, 1, 3, 2)
    return np.ascontiguousarray(mods)


def run_reduce(parts, x_fm, ng, gate, D, S, B):
    KC = D // 128
    G = NCORE // B
    TS = S // G
    nc = _prog(('red', D, TS, G), lambda: build_reduce(D, TS, G))
    ims = []
    for i in range(NCORE):
        b, q = divmod(i, G)
        ts = slice(q * TS, (q + 1) * TS)
        ims.append(dict(part=np.ascontiguousarray(np.stack([parts[b * G + g][:, :, ts] for g in range(G)])),
                        x=np.ascontiguousarray(x_fm[b][:, :, ts]), ng=chunked(ng, KC), gate=chunked(gate[b], KC)))
    res = _run(nc, ims)
    out = []
    for b in range(B):
        out.append(np.ascontiguousarray(np.concatenate([res[b * G + q]["xo"] for q in range(G)], axis=2)))
    return out


def kernel_unfused(x, c, w_ada, b_ada, norm_g, w_ffn_in, w_ffn_out, even_w_in, even_w_out, odd_w_in, odd_w_out, conv_w, conv_b,
           conv_ln_g, conv_ln_b, rwkv_mu, rwkv_w0, rwkv_w_up, rwkv_a0, rwkv_a_up, rwkv_g_up, rwkv_k_k, rwkv_k_a, rwkv_r_k,
           rwkv_lnx_g, rwkv_lnx_b):
    f = lambda a: np.asarray(a, dtype=np.float32)
    x, c = f(x), f(c)
    B, S, D = x.shape
    KC = D // 128
    G = NCORE // B
    depth = w_ada.shape[0]
    mods = run_ada(c, f(w_ada), f(b_ada))
    _dbg('mods', mods)
    x_fm = [fm(x[b], KC) for b in range(B)]
    FH = w_ffn_out.shape[1]
    HCt = -(-(FH // 128) // G)
    for layer in range(depth):
        sh_m, sc_m, g_m, sh_f, sc_f, g_f = [mods[layer, i] for i in range(6)]
        jj = layer // 2
        if layer % 2 == 0:
            nc = _prog(('mix0', D, S), lambda: build_mix0(D, S))
            w_in, w_out = f(even_w_in[jj]), f(even_w_out[jj])
            wl = [(tile_w(w_in[:, mix0_cols(j)], KC), tile_w(w_out[mix0_rows(j), :], 12)) for j in range(G)]
            ims = []
            for i in range(NCORE):
                b, j = divmod(i, G)
                im = mix0_inputs_fast(x_fm[b], norm_g[layer, 0], sc_m[b], sh_m[b], wl[j], j, S)
                ims.append(im)
            res = _run(nc, ims)
            parts = [res[i]["part"] for i in range(NCORE)]
            del res, ims, wl
        else:
            p = dict(odd_w_in=f(odd_w_in[jj]), odd_w_out=f(odd_w_out[jj]), conv_w=f(conv_w[jj]), conv_b=f(conv_b[jj]),
                     conv_ln_g=f(conv_ln_g[jj]), conv_ln_b=f(conv_ln_b[jj]), rwkv_mu=f(rwkv_mu[jj]), rwkv_w0=f(rwkv_w0[jj]),
                     rwkv_w_up=f(rwkv_w_up[jj]), rwkv_a0=f(rwkv_a0[jj]), rwkv_a_up=f(rwkv_a_up[jj]), rwkv_g_up=f(rwkv_g_up[jj]),
                     rwkv_k_k=f(rwkv_k_k[jj]), rwkv_k_a=f(rwkv_k_a[jj]), rwkv_r_k=f(rwkv_r_k[jj]), rwkv_lnx_g=f(rwkv_lnx_g[jj]),
                     rwkv_lnx_b=f(rwkv_lnx_b[jj]))
            nca = _prog(('mix1a', D, S), lambda: build_mix1a(D, S))
            base = [mix1a_inputs(None, norm_g[layer, 0], sc_m[0], sh_m[0], p, j, S) for j in range(G)]
            ims = []
            for i in range(NCORE):
                b, j = divmod(i, G)
                im = dict(base[j])
                im.update(x=x_fm[b], sc=chunked(sc_m[b], KC), shf=chunked(sh_m[b], KC))
                ims.append(im)
            res = _run(nca, ims)
            ncb = _prog(('mix1b', D, S), lambda: build_mix1b(D, S))
            ims = []
            for i in range(NCORE):
                b, j = divmod(i, G)
                call = np.ascontiguousarray(np.concatenate([res[b * G + g]["convo"] for g in range(G)], axis=0))
                ims.append(mix1b_inputs(call, res[i]["convo"], res[i]["yout"], p, j, D))
            res = _run(ncb, ims)
            parts = [res[i]["part"] for i in range(NCORE)]
            del res, ims, base
        _dbg('parts_mix', parts[0])
        x_fm = run_reduce(parts, x_fm, norm_g[layer, 1], g_m, D, S, B)
        _dbg('x_after_mix', x_fm[0])
        del parts
        nc = _prog(('ffn', D, S, HCt), lambda: build_ffn(D, S, HCt))
        w1, w2 = f(w_ffn_in[layer]), f(w_ffn_out[layer])
        wl = []
        for j in range(G):
            h0, h1 = j * HCt * 128, min((j + 1) * HCt * 128, FH)
            g_ = np.zeros((D, HCt * 128), np.float32)
            u_ = np.zeros((D, HCt * 128), np.float32)
            g_[:, :h1 - h0] = w1[:, h0:h1]
            u_[:, :h1 - h0] = w1[:, FH + h0:FH + h1]
            w1t = np.empty((2 * HCt, 128, KC * 128), np.float32)
            w1t[0::2] = tile_w(g_, KC)
            w1t[1::2] = tile_w(u_, KC)
            w2p = np.zeros((HCt * 128, D), np.float32)
            w2p[:h1 - h0] = w2[h0:h1]
            wl.append((w1t, tile_w(w2p, HCt)))
        ims = []
        for i in range(NCORE):
            b, j = divmod(i, G)
            ims.append(dict(x=x_fm[b], ng=chunked(norm_g[layer, 2], KC), sc=chunked(sc_f[b], KC), shf=chunked(sh_f[b], KC),
                            w1t=wl[j][0], w2t=wl[j][1]))
        res = _run(nc, ims)
        parts = [res[i]["y"] for i in range(NCORE)]
        del res, ims, wl
        _dbg('parts_ffn', parts[0])
        x_fm = run_reduce(parts, x_fm, norm_g[layer, 3], g_f, D, S, B)
        _dbg('x_after_ffn', x_fm[0])
        del parts
    out = np.stack([x_fm[b].reshape(D, S).T for b in range(B)]).astype(np.float32)
    return np.ascontiguousarray(out)


def mix0_inputs_fast(xfm_b, ng, sc, shf, wl, j, S):
    KC = xfm_b.shape[0]
    cm, sm = rope_np(S, 128)
    cr, sr = rope_np(S, 256)
    rope = np.ascontiguousarray(np.stack([np.concatenate([cm.T, cm.T]), np.concatenate([sm.T, sm.T]), cr.T, sr.T]).astype(np.float32))
    im = dict(x=xfm_b, ng=chunked(ng, KC), sc=chunked(sc, KC), shf=chunked(shf, KC), wint=wl[0], woutt=wl[1], rope=rope)
    im.update(moba_const_arrays(S))
    im.update(ret_const_arrays([2 * j, 2 * j + 1]))
    return im


GRP4 = [[0, 1, 2, 3], [4, 5, 6, 7]]
GRP8 = [[0, 1, 2, 3, 4, 5, 6, 7]]
AG_MAX_BYTES = 1 << 20


def emit_cc(P, kind, groups, src2d, dst2d, reads=(), writes=(), chain=True):
    op = ALU.bypass if kind == 'AllGather' else ALU.add
    ch = ['cc_chain'] if chain else []
    P.op('pool', lambda e: e.collective_compute(kind, op, replica_groups=groups, ins=[src2d], outs=[dst2d]),
         reads=list(reads) + ch, writes=list(writes) + ch, lane='cc', lane_inc=1)


def build_fused(D, S, HCt, depth=2):
    nc = bass.Bass("TRN2", target_bir_lowering=False)
    KC = D // 128
    G = 4
    TS = S // G
    TT = min(1024, TS)
    N6 = 6 * D
    per = depth * N6 // 4
    nch = per // 128
    di = lambda n, s: nc.dram_tensor(n, s, F32, kind="ExternalInput").ap()
    scr = lambda n, s: nc.dram_tensor(n, s, F32, kind="Internal").ap()
    x_full = di("x_full", [KC, 128, S])
    x_own = di("x_own", [KC, 128, TS])
    sel = di("sel", [128, 2])
    ngall = di("ngall", [depth * 4, 128, KC])
    ct, wt_ada, bt_ada = di("ct", [128, KC * 2]), di("wt_ada", [nch, 128, KC * 128]), di("bt_ada", [128, nch])
    T0 = decl_mix0(nc, D, S)
    T1 = decl_mix1a(nc, D, S)
    lg, lb = di("lg", [128, 4]), di("lb", [128, 4])
    woutt1 = di("woutt1", [KC, 128, 8 * 128])
    w1t = [di("w1t_%d" % l, [2 * HCt, 128, KC * 128]) for l in range(depth)]
    w2t = [di("w2t_%d" % l, [KC, 128, HCt * 128]) for l in range(depth)]
    out = nc.dram_tensor("out", [KC, 128, TS], F32, kind="ExternalOutput").ap()
    part_rs = scr("part_rs", [G, KC, 128, TS])
    red = scr("red", [1, KC, 128, TS])
    xo = [scr("xo%d" % i, [KC, 128, TS]) for i in range(2)]
    rp = min(KC * 128, max(1, AG_MAX_BYTES // (TS * 4)))
    npc = KC * 128 // rp
    xg = scr("xg", [npc, G, rp, TS])
    ada_src = scr("ada_src", [128, 2 * nch])
    ada_all = scr("ada_all", [4 * 128, 2 * nch])
    mods_d = scr("mods_d", [depth * 6, 128, KC])
    NS = S // 512
    convo_s = scr("convo_s", [NS, 4 * 128, 512])
    callg = scr("callg", [NS, G * 4 * 128, 512])
    yout = scr("yout", [4, 128, S])

    P = Prog(nc)
    C = Common(P, D, TT)

    def part_put(m, t0, n):
        return part_rs[t0 // TS, m, :, t0 % TS:t0 % TS + n]

    def xg_get(k, t0, n):
        pc, r0 = divmod(k * 128, rp)
        return xg[pc, t0 // TS, r0:r0 + 128, t0 % TS:t0 % TS + n]

    def exchange(x_prev_own, x_new_own, ng_ap, gate_ap, gather=True):
        P.barrier()
        emit_cc(P, 'ReduceScatter', GRP4, part_rs.rearrange("g k p t -> (g k p) t"), red.rearrange("o k p t -> (o k p) t"))
        P.barrier()
        P.phase_begin()
        emit_reduce(P, C, red, x_prev_own, ng_ap, gate_ap, x_new_own, TS, 1)
        P.phase_end()
        if gather:
            src2d = x_new_own.rearrange("k p t -> (k p) t")
            for pc in range(npc):
                emit_cc(P, 'AllGather', GRP4, src2d[pc * rp:(pc + 1) * rp, :], xg[pc].rearrange("g r t -> (g r) t"), chain=False)
            P.barrier()

    P.phase_begin()
    c_sb = load_vec(P, 'c_sb', ct, KC * 2)
    b_sb = load_vec(P, 'b_sb', bt_ada, nch)
    sel_sb = load_vec(P, 'sel_sb', sel, 2)
    P.op('act', lambda e: e.activation(out=c_sb[:], in_=c_sb[:], func=AF.Silu), reads=['c_sb'], writes=['c_sb'])
    res = P.sb('res', [128, 2 * nch], F32)
    slots = [P.sb('aw%d' % i, [128, KC * 128], F32) for i in range(3)]
    for m in range(nch):
        w, wk = slots[m % 3], 'aw%d' % (m % 3)
        P.op('sp', lambda e, w=w, m=m: e.dma_start(out=w[:], in_=wt_ada[m]), writes=[wk], lane=wk)
        b = m % 8
        for k in range(KC):
            P.op('pe', lambda e, w=w, k=k, b=b: e.matmul(C.psum[b][:, 0:2], w[:, k * 128:(k + 1) * 128], c_sb[:, 2 * k:2 * k + 2], start=(k == 0), stop=(k == KC - 1)),
                 reads=[wk, 'c_sb'], writes=['ps%d' % b])
        for bb in range(2):
            P.op('dve', lambda e, m=m, b=b, bb=bb: e.tensor_scalar(out=res[:, bb * nch + m:bb * nch + m + 1], in0=C.psum[b][:, bb:bb + 1], scalar1=b_sb[:, m:m + 1],
                                                                  scalar2=None, op0=ALU.add), reads=['ps%d' % b, 'b_sb'], writes=['res'])
    P.op('sp', lambda e: e.dma_start(out=ada_src, in_=res[:]), reads=['res'], lane='res_st')
    P.barrier()
    emit_cc(P, 'AllGather', GRP4, ada_src, ada_all)
    P.barrier()
    vb = [P.sb('vb%d' % b, [128, depth * 6, KC], F32) for b in range(2)]
    vm_ = P.sb('vmix', [128, depth * 6, KC], F32)
    for v in range(depth * 6):
        gc0 = v * KC
        k0 = 0
        while k0 < KC:
            i, m0 = divmod(gc0 + k0, nch)
            ln = min(KC - k0, nch - m0)
            for b in range(2):
                P.op('sp', lambda e, b=b, v=v, k0=k0, ln=ln, i=i, m0=m0: e.dma_start(out=vb[b][:, v, k0:k0 + ln], in_=ada_all[i * 128:(i + 1) * 128, b * nch + m0:b * nch + m0 + ln]),
                     writes=['vb%d' % b], lane='vb%d_%d' % (b, (v * 2 + (1 if k0 else 0)) % 8))
            k0 += ln
    P.barrier()
    P.op('dve', lambda e: e.tensor_scalar(out=vm_[:], in0=vb[0][:], scalar1=sel_sb[:, 0:1], scalar2=None, op0=ALU.mult), writes=['vmix'])
    P.op('dve', lambda e: e.scalar_tensor_tensor(out=vm_[:], in0=vb[1][:], scalar=sel_sb[:, 1:2], in1=vm_[:], op0=ALU.mult, op1=ALU.add), reads=['vmix'], writes=['vmix'])
    P.op('sp', lambda e: e.dma_start(out=mods_d.rearrange("v p k -> p v k"), in_=vm_[:]), reads=['vmix'], lane='vmix_st')
    P.phase_end()

    cur = 0
    x_prev = x_own
    x_src = x_full
    for layer in range(depth):
        sh_m, sc_m, g_m, sh_f, sc_f, g_f = [mods_d[layer * 6 + i] for i in range(6)]
        if layer % 2 == 0:
            stage_mix0(P, C, nc, T0, x_src, ngall[layer * 4 + 0], sc_m, sh_m, part_put, S)
        else:
            def convo_put(c, t0, n):
                return convo_s[t0 // 512, c * 128:(c + 1) * 128, :]
            stage_mix1a(P, C, nc, T1, x_src, ngall[layer * 4 + 0], sc_m, sh_m, convo_put, yout, S)
            P.barrier()
            for sb_ in range(NS):
                emit_cc(P, 'AllGather', GRP4, convo_s[sb_], callg[sb_], chain=False)
            P.barrier()
            stage_mix1b(P, C, nc, lambda c, t0, n: callg[t0 // 512, c * 128:(c + 1) * 128, :],
                        lambda c, t0, n: convo_s[t0 // 512, c * 128:(c + 1) * 128, :], yout, lg, lb, woutt1, part_put, S)
        exchange(x_prev, xo[cur], ngall[layer * 4 + 1], g_m)
        x_prev, x_src = xo[cur], xg_get
        cur ^= 1
        P.phase_begin()
        emit_ffn(P, C, x_src, ngall[layer * 4 + 2], sc_f, sh_f, w1t[layer], w2t[layer], part_put, S, HCt)
        P.phase_end()
        last = (layer == depth - 1)
        exchange(x_prev, out if last else xo[cur], ngall[layer * 4 + 3], g_f, gather=not last)
        x_prev = xo[cur]
        cur ^= 1
    print('[fused] ops=%d sems=%d counts=%s' % (P.n_ops, len(P.sem), {k: v for k, v in P.cnt.items() if k in ENG_ATTR}), flush=True)
    P.finish()
    return nc


def kernel_fused(x, c, w_ada, b_ada, norm_g, w_ffn_in, w_ffn_out, even_w_in, even_w_out, odd_w_in, odd_w_out, conv_w, conv_b,
                 conv_ln_g, conv_ln_b, rwkv_mu, rwkv_w0, rwkv_w_up, rwkv_a0, rwkv_a_up, rwkv_g_up, rwkv_k_k, rwkv_k_a, rwkv_r_k,
                 rwkv_lnx_g, rwkv_lnx_b):
    f = lambda a: np.asarray(a, dtype=np.float32)
    x, c = f(x), f(c)
    B, S, D = x.shape
    KC = D // 128
    G = NCORE // B
    TS = S // G
    depth = w_ada.shape[0]
    FH = w_ffn_out.shape[1]
    HCt = -(-(FH // 128) // G)
    N6 = 6 * D
    per = depth * N6 // G
    nch = per // 128
    nc = build_fused(D, S, HCt, depth)
    w_ada, b_ada = f(w_ada), f(b_ada)
    x_fm = [fm(x[b], KC) for b in range(B)]
    ct = np.ascontiguousarray(c.T.reshape(KC, 128, B).transpose(1, 0, 2).reshape(128, KC * B))
    ngall = np.ascontiguousarray(np.stack([chunked(f(norm_g[l, i]), KC) for l in range(depth) for i in range(4)]))
    p = dict(odd_w_in=f(odd_w_in[0]), odd_w_out=f(odd_w_out[0]), conv_w=f(conv_w[0]), conv_b=f(conv_b[0]),
             conv_ln_g=f(conv_ln_g[0]), conv_ln_b=f(conv_ln_b[0]), rwkv_mu=f(rwkv_mu[0]), rwkv_w0=f(rwkv_w0[0]),
             rwkv_w_up=f(rwkv_w_up[0]), rwkv_a0=f(rwkv_a0[0]), rwkv_a_up=f(rwkv_a_up[0]), rwkv_g_up=f(rwkv_g_up[0]),
             rwkv_k_k=f(rwkv_k_k[0]), rwkv_k_a=f(rwkv_k_a[0]), rwkv_r_k=f(rwkv_r_k[0]), rwkv_lnx_g=f(rwkv_lnx_g[0]),
             rwkv_lnx_b=f(rwkv_lnx_b[0]))
    w_in0, w_out0 = f(even_w_in[0]), f(even_w_out[0])
    cm, sm = rope_np(S, 128)
    cr, sr = rope_np(S, 256)
    rope = np.ascontiguousarray(np.stack([np.concatenate([cm.T, cm.T]), np.concatenate([sm.T, sm.T]), cr.T, sr.T]).astype(np.float32))
    per_j = []
    for j in range(G):
        d = dict(wint=tile_w(w_in0[:, mix0_cols(j)], KC), woutt=tile_w(w_out0[mix0_rows(j), :], 12), rope=rope)
        d.update(moba_const_arrays(S))
        d.update(ret_const_arrays([2 * j, 2 * j + 1]))
        m1 = mix1a_inputs(None, np.zeros(D, np.float32), np.zeros(D, np.float32), np.zeros(D, np.float32), p, j, S)
        for k_ in ('x', 'ng', 'sc', 'shf'):
            m1.pop(k_)
        d.update(m1)
        mb = mix1b_inputs(None, None, None, p, j, D)
        d.update(lg=mb['lg'], lb=mb['lb'], woutt1=mb['woutt'])
        for l in range(depth):
            w1, w2 = f(w_ffn_in[l]), f(w_ffn_out[l])
            h0, h1 = j * HCt * 128, min((j + 1) * HCt * 128, FH)
            g_ = np.zeros((D, HCt * 128), np.float32)
            u_ = np.zeros((D, HCt * 128), np.float32)
            g_[:, :h1 - h0] = w1[:, h0:h1]
            u_[:, :h1 - h0] = w1[:, FH + h0:FH + h1]
            w1t = np.empty((2 * HCt, 128, KC * 128), np.float32)
            w1t[0::2] = tile_w(g_, KC)
            w1t[1::2] = tile_w(u_, KC)
            w2p = np.zeros((HCt * 128, D), np.float32)
            w2p[:h1 - h0] = w2[h0:h1]
            d['w1t_%d' % l] = w1t
            d['w2t_%d' % l] = tile_w(w2p, HCt)
        per_j.append(d)
    ims = []
    for i in range(NCORE):
        b, j = divmod(i, G)
        im = dict(per_j[j])
        g0 = j * per
        layer, col0 = divmod(g0, N6)
        selv = np.zeros((128, 2), np.float32)
        selv[:, b] = 1.0
        im.update(x_full=x_fm[b], x_own=np.ascontiguousarray(x_fm[b][:, :, j * TS:(j + 1) * TS]), sel=selv, ngall=ngall, ct=ct,
                  wt_ada=tile_w(w_ada[layer][:, col0:col0 + per], KC), bt_ada=chunked(b_ada[layer][col0:col0 + per], nch))
        ims.append(im)
    res = _run(nc, ims)
    out = np.empty((B, S, D), np.float32)
    for i in range(NCORE):
        b, j = divmod(i, G)
        out[b, j * TS:(j + 1) * TS, :] = res[i]["out"].reshape(D, TS).T
    return out


def kernel(**inputs):
    return kernel_fused(**inputs)
```

```python
from contextlib import ExitStack
import numpy as np
import concourse.bass as bass
import concourse.mybir as mybir
from concourse.bass_utils import run_bass_kernel_spmd

F32 = mybir.dt.float32
BF16 = mybir.dt.bfloat16
AF = mybir.ActivationFunctionType
ALU = mybir.AluOpType
AX = mybir.AxisListType

ENG_ATTR = {'pe': 'tensor', 'act': 'scalar', 'dve': 'vector', 'pool': 'gpsimd', 'sp': 'sync'}


class Prog:
    def __init__(self, nc, same_engine_sync=False):
        self.nc = nc
        self.es = ExitStack()
        self.ops = {e: [] for e in ENG_ATTR}
        self.sem = {}
        self.cnt = {}
        self.waited = {e: {} for e in ENG_ATTR}
        self.last_w = {}
        self.readers = {}
        self.same = same_engine_sync
        for e in ENG_ATTR:
            self.sem[e] = self.es.enter_context(nc.semaphore('s_' + e))
            self.cnt[e] = 0
        self.n_ops = 0
        self.pending = {e: [] for e in ENG_ATTR}
        self.phase_es = None

    def sb(self, name, shape, dt):
        es = self.phase_es if self.phase_es is not None else self.es
        self.n_sb = getattr(self, 'n_sb', 0) + 1
        return es.enter_context(self.nc.sbuf_tensor('%s_u%d' % (name, self.n_sb), shape, dt))

    def barrier(self):
        cur = [(s, c) for s, c in self.cnt.items() if c > 0]
        for e in ENG_ATTR:
            self.pending[e] = list(cur)

    def phase_begin(self):
        self.phase_es = ExitStack()

    def phase_end(self):
        self.barrier()
        self.phase_es.close()
        self.phase_es = None
        for lane in self.__dict__.get('phase_lanes', []):
            k = self.lane_map.pop(lane, None)
            if k is not None:
                self.lane_free.append(k)
        self.phase_lanes = []

    def ps(self, name, shape, dt=F32):
        return self.es.enter_context(self.nc.psum_tensor(name, shape, dt))

    def _lane(self, lane):
        m = self.__dict__.setdefault('lane_map', {})
        if lane in m:
            return m[lane]
        free = self.__dict__.setdefault('lane_free', [])
        if self.phase_es is not None and free:
            k = free.pop()
        else:
            k = 'L_%d' % len([x for x in self.sem if x.startswith('L_')])
            self.sem[k] = self.__dict__.setdefault('sem_es', ExitStack()).enter_context(self.nc.semaphore(k))
            self.cnt[k] = 0
        m[lane] = k
        if self.phase_es is not None:
            self.__dict__.setdefault('phase_lanes', []).append(lane)
        return k

    def op(self, eng, fn, reads=(), writes=(), lane=None, force=False, lane_inc=16):
        deps = []
        for r in reads:
            if r in self.last_w:
                s_, v_, e_ = self.last_w[r]
                deps.append((s_, v_, e_ if eng == 'pe' else None))
        for w in writes:
            if w in self.last_w:
                deps.append(self.last_w[w])
            deps.extend(self.readers.get(w, {}).values())
        waits = []
        wd = self.waited[eng]
        for (s, v) in self.pending[eng]:
            if wd.get(s, 0) < v:
                wd[s] = v
                waits.append((s, v))
        self.pending[eng] = []
        for (s, v, e2) in deps:
            if e2 == eng and lane is None and not self.same and not force and not s.startswith('L_'):
                continue
            if wd.get(s, 0) >= v:
                continue
            wd[s] = v
            waits.append((s, v))
        if lane is not None:
            s = self._lane(lane)
            self.cnt[s] += lane_inc
            tok = (s, self.cnt[s], eng)
            inc = (s, lane_inc)
        else:
            self.cnt[eng] += 1
            tok = (eng, self.cnt[eng], eng)
            inc = (eng, 1)
        self.ops[eng].append((waits, fn, inc))
        for w in writes:
            self.last_w[w] = tok
            self.readers[w] = {}
        for r in reads:
            self.readers.setdefault(r, {})[tok[0]] = tok
        self.n_ops += 1
        return tok

    def finish(self):
        nc = self.nc
        final = [(s, c) for s, c in self.cnt.items() if c > 0]
        with nc.Block() as block:
            def mk(eng):
                def body(e):
                    for waits, fn, inc in self.ops[eng]:
                        for s, v in waits:
                            e.wait_ge(self.sem[s], v)
                        ins = fn(e)
                        ins.then_inc(self.sem[inc[0]], inc[1])
                    if eng == 'sp':
                        for s, c in final:
                            e.wait_ge(self.sem[s], c)
                return body
            for eng, attr in ENG_ATTR.items():
                getattr(block, attr)(mk(eng))
        self.es.close()
        if 'sem_es' in self.__dict__:
            self.sem_es.close()


def chunked(v, nchunk):
    return np.ascontiguousarray(np.asarray(v, np.float32).reshape(nchunk, 128).T)


class Common:
    def __init__(self, P, D, TT):
        self.P = P
        self.KC = D // 128
        self.TT = TT
        self.D = D
        nc = P.nc
        self.ones_f = P.sb('ones_f', [128, 128], F32)
        self.ones_b = P.sb('ones_b', [128, 128], BF16)
        P.op('dve', lambda e: e.memset(self.ones_f[:], 1.0), writes=['ones_f'])
        P.op('dve', lambda e: e.memset(self.ones_b[:], 1.0), writes=['ones_b'])
        self.psum_all = P.ps('psum_all', [128, 4096], F32)
        self.psum = [self.psum_all[:, i * 512:(i + 1) * 512] for i in range(8)]
        self.ps_i = 0

    def bank(self):
        i = self.ps_i
        self.ps_i = (self.ps_i + 1) % 8
        return i


def emit_norm_mod(P, C, x_dram, tok0, h, hkey, gs, sh, xs, eps, sq, rstd, tagp):
    KC, TT = C.KC, C.TT
    nh = TT // 512
    banks = [C.bank() for _ in range(nh)]
    xget = x_dram if callable(x_dram) else (lambda k, t0, n: x_dram[k, :, t0:t0 + n])
    for k in range(KC):
        xt = xs[k % 2]
        xk = 'xs%d' % (k % 2)
        P.op('sp', lambda e, xt=xt, k=k: e.dma_start(out=xt[:], in_=xget(k, tok0, TT)),
             writes=[xk], lane=xk)
        P.op('act', lambda e, xt=xt: e.activation(out=sq[:], in_=xt[:], func=AF.Square),
             reads=[xk], writes=['sq'])
        for hh in range(nh):
            P.op('pe', lambda e, hh=hh, k=k: e.matmul(C.psum[banks[hh]][:], C.ones_f[:], sq[:, hh * 512:(hh + 1) * 512],
                                                     start=(k == 0), stop=(k == KC - 1)),
                 reads=['sq', 'ones_f'], writes=['ps%d' % banks[hh]])
    for hh in range(nh):
        sl = slice(hh * 512, (hh + 1) * 512)
        P.op('act', lambda e, hh=hh, sl=sl: e.activation(out=rstd[:, sl], in_=C.psum[banks[hh]][:], func=AF.Sqrt,
                                                        scale=1.0 / C.D, bias=eps[:, 0:1]),
             reads=['ps%d' % banks[hh], tagp + 'eps'], writes=['rstd'])
    P.op('dve', lambda e: e.reciprocal(out=rstd[:], in_=rstd[:]), reads=['rstd'], writes=['rstd'])
    for k in range(KC):
        xt = xs[k % 2]
        xk = 'xs%d' % (k % 2)
        P.op('sp', lambda e, xt=xt, k=k: e.dma_start(out=xt[:], in_=xget(k, tok0, TT)),
             writes=[xk], lane=xk)
        P.op('dve', lambda e, xt=xt: e.tensor_tensor(out=xt[:], in0=xt[:], in1=rstd[:], op=ALU.mult),
             reads=[xk, 'rstd'], writes=[xk])
        P.op('pool', lambda e, xt=xt, k=k: e.tensor_scalar(out=h[:, k, :], in0=xt[:], scalar1=gs[:, k:k + 1],
                                                          scalar2=sh[:, k:k + 1], op0=ALU.mult, op1=ALU.add),
             reads=[xk, tagp + 'gs', tagp + 'sh'], writes=[hkey])


class WStream:
    def __init__(self, P, name, nslot, width):
        self.P = P
        self.name = name
        self.nslot = nslot
        self.slots = [P.sb('%s_w%d' % (name, i), [128, width], BF16) for i in range(nslot)]
        self.reqs = []
        self.issued = 0

    def plan(self, reqs):
        self.reqs = self.reqs + list(reqs)

    def _issue(self, i):
        ap, ncols = self.reqs[i]
        sl = self.slots[i % self.nslot]
        key = '%s_w%d' % (self.name, i % self.nslot)
        self.P.op('pool', lambda e: e.dma_start(out=sl[:, 0:ncols], in_=ap), writes=[key], lane=key)

    def get(self, i, oldest=None):
        if oldest is None:
            oldest = i
        while self.issued < min(len(self.reqs), oldest + self.nslot):
            self._issue(self.issued)
            self.issued += 1
        return self.slots[i % self.nslot], '%s_w%d' % (self.name, i % self.nslot)


def load_vec(P, name, dram, ncol, eng='sp'):
    t = P.sb(name, [128, ncol], F32)
    P.op(eng, lambda e: e.dma_start(out=t[:], in_=dram), writes=[name], lane=name)
    return t


def build_ffn(D, S, HC, TT=1024):
    nc = bass.Bass("TRN2", target_bir_lowering=False)
    KC = D // 128
    x = nc.dram_tensor("x", [KC, 128, S], F32, kind="ExternalInput").ap()
    ng = nc.dram_tensor("ng", [128, KC], F32, kind="ExternalInput").ap()
    sc = nc.dram_tensor("sc", [128, KC], F32, kind="ExternalInput").ap()
    shf = nc.dram_tensor("shf", [128, KC], F32, kind="ExternalInput").ap()
    w1t = nc.dram_tensor("w1t", [2 * HC, 128, KC * 128], F32, kind="ExternalInput").ap()
    w2t = nc.dram_tensor("w2t", [KC, 128, HC * 128], F32, kind="ExternalInput").ap()
    y = nc.dram_tensor("y", [KC, 128, S], F32, kind="ExternalOutput").ap()
    P = Prog(nc)
    C = Common(P, D, TT)
    dbg = None
    if DEBUG:
        dbg = (nc.dram_tensor("dbg_h", [KC, 128, S], F32, kind="ExternalOutput").ap(),
               nc.dram_tensor("dbg_a", [HC, 128, S], F32, kind="ExternalOutput").ap())
    emit_ffn(P, C, x, ng, sc, shf, w1t, w2t, y, S, HC, dbg=dbg)
    P.finish()
    return nc


def emit_gs(P, ngt, sct, KC, name):
    gs = P.sb(name, [128, KC], F32)
    P.op('dve', lambda e: e.tensor_scalar(out=gs[:], in0=sct[:], scalar1=1.0, scalar2=None, op0=ALU.add),
         reads=[name + '_sc'], writes=[name])
    P.op('dve', lambda e: e.tensor_tensor(out=gs[:], in0=gs[:], in1=ngt[:], op=ALU.mult),
         reads=[name + '_ng'], writes=[name])
    return gs


DEBUG = False


def emit_ffn(P, C, x, ng, sc, shf, w1t, w2t, y, S, HC, pfx='f', dbg=None):
    KC, TT = C.KC, C.TT
    nh = TT // 512
    ngt = load_vec(P, pfx + 'gs_ng', ng, KC)
    sct = load_vec(P, pfx + 'gs_sc', sc, KC)
    sht = load_vec(P, pfx + 'sh', shf, KC)
    gs = emit_gs(P, ngt, sct, KC, pfx + 'gs')
    eps = P.sb(pfx + 'eps', [128, 1], F32)
    P.op('dve', lambda e: e.memset(eps[:], 1e-6), writes=[pfx + 'eps'])
    h = P.sb(pfx + 'h', [128, KC, TT], BF16)
    actb = P.sb(pfx + 'actb', [128, HC, TT], BF16)
    xs = [P.sb(pfx + 'xs%d' % i, [128, TT], F32) for i in range(2)]
    sq = P.sb(pfx + 'sq', [128, TT], F32)
    rstd = P.sb(pfx + 'rstd', [128, TT], F32)
    sg = [P.sb(pfx + 'sg%d' % i, [128, 512], F32) for i in range(2)]
    ost = [P.sb(pfx + 'ost%d' % i, [128, 512], F32) for i in range(4)]
    ws = WStream(P, pfx + 'ws', 4, max(KC, HC) * 128)
    ntile = S // TT
    reqs = []
    for t in range(ntile):
        for i in range(2 * HC):
            reqs.append((w1t[i], KC * 128))
        for m in range(KC):
            reqs.append((w2t[m], HC * 128))
    ws.plan(reqs)
    wi = 0
    sgi = 0
    oi = 0
    for t in range(ntile):
        tok0 = t * TT
        emit_norm_mod(P, C, x, tok0, h, pfx + 'h', gs, sht, xs, eps, sq, rstd, pfx)
        for hc in range(HC):
            wg, wgk = ws.get(wi)
            wu, wuk = ws.get(wi + 1, oldest=wi)
            wi += 2
            for hh in range(nh):
                sl = slice(hh * 512, (hh + 1) * 512)
                bg = C.bank()
                bu = C.bank()
                for k in range(KC):
                    P.op('pe', lambda e, k=k, sl=sl, bg=bg, wg=wg: e.matmul(
                        C.psum[bg][:], wg[:, k * 128:(k + 1) * 128], h[:, k, sl], start=(k == 0), stop=(k == KC - 1)),
                        reads=[wgk, pfx + 'h'], writes=['ps%d' % bg])
                for k in range(KC):
                    P.op('pe', lambda e, k=k, sl=sl, bu=bu, wu=wu: e.matmul(
                        C.psum[bu][:], wu[:, k * 128:(k + 1) * 128], h[:, k, sl], start=(k == 0), stop=(k == KC - 1)),
                        reads=[wuk, pfx + 'h'], writes=['ps%d' % bu])
                sgt = sg[sgi % 2]
                sgk = pfx + 'sg%d' % (sgi % 2)
                sgi += 1
                P.op('act', lambda e, sgt=sgt, bg=bg: e.activation(out=sgt[:], in_=C.psum[bg][:], func=AF.Silu),
                     reads=['ps%d' % bg], writes=[sgk])
                P.op('dve', lambda e, sgt=sgt, bu=bu, hc=hc, sl=sl: e.tensor_tensor(
                    out=actb[:, hc, sl], in0=sgt[:], in1=C.psum[bu][:], op=ALU.mult),
                    reads=[sgk, 'ps%d' % bu], writes=[pfx + 'actb'])
        if dbg is not None:
            for k in range(KC):
                P.op('pool', lambda e, k=k, tok0=tok0: e.dma_start(out=dbg[0][k, :, tok0:tok0 + TT], in_=h[:, k, :]),
                     reads=[pfx + 'h'], lane='dbgh')
            for hc in range(HC):
                P.op('pool', lambda e, hc=hc, tok0=tok0: e.dma_start(out=dbg[1][hc, :, tok0:tok0 + TT], in_=actb[:, hc, :]),
                     reads=[pfx + 'actb'], lane='dbga')
        for m in range(KC):
            w2, w2k = ws.get(wi)
            wi += 1
            for hh in range(nh):
                sl = slice(hh * 512, (hh + 1) * 512)
                bo = C.bank()
                for hc in range(HC):
                    P.op('pe', lambda e, hc=hc, sl=sl, bo=bo, w2=w2: e.matmul(
                        C.psum[bo][:], w2[:, hc * 128:(hc + 1) * 128], actb[:, hc, sl], start=(hc == 0), stop=(hc == HC - 1)),
                        reads=[w2k, pfx + 'actb'], writes=['ps%d' % bo])
                o = ost[oi % 4]
                ok = pfx + 'ost%d' % (oi % 4)
                oi += 1
                P.op('act', lambda e, o=o, bo=bo: e.copy(out=o[:], in_=C.psum[bo][:]),
                     reads=['ps%d' % bo], writes=[ok])
                yput = y if callable(y) else (lambda m_, t0_, n_: y[m_, :, t0_:t0_ + n_])
                P.op('sp', lambda e, o=o, m=m, hh=hh, tok0=tok0, yput=yput: e.dma_start(
                    out=yput(m, tok0 + hh * 512, 512), in_=o[:]),
                    reads=[ok], lane=ok + '_st')


def build_reduce(D, TS, G):
    nc = bass.Bass("TRN2", target_bir_lowering=False)
    KC = D // 128
    part = nc.dram_tensor("part", [G, KC, 128, TS], F32, kind="ExternalInput").ap()
    x = nc.dram_tensor("x", [KC, 128, TS], F32, kind="ExternalInput").ap()
    ng = nc.dram_tensor("ng", [128, KC], F32, kind="ExternalInput").ap()
    gate = nc.dram_tensor("gate", [128, KC], F32, kind="ExternalInput").ap()
    xo = nc.dram_tensor("xo", [KC, 128, TS], F32, kind="ExternalOutput").ap()
    P = Prog(nc)
    C = Common(P, D, 512)
    emit_reduce(P, C, part, x, ng, gate, xo, TS, G)
    P.finish()
    return nc


def emit_reduce(P, C, part, x, ng, gate, xo, TS, G, pfx='r'):
    KC = C.KC
    ngt = load_vec(P, pfx + 'ng', ng, KC)
    gt = load_vec(P, pfx + 'gate', gate, KC)
    gg = P.sb(pfx + 'gg', [128, KC], F32)
    P.op('dve', lambda e: e.tensor_tensor(out=gg[:], in0=ngt[:], in1=gt[:], op=ALU.mult),
         reads=[pfx + 'ng', pfx + 'gate'], writes=[pfx + 'gg'])
    eps = P.sb(pfx + 'eps', [128, 1], F32)
    P.op('dve', lambda e: e.memset(eps[:], 1e-6), writes=[pfx + 'eps'])
    osum = P.sb(pfx + 'osum', [128, KC, 512], F32)
    pst = [P.sb(pfx + 'pst%d' % i, [128, 512], F32) for i in range(4)]
    xst = [P.sb(pfx + 'xst%d' % i, [128, 512], F32) for i in range(2)]
    sq = P.sb(pfx + 'sq', [128, 512], F32)
    rstd = P.sb(pfx + 'rstd', [128, 512], F32)
    pi = 0
    for hh in range(TS // 512):
        sl = slice(hh * 512, (hh + 1) * 512)
        bk = C.bank()
        for k in range(KC):
            for g in range(G):
                if g == 0:
                    P.op('sp', lambda e, k=k, g=g, sl=sl: e.dma_start(out=osum[:, k, :], in_=part[g, k, :, sl]),
                         writes=[pfx + 'osum%d' % k], lane=pfx + 'osum%d' % k)
                else:
                    st = pst[pi % 4]
                    sk = pfx + 'pst%d' % (pi % 4)
                    pi += 1
                    P.op('sp', lambda e, k=k, g=g, st=st, sl=sl: e.dma_start(out=st[:], in_=part[g, k, :, sl]),
                         writes=[sk], lane=sk)
                    P.op('dve', lambda e, k=k, st=st: e.tensor_tensor(out=osum[:, k, :], in0=osum[:, k, :], in1=st[:], op=ALU.add),
                         reads=[sk], writes=[pfx + 'osum%d' % k])
            P.op('act', lambda e, k=k: e.activation(out=sq[:], in_=osum[:, k, :], func=AF.Square),
                 reads=[pfx + 'osum%d' % k], writes=[pfx + 'sq'])
            P.op('pe', lambda e, k=k, bk=bk: e.matmul(C.psum[bk][:], C.ones_f[:], sq[:], start=(k == 0), stop=(k == KC - 1)),
                 reads=[pfx + 'sq', 'ones_f'], writes=['ps%d' % bk])
        P.op('act', lambda e, bk=bk: e.activation(out=rstd[:], in_=C.psum[bk][:], func=AF.Sqrt, scale=1.0 / C.D, bias=eps[:, 0:1]),
             reads=['ps%d' % bk, pfx + 'eps'], writes=[pfx + 'rstd'])
        P.op('dve', lambda e: e.reciprocal(out=rstd[:], in_=rstd[:]), reads=[pfx + 'rstd'], writes=[pfx + 'rstd'])
        for k in range(KC):
            xt = xst[k % 2]
            xk = pfx + 'xst%d' % (k % 2)
            P.op('sp', lambda e, k=k, xt=xt, sl=sl: e.dma_start(out=xt[:], in_=x[k, :, sl]), writes=[xk], lane=xk)
            P.op('pool', lambda e, k=k: e.tensor_tensor(out=osum[:, k, :], in0=osum[:, k, :], in1=rstd[:], op=ALU.mult),
                 reads=[pfx + 'rstd'], writes=[pfx + 'osum%d' % k])
            P.op('dve', lambda e, k=k, xt=xt: e.scalar_tensor_tensor(out=xt[:], in0=osum[:, k, :], scalar=gg[:, k:k + 1],
                                                                    in1=xt[:], op0=ALU.mult, op1=ALU.add),
                 reads=[pfx + 'osum%d' % k, xk, pfx + 'gg'], writes=[xk], force=(k == 0))
            P.op('sp', lambda e, k=k, xt=xt, sl=sl: e.dma_start(out=xo[k, :, sl], in_=xt[:]), reads=[xk], lane=xk + '_st')


def emit_inproj(P, C, x, ng, sc, shf, wint, S, groups, tile_begin=None, pfx='i'):
    KC, TT = C.KC, C.TT
    nh = TT // 512
    ngt = load_vec(P, pfx + 'gs_ng', ng, KC)
    sct = load_vec(P, pfx + 'gs_sc', sc, KC)
    sht = load_vec(P, pfx + 'sh', shf, KC)
    gs = emit_gs(P, ngt, sct, KC, pfx + 'gs')
    eps = P.sb(pfx + 'eps', [128, 1], F32)
    P.op('dve', lambda e: e.memset(eps[:], 1e-6), writes=[pfx + 'eps'])
    h = P.sb(pfx + 'h', [128, KC, TT], BF16)
    xs = [P.sb(pfx + 'xs%d' % i, [128, TT], F32) for i in range(2)]
    sq = P.sb(pfx + 'sq', [128, TT], F32)
    rstd = P.sb(pfx + 'rstd', [128, TT], F32)
    ws = WStream(P, pfx + 'ws', 4, KC * 128)
    ntile = S // TT
    nchunk = sum(g[0] for g in groups)
    reqs = []
    for t in range(ntile):
        for i in range(nchunk):
            reqs.append((wint[i], KC * 128))
    ws.plan(reqs)
    wi = 0
    for t in range(ntile):
        tok0 = t * TT
        emit_norm_mod(P, C, x, tok0, h, pfx + 'h', gs, sht, xs, eps, sq, rstd, pfx)
        if tile_begin is not None:
            tile_begin(tok0)
        for (n, epi) in groups:
            tl = [ws.get(wi + i, oldest=wi) for i in range(n)]
            wi += n
            for hh in range(nh):
                sl = slice(hh * 512, (hh + 1) * 512)
                banks = []
                for (wt, wk) in tl:
                    b = C.bank()
                    banks.append(b)
                    for k in range(KC):
                        P.op('pe', lambda e, k=k, sl=sl, b=b, wt=wt: e.matmul(
                            C.psum[b][:], wt[:, k * 128:(k + 1) * 128], h[:, k, sl], start=(k == 0), stop=(k == KC - 1)),
                            reads=[wk, pfx + 'h'], writes=['ps%d' % b])
                epi(banks, tok0 + hh * 512, hh)


class Stager:
    def __init__(self, P, name, n, dt, width=512):
        self.P = P
        self.name = name
        self.tiles = [P.sb('%s%d' % (name, i), [128, width], dt) for i in range(n)]
        self.i = 0

    def get(self):
        j = self.i % len(self.tiles)
        self.i += 1
        return self.tiles[j], '%s%d' % (self.name, j)


def st_dma(P, dst_ap, src_ap, key, eng='sp'):
    P.op(eng, lambda e: e.dma_start(out=dst_ap, in_=src_ap), reads=[key], lane=key + '_st')


def make_rope_epi(P, C, stg, cos, sin, ckey, dstA, dstB):
    def epi(banks, tok0, hh):
        bA, bB = banks
        sl = slice(hh * 512, (hh + 1) * 512)
        a, ak = stg.get()
        b, bk = stg.get()
        c, ck = stg.get()
        d, dk = stg.get()
        pa, pb = C.psum[bA], C.psum[bB]
        P.op('dve', lambda e: e.tensor_tensor(out=a[:], in0=pa[:], in1=cos[:, sl], op=ALU.mult), reads=['ps%d' % bA] + ckey, writes=[ak])
        P.op('dve', lambda e: e.tensor_tensor(out=b[:], in0=pb[:], in1=sin[:, sl], op=ALU.mult), reads=['ps%d' % bB] + ckey, writes=[bk])
        P.op('dve', lambda e: e.tensor_tensor(out=c[:], in0=pb[:], in1=cos[:, sl], op=ALU.mult), reads=['ps%d' % bB] + ckey, writes=[ck])
        P.op('dve', lambda e: e.tensor_tensor(out=d[:], in0=pa[:], in1=sin[:, sl], op=ALU.mult), reads=['ps%d' % bA] + ckey, writes=[dk])
        P.op('pool', lambda e: e.tensor_tensor(out=a[:], in0=a[:], in1=b[:], op=ALU.subtract), reads=[ak, bk], writes=[ak])
        P.op('pool', lambda e: e.tensor_tensor(out=c[:], in0=c[:], in1=d[:], op=ALU.add), reads=[ck, dk], writes=[ck])
        dstA(a, ak, tok0)
        dstB(c, ck, tok0)
    return epi


def make_act_epi(P, C, stg, func, dst):
    def epi(banks, tok0, hh):
        (b,) = banks
        o, ok = stg.get()
        P.op('act', lambda e: e.activation(out=o[:], in_=C.psum[b][:], func=func), reads=['ps%d' % b], writes=[ok])
        dst(o, ok, tok0)
    return epi


NEG = -30000.0
NOMASK = False


def emit_moba_head(P, C, K, qsrc, ksrc, vsrc, odst, S, pfx='m'):
    NB = S // 256
    QT = S // 128
    qf = P.sb(pfx + 'qf', [128, S], F32)
    kf = P.sb(pfx + 'kf', [128, S], F32)
    vf = P.sb(pfx + 'vf', [128, S], F32)
    qb = P.sb(pfx + 'qb', [128, S], BF16)
    kb = P.sb(pfx + 'kb', [128, S], BF16)
    vtm = P.sb(pfx + 'vtm', [128, QT, 128], BF16)
    maskT = P.sb(pfx + 'maskT', [16, S], BF16)
    kmean = P.sb(pfx + 'kmean', [128, 16], F32)
    gm = P.sb(pfx + 'gm', [128, 16], F32)
    m8 = P.sb(pfx + 'm8', [128, 8], F32)
    mv = P.sb(pfx + 'mv', [128, 16], F32)
    rden = P.sb(pfx + 'rden', [128, 256], F32)
    pst = Stager(P, pfx + 'pT', 3, BF16, 256)
    ost = Stager(P, pfx + 'o', 2, F32, 256)
    P.op('sp', lambda e: e.dma_start(out=qf[:], in_=qsrc), writes=[pfx + 'qf'], lane=pfx + 'qf')
    P.op('sp', lambda e: e.dma_start(out=kf[:], in_=ksrc), writes=[pfx + 'kf'], lane=pfx + 'kf')
    P.op('sp', lambda e: e.dma_start(out=vf[:], in_=vsrc), writes=[pfx + 'vf'], lane=pfx + 'vf')
    P.op('act', lambda e: e.copy(out=qb[:], in_=qf[:]), reads=[pfx + 'qf'], writes=[pfx + 'qb'])
    P.op('pool', lambda e: e.tensor_copy(out=kb[:], in_=kf[:]), reads=[pfx + 'kf'], writes=[pfx + 'kb'])
    P.op('dve', lambda e: e.memset(kmean[:], 0.0), writes=[pfx + 'kmean'])
    P.op('dve', lambda e: e.tensor_reduce(out=kmean[:, 0:NB], in_=kf[:].rearrange("p (n l) -> p n l", l=256), axis=AX.X, op=ALU.add),
         reads=[pfx + 'kf'], writes=[pfx + 'kmean'])
    for t in range(QT):
        P.op('pe', lambda e, t=t: e.transpose(out=C.psum[7][:, 0:128], in_=vf[:, t * 128:(t + 1) * 128], identity=K['ident_f'][:]),
             reads=[pfx + 'vf', 'ident_f'], writes=['ps7'])
        P.op('act', lambda e, t=t: e.copy(out=vtm[:, t, :], in_=C.psum[7][:, 0:128]), reads=['ps7'], writes=[pfx + 'vtm'])
    for t in range(QT):
        ob = t // 2
        P.op('pe', lambda e, t=t: e.matmul(C.psum[7][:, 256:272], qf[:, t * 128:(t + 1) * 128], kmean[:, 0:16], start=True, stop=True),
             reads=[pfx + 'qf', pfx + 'kmean'], writes=['ps7'])
        P.op('dve', lambda e, ob=ob: e.tensor_tensor(out=gm[:], in0=C.psum[7][:, 256:272], in1=K['addmask'][:, ob, :], op=ALU.add),
             reads=['ps7', 'addmask'], writes=[pfx + 'gm'])
        P.op('dve', lambda e: e.max(out=m8[:], in_=gm[:]), reads=[pfx + 'gm'], writes=[pfx + 'm8'], force=True)
        P.op('dve', lambda e: e.tensor_scalar(out=mv[:], in0=gm[:], scalar1=m8[:, 2:3], scalar2=None, op0=ALU.is_ge),
             reads=[pfx + 'gm', pfx + 'm8'], writes=[pfx + 'mv'], force=True)
        P.op('dve', lambda e: e.tensor_scalar(out=mv[:], in0=mv[:], scalar1=-NEG, scalar2=NEG, op0=ALU.mult, op1=ALU.add),
             reads=[pfx + 'mv'], writes=[pfx + 'mv'], force=True)
        P.op('pe', lambda e: e.transpose(out=C.psum[6][0:16, 0:128], in_=mv[:], identity=K['ident_f'][:]),
             reads=[pfx + 'mv', 'ident_f'], writes=['ps6'])
        P.op('act', lambda e, t=t: e.copy(out=maskT[0:16, t * 128:(t + 1) * 128], in_=C.psum[6][0:16, 0:128]),
             reads=['ps6'], writes=[pfx + 'maskT'])
    sbi = 0
    scale = 128.0 ** -0.5
    for Q in range(NB):
        acc = 3 + (Q % 2)
        nk = 2 * (Q + 1)
        qs = slice(Q * 256, (Q + 1) * 256)
        for kt in range(nk):
            blk = kt // 2
            sb_ = sbi % 3
            sbi += 1
            P.op('pe', lambda e, kt=kt, sb_=sb_, qs=qs: e.matmul(C.psum[sb_][:, 0:256], kb[:, kt * 128:(kt + 1) * 128], qb[:, qs], start=True, stop=(NOMASK and blk < Q)),
                 reads=[pfx + 'kb', pfx + 'qb'], writes=['ps%d' % sb_])
            if blk < Q and not NOMASK:
                P.op('pe', lambda e, blk=blk, sb_=sb_, qs=qs: e.matmul(C.psum[sb_][:, 0:256], K['Eall'][0:16, blk * 128:(blk + 1) * 128], maskT[0:16, qs], start=False, stop=True),
                     reads=['Eall', pfx + 'maskT'], writes=['ps%d' % sb_])
            elif blk == Q:
                P.op('pe', lambda e, kt=kt, sb_=sb_: e.matmul(C.psum[sb_][:, 0:256], K['ident_b'][:], K['caus'][:, kt % 2, :], start=False, stop=True),
                     reads=['ident_b', 'caus'], writes=['ps%d' % sb_])
            pT, pk = pst.get()
            P.op('act', lambda e, pT=pT, sb_=sb_: e.activation(out=pT[:], in_=C.psum[sb_][:, 0:256], func=AF.Exp, scale=scale),
                 reads=['ps%d' % sb_], writes=[pk])
            P.op('pe', lambda e, kt=kt, pT=pT, acc=acc, nk=nk: e.matmul(C.psum[acc][:, 0:256], vtm[:, kt, :], pT[:], start=(kt == 0), stop=(kt == nk - 1)),
                 reads=[pfx + 'vtm', pk], writes=['ps%d' % acc])
            P.op('pe', lambda e, kt=kt, pT=pT, acc=acc, nk=nk: e.matmul(C.psum[acc + 2][:, 0:256], C.ones_b[:], pT[:], start=(kt == 0), stop=(kt == nk - 1)),
                 reads=['ones_b', pk], writes=['ps%d' % (acc + 2)])
        P.op('dve', lambda e, acc=acc: e.reciprocal(out=rden[:], in_=C.psum[acc + 2][:, 0:256]), reads=['ps%d' % (acc + 2)], writes=[pfx + 'rden'])
        o, ok = ost.get()
        P.op('dve', lambda e, acc=acc, o=o: e.tensor_tensor(out=o[:], in0=C.psum[acc][:, 0:256], in1=rden[:], op=ALU.mult),
             reads=['ps%d' % acc, pfx + 'rden'], writes=[ok])
        st_dma(P, odst[:, qs], o[:], ok)


def load_consts(P, names_dram):
    K = {}
    for name, (ap, shape, dt) in names_dram.items():
        t = P.sb('k_%s_%d' % (name, P.n_ops), shape, dt)
        eng = 'pool' if dt == BF16 else 'sp'
        P.op(eng, lambda e, t=t, ap=ap: e.dma_start(out=t[:], in_=ap), writes=[name], lane=name)
        K[name] = t
    return K


def moba_const_arrays(S):
    NBP = 16
    ident = np.eye(128, dtype=np.float32)
    Eall = np.zeros((16, 16, 128), np.float32)
    for b in range(16):
        Eall[b, b, :] = 1.0
    Eall = Eall.reshape(16, 16 * 128)
    j = np.arange(128)[:, None]
    i = np.arange(256)[None, :]
    caus = np.stack([np.where(p * 128 + j <= i, 0.0, NEG) for p in range(2)], axis=1).astype(np.float32)
    addmask = np.zeros((128, NBP, 16), np.float32)
    for ob in range(NBP):
        addmask[:, ob, ob:] = -1e30
    return dict(ident=ident, Eall=Eall, caus=np.ascontiguousarray(caus), addmask=addmask)


def emit_ret_head(P, C, K, hr, qsrc, ksrc, vsrc, gsrc, odst, S, pfx='t'):
    BT = min(S, 1024)
    ncb = BT // 128
    qb = [P.sb(pfx + 'qb%d' % i, [128, BT], BF16) for i in range(2)]
    kb = [P.sb(pfx + 'kb%d' % i, [128, BT], BF16) for i in range(2)]
    qd = [P.sb(pfx + 'qd%d' % i, [128, BT], BF16) for i in range(2)]
    kf = [P.sb(pfx + 'kf%d' % i, [128, BT], F32) for i in range(2)]
    vf = [P.sb(pfx + 'vf%d' % i, [128, BT], F32) for i in range(4)]
    gf = [P.sb(pfx + 'gf%d' % i, [128, BT], F32) for i in range(4)]
    ktm = P.sb(pfx + 'ktm', [128, 256], BF16)
    vtm = P.sb(pfx + 'vtm', [128, 512], BF16)
    inn = P.sb(pfx + 'inn', [128, 128], BF16)
    o_sb = P.sb(pfx + 'o_sb', [128, 512], F32)
    osq = P.sb(pfx + 'osq', [128, 512], F32)
    rs = P.sb(pfx + 'rs', [128, 128], F32)
    eps = P.sb(pfx + 'eps', [128, 1], F32)
    st_f = [P.sb(pfx + 'stf%d' % i, [128, 512], F32) for i in range(2)]
    st_b = [P.sb(pfx + 'stb%d' % i, [128, 512], BF16) for i in range(2)]
    ost = Stager(P, pfx + 'om', 4, F32, 128)
    P.op('dve', lambda e: e.memset(eps[:], 1e-6), writes=[pfx + 'eps'])
    for dc in range(2):
        P.op('dve', lambda e, dc=dc: e.memset(st_f[dc][:], 0.0), writes=[pfx + 'stf%d' % dc])
    nblk = S // BT
    obi = 0
    for blk in range(nblk):
        bs = slice(blk * BT, (blk + 1) * BT)
        for dc in range(2):
            P.op('pool', lambda e, dc=dc, bs=bs: e.dma_start(out=qb[dc][:], in_=qsrc[dc][:, bs]), writes=[pfx + 'qb%d' % dc], lane=pfx + 'qb%d' % dc)
            P.op('pool', lambda e, dc=dc, bs=bs: e.dma_start(out=kb[dc][:], in_=ksrc[dc][:, bs]), writes=[pfx + 'kb%d' % dc], lane=pfx + 'kb%d' % dc)
            P.op('sp', lambda e, dc=dc, bs=bs: e.dma_start(out=kf[dc][:], in_=ksrc[dc][:, bs]), writes=[pfx + 'kf%d' % dc], lane=pfx + 'kf%d' % dc)
        for vc in range(4):
            P.op('sp', lambda e, vc=vc, bs=bs: e.dma_start(out=vf[vc][:], in_=vsrc[vc][:, bs]), writes=[pfx + 'vf%d' % vc], lane=pfx + 'vf%d' % vc)
            P.op('sp', lambda e, vc=vc, bs=bs: e.dma_start(out=gf[vc][:], in_=gsrc[vc][:, bs]), writes=[pfx + 'gf%d' % vc], lane=pfx + 'gf%d' % vc)
        for dc in range(2):
            for c in range(ncb):
                cs = slice(c * 128, (c + 1) * 128)
                P.op('pool', lambda e, dc=dc, cs=cs: e.tensor_tensor(out=qd[dc][:, cs], in0=qb[dc][:, cs], in1=K['qdec'][:, hr, :], op=ALU.mult),
                     reads=[pfx + 'qb%d' % dc, 'qdec'], writes=[pfx + 'qd%d' % dc])
        for c in range(ncb):
            cg = blk * ncb + c
            cs = slice(c * 128, (c + 1) * 128)
            gsl = slice(cg * 128, (cg + 1) * 128)
            for dc in range(2):
                P.op('pe', lambda e, dc=dc, cs=cs: e.transpose(out=C.psum[7][:, dc * 128:(dc + 1) * 128], in_=kf[dc][:, cs], identity=K['ident_f'][:]),
                     reads=[pfx + 'kf%d' % dc, 'ident_f'], writes=['ps7'])
            P.op('dve', lambda e: e.tensor_scalar(out=ktm[:], in0=C.psum[7][:, 0:256], scalar1=K['kdec'][:, hr:hr + 1], scalar2=None, op0=ALU.mult),
                 reads=['ps7', 'kdec'], writes=[pfx + 'ktm'])
            for vc in range(4):
                P.op('pe', lambda e, vc=vc, cs=cs: e.transpose(out=C.psum[6][:, vc * 128:(vc + 1) * 128], in_=vf[vc][:, cs], identity=K['ident_f'][:]),
                     reads=[pfx + 'vf%d' % vc, 'ident_f'], writes=['ps6'])
            P.op('act', lambda e: e.copy(out=vtm[:], in_=C.psum[6][:, 0:512]), reads=['ps6'], writes=[pfx + 'vtm'])
            for dc in range(2):
                P.op('pe', lambda e, dc=dc, cs=cs: e.matmul(C.psum[5][:, 0:128], kb[dc][:, cs], qb[dc][:, cs], start=(dc == 0), stop=(dc == 1)),
                     reads=[pfx + 'kb%d' % dc, pfx + 'qb%d' % dc], writes=['ps5'])
            P.op('dve', lambda e: e.tensor_tensor(out=inn[:], in0=C.psum[5][:, 0:128], in1=K['dmT'][:, hr, :], op=ALU.mult),
                 reads=['ps5', 'dmT'], writes=[pfx + 'inn'])
            ob = obi % 2
            obi += 1
            for vc in range(4):
                vs = slice(vc * 128, (vc + 1) * 128)
                P.op('pe', lambda e, vs=vs, ob=ob, cg=cg: e.matmul(C.psum[ob][:, vs], vtm[:, vs], inn[:], start=True, stop=(cg == 0)),
                     reads=[pfx + 'vtm', pfx + 'inn'], writes=['ps%d' % ob])
                if cg > 0:
                    for dc in range(2):
                        P.op('pe', lambda e, vs=vs, ob=ob, dc=dc, cs=cs: e.matmul(C.psum[ob][:, vs], st_b[dc][:, vs], qd[dc][:, cs], start=False, stop=(dc == 1)),
                             reads=[pfx + 'stb%d' % dc, pfx + 'qd%d' % dc], writes=['ps%d' % ob])
            P.op('act', lambda e, ob=ob: e.copy(out=o_sb[:], in_=C.psum[ob][:]), reads=['ps%d' % ob], writes=[pfx + 'o_sb'])
            P.op('act', lambda e, ob=ob: e.activation(out=osq[:], in_=C.psum[ob][:], func=AF.Square), reads=['ps%d' % ob], writes=[pfx + 'osq'])
            for vc in range(4):
                P.op('pe', lambda e, vc=vc: e.matmul(C.psum[2][:, 0:128], C.ones_f[:], osq[:, vc * 128:(vc + 1) * 128], start=(vc == 0), stop=(vc == 3)),
                     reads=['ones_f', pfx + 'osq'], writes=['ps2'])
            P.op('act', lambda e: e.activation(out=rs[:], in_=C.psum[2][:, 0:128], func=AF.Sqrt, scale=1.0 / 512, bias=eps[:, 0:1]),
                 reads=['ps2', pfx + 'eps'], writes=[pfx + 'rs'])
            P.op('dve', lambda e: e.reciprocal(out=rs[:], in_=rs[:]), reads=[pfx + 'rs'], writes=[pfx + 'rs'])
            for vc in range(4):
                vs = slice(vc * 128, (vc + 1) * 128)
                om, omk = ost.get()
                P.op('dve', lambda e, vs=vs, om=om: e.tensor_tensor(out=om[:], in0=o_sb[:, vs], in1=rs[:], op=ALU.mult),
                     reads=[pfx + 'o_sb', pfx + 'rs'], writes=[omk])
                P.op('pool', lambda e, vc=vc, om=om, cs=cs: e.tensor_tensor(out=om[:], in0=om[:], in1=gf[vc][:, cs], op=ALU.mult),
                     reads=[omk, pfx + 'gf%d' % vc], writes=[omk])
                st_dma(P, odst[vc][:, gsl], om[:], omk)
            for dc in range(2):
                P.op('pe', lambda e, dc=dc: e.matmul(C.psum[3 + dc][:], ktm[:, dc * 128:(dc + 1) * 128], vtm[:], start=True, stop=True),
                     reads=[pfx + 'ktm', pfx + 'vtm'], writes=['ps%d' % (3 + dc)])
                P.op('dve', lambda e, dc=dc: e.scalar_tensor_tensor(out=st_f[dc][:], in0=st_f[dc][:], scalar=K['gC'][:, hr:hr + 1], in1=C.psum[3 + dc][:],
                                                                   op0=ALU.mult, op1=ALU.add),
                     reads=['ps%d' % (3 + dc), 'gC'], writes=[pfx + 'stf%d' % dc])
                P.op('act', lambda e, dc=dc: e.copy(out=st_b[dc][:], in_=st_f[dc][:]), reads=[pfx + 'stf%d' % dc], writes=[pfx + 'stb%d' % dc])


def ret_const_arrays(heads):
    C = 128
    dmT = np.zeros((128, len(heads), 128), np.float32)
    qdec = np.zeros((128, len(heads), 128), np.float32)
    kdec = np.zeros((128, len(heads)), np.float32)
    gC = np.zeros((128, len(heads)), np.float32)
    idx = np.arange(C, dtype=np.float32)
    for n, hd in enumerate(heads):
        log_g = np.log1p(-np.exp2(np.float32(-5.0 - hd))).astype(np.float32)
        diff = idx[None, :] - idx[:, None]
        dmT[:, n, :] = np.where(diff >= 0, np.exp(np.maximum(diff, 0) * log_g), 0.0) * (256 ** -0.5)
        qdec[:, n, :] = np.exp((idx + 1.0) * log_g)[None, :]
        kdec[:, n] = np.exp((C - 1.0 - idx) * log_g) * (256 ** -0.5)
        gC[:, n] = np.exp(C * log_g)
    return dict(dmT=dmT, qdec=qdec, kdec=kdec, gC=gC)


def emit_outproj(P, C, merged, woutt, part, S, NKC, pfx='o'):
    KC, TT = C.KC, C.TT
    nh = TT // 512
    mt = P.sb(pfx + 'mt', [128, NKC, TT], BF16)
    ws = WStream(P, pfx + 'ws', 4, NKC * 128)
    ost = Stager(P, pfx + 'ost', 4, F32, 512)
    ntile = S // TT
    ws.plan([(woutt[m], NKC * 128) for t in range(ntile) for m in range(KC)])
    wi = 0
    for t in range(ntile):
        tok0 = t * TT
        for kc in range(NKC):
            P.op('pool', lambda e, kc=kc, tok0=tok0: e.dma_start(out=mt[:, kc, :], in_=(merged[kc][:, tok0:tok0 + TT] if isinstance(merged, list) else merged[kc, :, tok0:tok0 + TT])),
                 writes=[pfx + 'mt%d' % kc], lane=pfx + 'mt%d' % kc)
        for m in range(KC):
            w, wk = ws.get(wi)
            wi += 1
            for hh in range(nh):
                sl = slice(hh * 512, (hh + 1) * 512)
                b = C.bank()
                for kc in range(NKC):
                    P.op('pe', lambda e, kc=kc, sl=sl, b=b, w=w: e.matmul(C.psum[b][:], w[:, kc * 128:(kc + 1) * 128], mt[:, kc, sl],
                                                                         start=(kc == 0), stop=(kc == NKC - 1)),
                         reads=[wk, pfx + 'mt%d' % kc], writes=['ps%d' % b])
                o, ok = ost.get()
                P.op('act', lambda e, o=o, b=b: e.copy(out=o[:], in_=C.psum[b][:]), reads=['ps%d' % b], writes=[ok])
                pput = part if callable(part) else (lambda m_, t0_, n_: part[m_, :, t0_:t0_ + n_])
                st_dma(P, pput(m, tok0 + hh * 512, 512), o[:], ok)


def decl_mix0(nc, D, S, sfx=''):
    KC = D // 128
    di = lambda n, s: nc.dram_tensor(n + sfx, s, F32, kind="ExternalInput").ap()
    T = dict(wint=di("wint", [36, 128, KC * 128]), woutt=di("woutt", [KC, 128, 12 * 128]), rope=di("rope", [4, 128, S]))
    T['cd'] = dict(ident=di("ident", [128, 128]), Eall=di("Eall", [16, 2048]), caus=di("caus", [128, 2, 256]),
                   addmask=di("addmask", [128, 16, 16]), dmT=di("dmT", [128, 2, 128]), qdec=di("qdec", [128, 2, 128]),
                   kdec=di("kdec", [128, 2]), gC=di("gC", [128, 2]))
    return T


def build_mix0(D, S, TT=1024):
    nc = bass.Bass("TRN2", target_bir_lowering=False)
    KC = D // 128
    TT = min(TT, S)
    di = lambda n, s: nc.dram_tensor(n, s, F32, kind="ExternalInput").ap()
    x = di("x", [KC, 128, S])
    ng, sc, shf = di("ng", [128, KC]), di("sc", [128, KC]), di("shf", [128, KC])
    T = decl_mix0(nc, D, S)
    part = nc.dram_tensor("part", [KC, 128, S], F32, kind="ExternalOutput").ap()
    P = Prog(nc)
    C = Common(P, D, TT)
    stage_mix0(P, C, nc, T, x, ng, sc, shf, part, S)
    P.finish()
    return nc


def stage_mix0(P, C, nc, T, x, ng, sc, shf, part, S, sfx=''):
    wint, woutt, rope, cd = T['wint'], T['woutt'], T['rope'], T['cd']
    TT = C.TT
    scr = lambda n, s: nc.dram_tensor(n + sfx, s, F32, kind="Internal").ap()
    qm, km, vm = scr("qm", [4, 128, S]), scr("km", [4, 128, S]), scr("vm", [4, 128, S])
    qr, kr = scr("qr", [2, 2, 128, S]), scr("kr", [2, 2, 128, S])
    vr, gr = scr("vr", [2, 4, 128, S]), scr("gr", [2, 4, 128, S])
    merged = scr("merged", [12, 128, S]) if not DEBUG else nc.dram_tensor("merged", [12, 128, S], F32, kind="ExternalOutput").ap()
    P.phase_begin()
    tabs = [P.sb('rope%d' % i, [128, TT], F32) for i in range(4)]

    def tile_begin(tok0):
        for i in range(4):
            P.op('sp', lambda e, i=i, tok0=tok0: e.dma_start(out=tabs[i][:], in_=rope[i, :, tok0:tok0 + TT]), writes=['rope%d' % i], lane='rope%d' % i)
    stg = Stager(P, 'stg', 8, F32)
    stp = Stager(P, 'stp', 4, F32)

    def halves(T, r0, h0, h1):
        def f(t, k, tok0):
            st_dma(P, T[h0][r0:r0 + 64, tok0:tok0 + 512], t[0:64, :], k)
            P.op('sp', lambda e: e.dma_start(out=T[h1][r0:r0 + 64, tok0:tok0 + 512], in_=t[64:128, :]), reads=[k], lane=k + '_st2')
        return f

    def whole(dst):
        def f(t, k, tok0):
            st_dma(P, dst[:, tok0:tok0 + 512], t[:], k)
        return f
    groups = []
    for (dstT, pr) in ((qm, 0), (qm, 1), (km, 0), (km, 1)):
        groups.append((2, make_rope_epi(P, C, stg, tabs[0], tabs[1], ['rope0', 'rope1'], halves(dstT, 0, 2 * pr, 2 * pr + 1), halves(dstT, 64, 2 * pr, 2 * pr + 1))))
    for hd in range(4):
        groups.append((1, make_act_epi(P, C, stp, AF.Copy, whole(vm[hd]))))
    for hr in range(2):
        for dstT in (qr, kr):
            groups.append((2, make_rope_epi(P, C, stg, tabs[2], tabs[3], ['rope2', 'rope3'], whole(dstT[hr, 0]), whole(dstT[hr, 1]))))
    for hr in range(2):
        for vc in range(4):
            groups.append((1, make_act_epi(P, C, stp, AF.Copy, whole(vr[hr, vc]))))
    for hr in range(2):
        for vc in range(4):
            groups.append((1, make_act_epi(P, C, stp, AF.Silu, whole(gr[hr, vc]))))
    emit_inproj(P, C, x, ng, sc, shf, wint, S, groups, tile_begin=tile_begin)
    P.phase_end()
    P.phase_begin()
    K = load_consts(P, dict(ident_f=(cd['ident'], [128, 128], F32), ident_b=(cd['ident'], [128, 128], BF16),
                            Eall=(cd['Eall'], [16, 2048], BF16), caus=(cd['caus'], [128, 2, 256], BF16),
                            addmask=(cd['addmask'], [128, 16, 16], F32)))
    P.es, es_keep = P.phase_es, P.es
    for hd in range(4):
        P.phase_begin()
        emit_moba_head(P, C, K, qm[hd], km[hd], vm[hd], merged[hd], S, pfx='m')
        P.phase_end()
    P.phase_es, P.es = P.es, es_keep
    P.phase_end()
    P.phase_begin()
    K = load_consts(P, dict(ident_f=(cd['ident'], [128, 128], F32), dmT=(cd['dmT'], [128, 2, 128], F32),
                            qdec=(cd['qdec'], [128, 2, 128], BF16), kdec=(cd['kdec'], [128, 2], F32), gC=(cd['gC'], [128, 2], F32)))
    P.es, es_keep = P.phase_es, P.es
    for hr in range(2):
        P.phase_begin()
        emit_ret_head(P, C, K, hr, [qr[hr, 0], qr[hr, 1]], [kr[hr, 0], kr[hr, 1]], [vr[hr, i] for i in range(4)],
                      [gr[hr, i] for i in range(4)], [merged[4 + hr * 4 + i] for i in range(4)], S, pfx='t')
        P.phase_end()
    P.phase_es, P.es = P.es, es_keep
    P.phase_end()
    P.phase_begin()
    emit_outproj(P, C, merged, woutt, part, S, 12)
    P.phase_end()


def tile_w(w, KCk):
    K, N = w.shape
    return np.ascontiguousarray(w.reshape(KCk, 128, N // 128, 128).transpose(2, 1, 0, 3).reshape(N // 128, 128, KCk * 128))


def fm(a, KC):
    return np.ascontiguousarray(a.T.reshape(KC, 128, -1))


def rope_np(S, dim):
    inv = (1.0 / (10000.0 ** (np.arange(0, dim, 2, dtype=np.float32) / np.float32(dim)))).astype(np.float32)
    ang = np.arange(S, dtype=np.float32)[:, None] * inv[None, :]
    return np.cos(ang).astype(np.float32), np.sin(ang).astype(np.float32)


def mix0_cols(j):
    dm, dk, dv = 2048, 2048, 4096
    cols = []
    offq, offk, offv = 0, dm, 2 * dm
    for off in (offq, offk):
        for pr in range(2):
            h0, h1 = 4 * j + 2 * pr, 4 * j + 2 * pr + 1
            A = np.concatenate([off + h0 * 128 + np.arange(64), off + h1 * 128 + np.arange(64)])
            B = np.concatenate([off + h0 * 128 + 64 + np.arange(64), off + h1 * 128 + 64 + np.arange(64)])
            cols += [A, B]
    for hd in range(4):
        cols.append(offv + (4 * j + hd) * 128 + np.arange(128))
    oqr, okr, ovr, ogr = 3 * dm, 3 * dm + dk, 3 * dm + 2 * dk, 3 * dm + 2 * dk + dv
    for hr in range(2):
        H = 2 * j + hr
        for off in (oqr, okr):
            cols.append(off + H * 256 + np.arange(128))
            cols.append(off + H * 256 + 128 + np.arange(128))
    for off in (ovr, ogr):
        for hr in range(2):
            H = 2 * j + hr
            for vc in range(4):
                cols.append(off + H * 512 + vc * 128 + np.arange(128))
    return np.concatenate(cols)


def mix0_rows(j):
    rows = []
    for hd in range(4):
        rows.append((4 * j + hd) * 128 + np.arange(128))
    for hr in range(2):
        H = 2 * j + hr
        for vc in range(4):
            rows.append(2048 + H * 512 + vc * 128 + np.arange(128))
    return np.concatenate(rows)


def mix0_inputs(x_b, ng, sc, shf, w_in, w_out, j, S):
    D = x_b.shape[1]
    KC = D // 128
    cm, sm = rope_np(S, 128)
    cr, sr = rope_np(S, 256)
    rope = np.stack([np.concatenate([cm.T, cm.T]), np.concatenate([-sm.T, -sm.T]) * -1.0 if False else np.concatenate([sm.T, sm.T]), cr.T, sr.T]).astype(np.float32)
    im = dict(x=fm(x_b, KC), ng=chunked(ng, KC), sc=chunked(sc, KC), shf=chunked(shf, KC),
              wint=tile_w(w_in[:, mix0_cols(j)], KC), woutt=tile_w(w_out[mix0_rows(j), :], 12), rope=np.ascontiguousarray(rope))
    im.update(moba_const_arrays(S))
    im.update(ret_const_arrays([2 * j, 2 * j + 1]))
    return im


CW = 0.6065306597126334


def rwkv_const_arrays():
    m = np.zeros((64, 320), np.float32)
    s = np.arange(64)[:, None]
    t = np.arange(64)[None, :]
    m[:, 0:64] = (s < t)
    m[:, 64:128] = (t < s)
    m[:, 128:192] = (s < t)
    m[:, 192:256] = (s <= t)
    m[:, 256:320] = (s <= t)
    blockones = np.zeros((128, 128), np.float32)
    blockones[:64, :64] = 1.0
    blockones[64:, 64:] = 1.0
    return dict(amask=m, blockones=blockones, identrep=np.tile(np.eye(64, dtype=np.float32), (1, 8)))


def emit_rwkv(P, C, K, src, yout, S, BT=512):
    BT = min(BT, S)
    NCH = BT // 64
    f32t = lambda n, shp=None: P.sb(n, shp or [128, BT], F32)
    xin = {n: f32t('x_' + n, [128, BT + 1]) for n in ('xw', 'xa', 'xg0', 'xg1')}
    txw, xas, sxg0, sxg1 = f32t('txw'), f32t('xas'), f32t('sxg0'), f32t('sxg1')
    dtmp = f32t('dtmp')
    STh = [[P.sb('ST%d_%d' % (pp, h), [64, 64], F32) for h in range(2)] for pp in range(4)]
    for pp in range(4):
        for h in range(2):
            P.op('dve', lambda e, pp=pp, h=h: e.memset(STh[pp][h][:], 0.0), writes=['ST%d_%d' % (pp, h)])
    eps_ln = P.sb('eps_ln', [128, 1], F32)
    P.op('dve', lambda e: e.memset(eps_ln[:], 64e-5), writes=['eps_ln'])
    SL = []
    for sl in range(1):
        d = {}
        for n in ('rin', 'kin', 'vin'):
            d[n] = f32t('%s%d' % (n, sl), [128, BT + 1])
        for n in ('rs', 'ks', 'vs', 'sgw', 'asig', 'g', 'kk', 'kkn', 'k2', 'bvec', 'bonus', 'Lp', 'Lm', 'eL', 'enL', 'eLm1', 'eTL',
                  'ah', 'bh', 'kh', 'rh', 'bt', 'kt', 'yT', 'tmp', 'tmp2'):
            d[n] = f32t('%s%d' % (n, sl))
        for n in ('ah', 'bh', 'kh', 'rh'):
            d[n + '1'] = P.sb('%s1_%d' % (n, sl), [64, BT], F32)
        d['PC1'] = P.sb('PC1_%d' % sl, [64, NCH], F32)
        d['ntot'] = P.sb('ntot%d' % sl, [128, NCH], F32)
        d['PC'] = P.sb('PC%d' % sl, [128, NCH], F32)
        d['tm'] = P.sb('tm%d' % sl, [64, NCH, 384], F32)
        for h in range(2):
            d['Am%d' % h] = P.sb('Am%d_%d' % (h, sl), [64, NCH, 320], F32)
            d['Tt%d' % h] = P.sb('Tt%d_%d' % (h, sl), [64, NCH, 64], F32)
            for n in ('Mi', 'Ni'):
                d['%s%d' % (n, h)] = [P.sb('%s%d_%d_%d' % (n, h, sl, q), [64, NCH, 64], F32) for q in range(2)]
            d['Zs%d' % h] = P.sb('Zs%d_%d' % (h, sl), [64, 64], F32)
            d['Us%d' % h] = P.sb('Us%d_%d' % (h, sl), [64, 64], F32)
        SL.append(d)

    def k_(sl, n):
        return 'w%d_%s' % (sl, n)

    def shift(eng_sub, xin_t, xin_key, mu_ap, out_t, out_key, tmp_t, tmp_key):
        P.op('pool', lambda e: e.tensor_tensor(out=tmp_t[:], in0=xin_t[:, 0:BT], in1=xin_t[:, 1:BT + 1], op=ALU.subtract),
             reads=[xin_key], writes=[tmp_key])
        P.op('dve', lambda e: e.scalar_tensor_tensor(out=out_t[:], in0=tmp_t[:], scalar=mu_ap, in1=xin_t[:, 1:BT + 1], op0=ALU.mult, op1=ALU.add),
             reads=[tmp_key, xin_key, 'kc'], writes=[out_key])

    def load_prev(t, key, dram_row, t0):
        if t0 == 0:
            P.op('dve', lambda e: e.memset(t[:, 0:1], 0.0), writes=[key])
            P.op('sp', lambda e: e.dma_start(out=t[:, 1:BT + 1], in_=dram_row[:, 0:BT]), writes=[key], lane=key)
        else:
            P.op('sp', lambda e: e.dma_start(out=t[:], in_=dram_row[:, t0 - 1:t0 + BT]), writes=[key], lane=key)

    ones64 = K['blockones']
    import os as _os
    RWS = float(_os.environ.get('RWS', '9'))
    for blk in range(S // BT):
        t0 = blk * BT
        load_prev(xin['xw'], 'x_xw', src['xw'], t0)
        load_prev(xin['xa'], 'x_xa', src['xa'], t0)
        load_prev(xin['xg0'], 'x_xg0', src['xg'][0], t0)
        load_prev(xin['xg1'], 'x_xg1', src['xg'][1], t0)
        shift(None, xin['xw'], 'x_xw', K['mux'][:, 0:1], txw, 'txw', dtmp, 'dtmp')
        P.op('act', lambda e: e.activation(out=txw[:], in_=txw[:], func=AF.Tanh), reads=['txw'], writes=['txw'])
        shift(None, xin['xa'], 'x_xa', K['mux'][:, 1:2], xas, 'xas', dtmp, 'dtmp')
        shift(None, xin['xg0'], 'x_xg0', K['mux'][:, 2:3], sxg0, 'sxg0', dtmp, 'dtmp')
        P.op('act', lambda e: e.activation(out=sxg0[:], in_=sxg0[:], func=AF.Sigmoid), reads=['sxg0'], writes=['sxg0'])
        shift(None, xin['xg1'], 'x_xg1', K['mux'][:, 3:4], sxg1, 'sxg1', dtmp, 'dtmp')
        P.op('act', lambda e: e.activation(out=sxg1[:], in_=sxg1[:], func=AF.Sigmoid), reads=['sxg1'], writes=['sxg1'])
        for half in range(4):
            pairs = [half]
            for sl, pp in enumerate(pairs):
                d = SL[sl]
                kk_ = lambda n, sl=sl: k_(sl, n)
                pc = slice(pp * 128, (pp + 1) * 128)
                for n, nm, mi in (('rin', 'r', 0), ('kin', 'k', 1), ('vin', 'v', 2)):
                    load_prev(d[n], kk_(n), src[nm][pp], t0)
                shift(None, d['rin'], kk_('rin'), K['mu3'][:, pp:pp + 1], d['rs'], kk_('rs'), d['tmp'], kk_('tmp'))
                shift(None, d['kin'], kk_('kin'), K['mu3'][:, 4 + pp:5 + pp], d['ks'], kk_('ks'), d['tmp'], kk_('tmp'))
                shift(None, d['vin'], kk_('vin'), K['mu3'][:, 8 + pp:9 + pp], d['vs'], kk_('vs'), d['tmp'], kk_('tmp'))

                def T(out, fn, reads, eng='dve', force=False, d=d, kk_=kk_):
                    P.op(eng, fn, reads=[kk_(r) if not r.startswith('!') else r[1:] for r in reads], writes=[kk_(out)], force=force)
                P.op('pe', lambda e, pc=pc: e.matmul(C.psum[0][:, 0:BT], K['w_up'][:, pc], txw[:], start=True, stop=True), reads=['kc', 'txw'], writes=['ps0'])
                T('sgw', lambda e, d=d, pp=pp: e.activation(out=d['sgw'][:], in_=C.psum[0][:, 0:BT], func=AF.Sigmoid, bias=K['w0'][:, pp:pp + 1]), ['!ps0', '!kc'], 'act')
                P.op('pe', lambda e, pc=pc: e.matmul(C.psum[1][:, 0:BT], K['a_up'][:, pc], xas[:], start=True, stop=True), reads=['kc', 'xas'], writes=['ps1'])
                T('asig', lambda e, d=d, pp=pp: e.activation(out=d['asig'][:], in_=C.psum[1][:, 0:BT], func=AF.Sigmoid, bias=K['a0'][:, pp:pp + 1]), ['!ps1', '!kc'], 'act')
                P.op('pe', lambda e, pc=pc: e.matmul(C.psum[2][:, 0:BT], K['g_up0'][:, pc], sxg0[:], start=True, stop=False), reads=['kc', 'sxg0'], writes=['ps2'])
                P.op('pe', lambda e, pc=pc: e.matmul(C.psum[2][:, 0:BT], K['g_up1'][:, pc], sxg1[:], start=False, stop=True), reads=['kc', 'sxg1'], writes=['ps2'])
                T('g', lambda e, d=d: e.copy(out=d['g'][:], in_=C.psum[2][:, 0:BT]), ['!ps2'], 'act')
                T('kk', lambda e, d=d, pp=pp: e.tensor_scalar(out=d['kk'][:], in0=d['ks'][:], scalar1=K['k_k'][:, pp:pp + 1], scalar2=None, op0=ALU.mult), ['ks', '!kc'])
                T('tmp', lambda e, d=d: e.activation(out=d['tmp'][:], in_=d['kk'][:], func=AF.Square), ['kk'], 'act')
                P.op('pe', lambda e, d=d: e.matmul(C.psum[3][:, 0:BT], ones64[:], d['tmp'][:], start=True, stop=True), reads=['kc', kk_('tmp')], writes=['ps3'])
                T('tmp2', lambda e, d=d: e.activation(out=d['tmp2'][:], in_=C.psum[3][:, 0:BT], func=AF.Sqrt), ['!ps3'], 'act')
                T('tmp2', lambda e, d=d: e.tensor_scalar(out=d['tmp2'][:], in0=d['tmp2'][:], scalar1=1e-12, scalar2=None, op0=ALU.max), ['tmp2'])
                T('tmp2', lambda e, d=d: e.reciprocal(out=d['tmp2'][:], in_=d['tmp2'][:]), ['tmp2'])
                T('kkn', lambda e, d=d: e.tensor_tensor(out=d['kkn'][:], in0=d['kk'][:], in1=d['tmp2'][:], op=ALU.mult), ['kk', 'tmp2'])
                T('tmp', lambda e, d=d, pp=pp: e.tensor_scalar(out=d['tmp'][:], in0=d['asig'][:], scalar1=-1.0, scalar2=K['k_a'][:, pp:pp + 1], op0=ALU.add, op1=ALU.mult), ['asig', '!kc'])
                T('k2', lambda e, d=d: e.scalar_tensor_tensor(out=d['k2'][:], in0=d['tmp'][:], scalar=1.0, in1=d['ks'][:], op0=ALU.add, op1=ALU.mult), ['tmp', 'ks'])
                T('bvec', lambda e, d=d: e.tensor_tensor(out=d['bvec'][:], in0=d['kkn'][:], in1=d['asig'][:], op=ALU.mult), ['kkn', 'asig'], 'pool')
                T('tmp', lambda e, d=d, pp=pp: e.scalar_tensor_tensor(out=d['tmp'][:], in0=d['rs'][:], scalar=K['r_k'][:, pp:pp + 1], in1=d['k2'][:], op0=ALU.mult, op1=ALU.mult), ['rs', 'k2', '!kc'])
                P.op('pe', lambda e, d=d: e.matmul(C.psum[3][:, 0:BT], ones64[:], d['tmp'][:], start=True, stop=True), reads=['kc', kk_('tmp')], writes=['ps3'])
                T('bonus', lambda e, d=d: e.tensor_tensor(out=d['bonus'][:], in0=C.psum[3][:, 0:BT], in1=d['vs'][:], op=ALU.mult), ['!ps3', 'vs'])
                T('Lp', lambda e, d=d: e.tensor_tensor_scan(out=d['Lp'][:], data0=K['rmask'][:, 0:BT], data1=d['sgw'][:], initial=0.0, op0=ALU.mult, op1=ALU.add), ['sgw', '!kc'])
                T('Lm', lambda e, d=d: e.tensor_tensor(out=d['Lm'][:], in0=d['Lp'][:], in1=d['sgw'][:], op=ALU.subtract), ['Lp', 'sgw'], 'pool')
                T('eL', lambda e, d=d: e.activation(out=d['eL'][:], in_=d['Lp'][:], func=AF.Exp, scale=-CW), ['Lp'], 'act')
                T('enL', lambda e, d=d: e.activation(out=d['enL'][:], in_=d['Lp'][:], func=AF.Exp, scale=CW), ['Lp'], 'act')
                T('eLm1', lambda e, d=d: e.activation(out=d['eLm1'][:], in_=d['Lm'][:], func=AF.Exp, scale=-CW), ['Lm'], 'act')
                T('ntot', lambda e, d=d: e.tensor_scalar(out=d['ntot'][:], in0=d['Lp'][:].rearrange("p (c t) -> p c t", t=64)[:, :, 63], scalar1=-CW, scalar2=None, op0=ALU.mult), ['Lp'])
                T('PC', lambda e, d=d: e.activation(out=d['PC'][:], in_=d['ntot'][:], func=AF.Exp), ['ntot'], 'act')
                for cc in range(NCH):
                    cs = slice(cc * 64, (cc + 1) * 64)
                    T('eTL', lambda e, d=d, cs=cs, cc=cc: e.activation(out=d['eTL'][:, cs], in_=d['Lp'][:, cs], func=AF.Exp, scale=CW, bias=d['ntot'][:, cc:cc + 1]), ['Lp', 'ntot'], 'act')
                T('ah', lambda e, d=d: e.scalar_tensor_tensor(out=d['ah'][:], in0=d['kkn'][:], scalar=-1.0, in1=d['eLm1'][:], op0=ALU.mult, op1=ALU.mult), ['kkn', 'eLm1'])
                T('bh', lambda e, d=d: e.tensor_tensor(out=d['bh'][:], in0=d['bvec'][:], in1=d['enL'][:], op=ALU.mult), ['bvec', 'enL'], 'pool')
                T('kh', lambda e, d=d: e.tensor_tensor(out=d['kh'][:], in0=d['k2'][:], in1=d['enL'][:], op=ALU.mult), ['k2', 'enL'])
                T('rh', lambda e, d=d: e.tensor_tensor(out=d['rh'][:], in0=d['rs'][:], in1=d['eL'][:], op=ALU.mult), ['rs', 'eL'], 'pool')
                T('bt', lambda e, d=d: e.tensor_tensor(out=d['bt'][:], in0=d['bvec'][:], in1=d['eTL'][:], op=ALU.mult), ['bvec', 'eTL'])
                T('kt', lambda e, d=d: e.tensor_tensor(out=d['kt'][:], in0=d['k2'][:], in1=d['eTL'][:], op=ALU.mult), ['k2', 'eTL'], 'pool')
                for n in ('ah', 'bh', 'kh', 'rh'):
                    P.op('sp', lambda e, d=d, n=n: e.dma_start(out=d[n + '1'][0:64, :], in_=d[n][64:128, :]), reads=[kk_(n)], writes=[kk_(n + '1')], lane=kk_(n + '1'))
                P.op('sp', lambda e, d=d: e.dma_start(out=d['PC1'][0:64, :], in_=d['PC'][64:128, :]), reads=[kk_('PC')], writes=[kk_('PC1')], lane=kk_('PC1'))
                for cc in range(NCH if RWS >= 2 else 0):
                    cs = slice(cc * 64, (cc + 1) * 64)
                    for qi, n in enumerate(('vs', 'bt', 'kt')):
                        P.op('pe', lambda e, d=d, n=n, cs=cs, qi=qi: e.transpose(out=C.psum[0][0:64, qi * 128:(qi + 1) * 128], in_=d[n][:, cs], identity=K['ident_f'][:]),
                             reads=[kk_(n), 'kc'], writes=['ps0'])
                    T('tm', lambda e, d=d, cc=cc: e.copy(out=d['tm'][:, cc, :], in_=C.psum[0][0:64, 0:384]), ['!ps0'], 'act')
                for h in range(2 if RWS >= 3 else 0):
                    hs = slice(64 * h, 64 * h + 64)
                    sfx = '' if h == 0 else '1'
                    Am = d['Am%d' % h]
                    for cc in range(NCH):
                        cs = slice(cc * 64, (cc + 1) * 64)
                        for qi, (l, r) in enumerate((('bh', 'ah'), ('ah', 'bh'), ('kh', 'ah'), ('bh', 'rh'), ('kh', 'rh'))):
                            P.op('pe', lambda e, d=d, l=l, r=r, sfx=sfx, cs=cs, qi=qi: e.matmul(C.psum[1][0:64, qi * 64:(qi + 1) * 64], d[l + sfx][0:64, cs], d[r + sfx][0:64, cs], start=True, stop=True),
                                 reads=[kk_(l + sfx), kk_(r + sfx)], writes=['ps1'])
                        if RWS >= 3.3:
                          T('Am%d' % h, lambda e, Am=Am, cc=cc: e.tensor_tensor(out=Am[:, cc, :], in0=C.psum[1][0:64, 0:320], in1=K['amask'][:], op=ALU.mult), ['!ps1', '!kc'])
                    Tt = d['Tt%d' % h]
                    Mi, Ni = d['Mi%d' % h], d['Ni%d' % h]
                    if RWS >= 3.6:
                      T('Tt%d' % h, lambda e, Tt=Tt, Am=Am: e.tensor_tensor(out=Tt[:], in0=Am[:, :, 0:64], in1=K['identrep'][:, 0:NCH * 64].rearrange("p (c t) -> p c t", t=64), op=ALU.add), ['Am%d' % h, '!kc'])
                bN, bM, bR = (2, 6), (3, 7), (0, 1)
                hst = {}
                for h in range(2 if RWS >= 4 else 0):
                    Am = d['Am%d' % h]
                    hst[h] = dict(Mprev=(lambda cc, Am=Am: Am[:, cc, 0:64]), Nprev=(lambda cc, Am=Am: Am[:, cc, 64:128]),
                                  mk=kk_('Am%d' % h), nk=kk_('Am%d' % h))
                for it in range(1, 6 if RWS >= 4 else 1):
                    q = it % 2
                    for h in range(2):
                        st_ = hst[h]
                        Mprev, Nprev, mk, nk_ = st_['Mprev'], st_['Nprev'], st_['mk'], st_['nk']
                        for cc in range(NCH):
                            P.op('pe', lambda e, cc=cc, Mprev=Mprev, Nprev=Nprev, h=h: e.matmul(C.psum[bN[h]][0:64, cc * 64:(cc + 1) * 64], Mprev(cc), Nprev(cc), start=True, stop=True),
                                 reads=[mk, nk_], writes=['ps%d' % bN[h]])
                        if it < 5:
                            for cc in range(NCH):
                                P.op('pe', lambda e, cc=cc, Mprev=Mprev, Nprev=Nprev, h=h: e.matmul(C.psum[bM[h]][0:64, cc * 64:(cc + 1) * 64], Nprev(cc), Mprev(cc), start=True, stop=True),
                                     reads=[mk, nk_], writes=['ps%d' % bM[h]])
                    for h in range(2):
                        Nn, Mn = d['Ni%d' % h][q], d['Mi%d' % h][q]
                        T('Ni%d_%d' % (h, q), lambda e, Nn=Nn, h=h: e.copy(out=Nn[:], in_=C.psum[bN[h]][0:64, 0:NCH * 64].rearrange("p (c t) -> p c t", t=64)), ['!ps%d' % bN[h]], 'act')
                        if it < 5:
                            T('Mi%d_%d' % (h, q), lambda e, Mn=Mn, h=h: e.tensor_copy(out=Mn[:], in_=C.psum[bM[h]][0:64, 0:NCH * 64].rearrange("p (c t) -> p c t", t=64)), ['!ps%d' % bM[h]])
                    for h in range(2):
                        Nn, Tt = d['Ni%d' % h][q], d['Tt%d' % h]
                        for cc in range(NCH):
                            P.op('pe', lambda e, cc=cc, Nn=Nn, Tt=Tt, h=h: e.matmul(C.psum[bR[h]][0:64, cc * 64:(cc + 1) * 64], Nn[:, cc, :], Tt[:, cc, :], start=True, stop=True),
                                 reads=[kk_('Ni%d_%d' % (h, q)), kk_('Tt%d' % h)], writes=['ps%d' % bR[h]])
                    for h in range(2):
                        Tt = d['Tt%d' % h]
                        Nn, Mn = d['Ni%d' % h][q], d['Mi%d' % h][q]
                        T('Tt%d' % h, lambda e, Tt=Tt, h=h: e.tensor_tensor(out=Tt[:], in0=Tt[:], in1=C.psum[bR[h]][0:64, 0:NCH * 64].rearrange("p (c t) -> p c t", t=64), op=ALU.add), ['Tt%d' % h, '!ps%d' % bR[h]])
                        hst[h] = dict(Mprev=(lambda cc, Mn=Mn: Mn[:, cc, :]), Nprev=(lambda cc, Nn=Nn: Nn[:, cc, :]),
                                      mk=kk_('Mi%d_%d' % (h, q)), nk=kk_('Ni%d_%d' % (h, q)))
            for cc in range(NCH if RWS >= 5 else 0):
                cs = slice(cc * 64, (cc + 1) * 64)
                for stage in range(4):
                    for sl, pp in enumerate(pairs):
                        d = SL[sl]
                        kk_ = lambda n, sl=sl: k_(sl, n)
                        for h in range(2):
                            hs = slice(64 * h, 64 * h + 64)
                            B = 4 + 2 * sl + h
                            bk = 'ps%d' % B
                            Am, Tt, tm = d['Am%d' % h], d['Tt%d' % h], d['tm']
                            Zs, Us = d['Zs%d' % h], d['Us%d' % h]
                            vtm = tm[:, cc, 64 * h:64 * h + 64]
                            btm = tm[:, cc, 128 + 64 * h:192 + 64 * h]
                            ktm = tm[:, cc, 256 + 64 * h:320 + 64 * h]
                            stk = 'ST%d_%d' % (pp, h)
                            ST = STh[pp][h]
                            sfx = '' if h == 0 else '1'
                            if stage == 0:
                                P.op('pe', lambda e, d=d, sfx=sfx, cs=cs, B=B, ST=ST: e.matmul(C.psum[B][0:64, 0:64], d['ah' + sfx][0:64, cs], ST[:], start=True, stop=False),
                                     reads=[kk_('ah' + sfx), stk], writes=[bk])
                                P.op('pe', lambda e, Am=Am, cc=cc, vtm=vtm, B=B: e.matmul(C.psum[B][0:64, 0:64], Am[:, cc, 128:192], vtm, start=False, stop=True),
                                     reads=[kk_('Am%d' % h), kk_('tm')], writes=[bk])
                                P.op('act', lambda e, Zs=Zs, B=B: e.copy(out=Zs[:], in_=C.psum[B][0:64, 0:64]), reads=[bk], writes=[kk_('Zs%d' % h)])
                            elif stage == 1:
                                P.op('pe', lambda e, Tt=Tt, cc=cc, Zs=Zs, B=B: e.matmul(C.psum[B][0:64, 64:128], Tt[:, cc, :], Zs[:], start=True, stop=True),
                                     reads=[kk_('Tt%d' % h), kk_('Zs%d' % h)], writes=[bk])
                                P.op('dve', lambda e, Us=Us, B=B: e.tensor_copy(out=Us[:], in_=C.psum[B][0:64, 64:128]), reads=[bk], writes=[kk_('Us%d' % h)])
                            elif stage == 2:
                                P.op('pe', lambda e, d=d, sfx=sfx, hs=hs, cs=cs, B=B, ST=ST: e.matmul(C.psum[B][hs, 128:192], ST[:], d['rh' + sfx][0:64, cs], start=True, stop=False),
                                     reads=[stk, kk_('rh' + sfx)], writes=[bk])
                                P.op('pe', lambda e, Us=Us, Am=Am, cc=cc, hs=hs, B=B: e.matmul(C.psum[B][hs, 128:192], Us[:], Am[:, cc, 192:256], start=False, stop=False),
                                     reads=[kk_('Us%d' % h), kk_('Am%d' % h)], writes=[bk])
                                P.op('pe', lambda e, vtm=vtm, Am=Am, cc=cc, hs=hs, B=B: e.matmul(C.psum[B][hs, 128:192], vtm, Am[:, cc, 256:320], start=False, stop=True),
                                     reads=[kk_('tm'), kk_('Am%d' % h)], writes=[bk])
                                P.op('act', lambda e, d=d, hs=hs, cs=cs, B=B: e.copy(out=d['yT'][hs, cs], in_=C.psum[B][hs, 128:192]), reads=[bk], writes=[kk_('yT')])
                            else:
                                P.op('pe', lambda e, btm=btm, Us=Us, B=B: e.matmul(C.psum[B][0:64, 192:256], btm, Us[:], start=True, stop=False),
                                     reads=[kk_('tm'), kk_('Us%d' % h)], writes=[bk])
                                P.op('pe', lambda e, ktm=ktm, vtm=vtm, B=B: e.matmul(C.psum[B][0:64, 192:256], ktm, vtm, start=False, stop=True),
                                     reads=[kk_('tm')], writes=[bk])
                                pcn = 'PC' if h == 0 else 'PC1'
                                P.op('dve', lambda e, d=d, pcn=pcn, cc=cc, B=B, ST=ST: e.scalar_tensor_tensor(out=ST[:], in0=ST[:], scalar=d[pcn][0:64, cc:cc + 1],
                                                                                                        in1=C.psum[B][0:64, 192:256], op0=ALU.mult, op1=ALU.add),
                                     reads=[bk, kk_(pcn), stk], writes=[stk])
            for sl, pp in enumerate(pairs):
                d = SL[sl]
                kk_ = lambda n, sl=sl: k_(sl, n)

                def T(out, fn, reads, eng='dve', d=d, kk_=kk_):
                    P.op(eng, fn, reads=[kk_(r) if not r.startswith('!') else r[1:] for r in reads], writes=[kk_(out)])
                P.op('pe', lambda e, d=d: e.matmul(C.psum[0][:, 0:BT], ones64[:], d['yT'][:], start=True, stop=True), reads=['kc', kk_('yT')], writes=['ps0'])
                T('tmp', lambda e, d=d: e.scalar_tensor_tensor(out=d['tmp'][:], in0=C.psum[0][:, 0:BT], scalar=-1.0 / 64, in1=d['yT'][:], op0=ALU.mult, op1=ALU.add), ['!ps0', 'yT'])
                T('tmp2', lambda e, d=d: e.activation(out=d['tmp2'][:], in_=d['tmp'][:], func=AF.Square), ['tmp'], 'act')
                P.op('pe', lambda e, d=d: e.matmul(C.psum[1][:, 0:BT], ones64[:], d['tmp2'][:], start=True, stop=True), reads=['kc', kk_('tmp2')], writes=['ps1'])
                T('tmp2', lambda e, d=d: e.activation(out=d['tmp2'][:], in_=C.psum[1][:, 0:BT], func=AF.Sqrt, scale=1.0 / 64, bias=eps_ln[:, 0:1]), ['!ps1', '!eps_ln'], 'act')
                T('tmp2', lambda e, d=d: e.reciprocal(out=d['tmp2'][:], in_=d['tmp2'][:]), ['tmp2'])
                T('tmp', lambda e, d=d: e.tensor_tensor(out=d['tmp'][:], in0=d['tmp'][:], in1=d['tmp2'][:], op=ALU.mult), ['tmp', 'tmp2'])
                T('tmp', lambda e, d=d, pp=pp: e.tensor_scalar(out=d['tmp'][:], in0=d['tmp'][:], scalar1=K['lng'][:, pp:pp + 1], scalar2=K['lnb'][:, pp:pp + 1], op0=ALU.mult, op1=ALU.add), ['tmp', '!kc'])
                T('tmp', lambda e, d=d: e.tensor_tensor(out=d['tmp'][:], in0=d['tmp'][:], in1=d['bonus'][:], op=ALU.add), ['tmp', 'bonus'], 'pool')
                T('yT', lambda e, d=d: e.tensor_tensor(out=d['yT'][:], in0=d['tmp'][:], in1=d['g'][:], op=ALU.mult), ['tmp', 'g'])
                P.op('sp', lambda e, d=d, pp=pp, t0=t0: e.dma_start(out=yout[pp, :, t0:t0 + BT], in_=d['yT'][:]), reads=[kk_('yT')], lane=kk_('yT') + '_st')


def decl_mix1a(nc, D, S, sfx=''):
    KC = D // 128
    di = lambda n, s: nc.dram_tensor(n + sfx, s, F32, kind="ExternalInput").ap()
    T = dict(wint=di("wint1", [24, 128, KC * 128]))
    T['cd'] = {n: di(n, shp) for n, shp in dict(convw=[128, 4, 31], convb=[128, 4], mu3=[128, 12], mux=[128, 4], w0=[128, 4], a0=[128, 4],
                                                k_k=[128, 4], k_a=[128, 4], r_k=[128, 4], lng=[128, 4], lnb=[128, 4], w_up=[128, 512],
                                                a_up=[128, 512], g_up0=[128, 512], g_up1=[128, 512], amask=[64, 320], blockones=[128, 128],
                                                identrep=[64, 512], ident1=[128, 128], rmask=[128, 512]).items()}
    return T


def build_mix1a(D, S, TT=1024):
    nc = bass.Bass("TRN2", target_bir_lowering=False)
    KC = D // 128
    TT = min(TT, S)
    di = lambda n, s: nc.dram_tensor(n, s, F32, kind="ExternalInput").ap()
    x = di("x", [KC, 128, S])
    ng, sc, shf = di("ng", [128, KC]), di("sc", [128, KC]), di("shf", [128, KC])
    T = decl_mix1a(nc, D, S)
    convo = nc.dram_tensor("convo", [4, 128, S], F32, kind="ExternalOutput").ap()
    yout = nc.dram_tensor("yout", [4, 128, S], F32, kind="ExternalOutput").ap()
    P = Prog(nc)
    C = Common(P, D, TT)
    stage_mix1a(P, C, nc, T, x, ng, sc, shf, lambda c, t0, n: convo[c, :, t0:t0 + n], yout, S)
    P.finish()
    return nc


def stage_mix1a(P, C, nc, T, x, ng, sc, shf, convo_put, yout, S, sfx=''):
    wint, cd = T['wint'], T['cd']
    TT = C.TT
    BT = min(512, S)
    scr = lambda n, s: nc.dram_tensor(n + sfx, s, F32, kind="Internal").ap()
    u = scr("u_s", [4, 128, S])
    rr, kr_, vr_ = scr("r_s", [4, 128, S]), scr("k_s", [4, 128, S]), scr("v_s", [4, 128, S])
    xw, xa, xg = scr("xw_s", [128, S]), scr("xa_s", [128, S]), scr("xg_s", [2, 128, S])
    P.phase_begin()
    stg = Stager(P, 'stg', 4, F32)
    stp = Stager(P, 'stp', 4, F32)

    def whole(dst):
        def f(t, k, tok0):
            st_dma(P, dst[:, tok0:tok0 + 512], t[:], k)
        return f

    def glu_epi(dst):
        def epi(banks, tok0, hh):
            bA, bB = banks
            s_, sk = stg.get()
            o, ok = stg.get()
            P.op('act', lambda e: e.activation(out=s_[:], in_=C.psum[bB][:], func=AF.Sigmoid), reads=['ps%d' % bB], writes=[sk])
            P.op('dve', lambda e: e.tensor_tensor(out=o[:], in0=C.psum[bA][:], in1=s_[:], op=ALU.mult), reads=['ps%d' % bA, sk], writes=[ok])
            st_dma(P, dst[:, tok0:tok0 + 512], o[:], ok)
        return epi
    groups = [(2, glu_epi(u[c])) for c in range(4)]
    for T_ in (rr, kr_, vr_):
        for c in range(4):
            groups.append((1, make_act_epi(P, C, stp, AF.Copy, whole(T_[c]))))
    for dst in (xw, xa, xg[0], xg[1]):
        groups.append((1, make_act_epi(P, C, stp, AF.Copy, whole(dst))))
    emit_inproj(P, C, x, ng, sc, shf, wint, S, groups)
    P.phase_end()
    import os as _os
    PH = int(_os.environ.get('PH', '7'))
    P.phase_begin()
    cw = load_vec(P, 'cw', cd['convw'].rearrange("p c k -> p (c k)"), 4 * 31)
    cb = load_vec(P, 'cb', cd['convb'], 4)
    P.barrier()
    up = [P.sb('up%d' % i, [128, S + 30], F32) for i in range(2)]
    acc = [P.sb('acc%d' % i, [128, S], F32) for i in range(2)]
    for c in range(4 if PH & 2 else 0):
        ut, uk = up[c % 2], 'up%d' % (c % 2)
        at, ak = acc[c % 2], 'acc%d' % (c % 2)
        P.op('dve', lambda e, ut=ut: e.memset(ut[:, 0:30], 0.0), writes=[uk])
        P.op('sp', lambda e, ut=ut, c=c: e.dma_start(out=ut[:, 30:30 + S], in_=u[c]), writes=[uk], lane=uk)
        P.op('dve', lambda e, ut=ut, at=at, c=c: e.tensor_scalar(out=at[:], in0=ut[:, 0:S], scalar1=cw[:, c * 31:c * 31 + 1], scalar2=cb[:, c:c + 1],
                                                                op0=ALU.mult, op1=ALU.add), reads=[uk], writes=[ak])
        for k in range(1, 31):
            P.op('dve', lambda e, ut=ut, at=at, c=c, k=k: e.scalar_tensor_tensor(out=at[:], in0=ut[:, k:k + S], scalar=cw[:, c * 31 + k:c * 31 + k + 1], in1=at[:],
                                                                                op0=ALU.mult, op1=ALU.add), reads=[uk, ak], writes=[ak])
        for sb_ in range(S // 512):
            P.op('sp', lambda e, at=at, c=c, sb_=sb_: e.dma_start(out=convo_put(c, sb_ * 512, 512), in_=at[:, sb_ * 512:(sb_ + 1) * 512]), reads=[ak], lane=ak + '_st')
    P.phase_end()
    P.phase_begin()
    spec = {n: (cd[n], list(cd[n].shape), F32) for n in ('mu3', 'mux', 'w0', 'a0', 'k_k', 'k_a', 'r_k', 'lng', 'lnb', 'w_up', 'a_up', 'g_up0', 'g_up1',
                                                        'amask', 'blockones', 'identrep', 'rmask')}
    spec['ident_f'] = (cd['ident1'], [128, 128], F32)
    K = load_consts(P, spec)
    P.barrier()
    if PH & 4:
        emit_rwkv(P, C, K, dict(r=rr, k=kr_, v=vr_, xw=xw, xa=xa, xg=xg), yout, S, BT=BT)
    P.phase_end()


def build_mix1b(D, S, TT=1024):
    nc = bass.Bass("TRN2", target_bir_lowering=False)
    KC = D // 128
    TT = min(TT, S)
    di = lambda n, s: nc.dram_tensor(n, s, F32, kind="ExternalInput").ap()
    call = di("call", [16, 128, S])
    own = di("own", [4, 128, S])
    yr = di("yr", [4, 128, S])
    lg, lb = di("lg", [128, 4]), di("lb", [128, 4])
    woutt = di("woutt", [KC, 128, 8 * 128])
    part = nc.dram_tensor("part", [KC, 128, S], F32, kind="ExternalOutput").ap()
    P = Prog(nc)
    C = Common(P, D, TT)
    stage_mix1b(P, C, nc, lambda c, t0, n: call[c, :, t0:t0 + n], lambda c, t0, n: own[c, :, t0:t0 + n], yr, lg, lb, woutt, part, S)
    P.finish()
    return nc


def stage_mix1b(P, C, nc, call_get, own_get, yr, lg, lb, woutt, part, S, sfx=''):
    un = nc.dram_tensor("un_s" + sfx, [4, 128, S], F32, kind="Internal").ap()
    P.phase_begin()
    lgt = load_vec(P, 'lgt', lg, 4)
    lbt = load_vec(P, 'lbt', lb, 4)
    eps = P.sb('eps5', [128, 1], F32)
    P.op('dve', lambda e: e.memset(eps[:], 1e-5), writes=['eps5'])
    P.barrier()
    xs = Stager(P, 'cx', 3, F32)
    sq = Stager(P, 'csq', 2, F32)
    mean, msq, rstd = P.sb('mean', [128, 512], F32), P.sb('msq', [128, 512], F32), P.sb('rstd5', [128, 512], F32)
    ost = Stager(P, 'co', 2, F32)
    for hh in range(S // 512):
        sl = slice(hh * 512, (hh + 1) * 512)
        for c in range(16):
            xt, xk = xs.get()
            st, sk = sq.get()
            P.op('sp', lambda e, xt=xt, c=c, sl=sl: e.dma_start(out=xt[:], in_=call_get(c, sl.start, 512)), writes=[xk], lane=xk)
            P.op('act', lambda e, xt=xt, st=st: e.activation(out=st[:], in_=xt[:], func=AF.Square), reads=[xk], writes=[sk])
            P.op('pe', lambda e, xt=xt, c=c: e.matmul(C.psum[0][:], C.ones_f[:], xt[:], start=(c == 0), stop=(c == 15)), reads=[xk, 'ones_f'], writes=['ps0'])
            P.op('pe', lambda e, st=st, c=c: e.matmul(C.psum[1][:], C.ones_f[:], st[:], start=(c == 0), stop=(c == 15)), reads=[sk, 'ones_f'], writes=['ps1'])
        P.op('act', lambda e: e.activation(out=mean[:], in_=C.psum[0][:], func=AF.Copy, scale=1.0 / 2048), reads=['ps0'], writes=['mean'])
        P.op('act', lambda e: e.activation(out=msq[:], in_=mean[:], func=AF.Square), reads=['mean'], writes=['msq'])
        P.op('dve', lambda e: e.scalar_tensor_tensor(out=rstd[:], in0=C.psum[1][:], scalar=1.0 / 2048, in1=msq[:], op0=ALU.mult, op1=ALU.subtract),
             reads=['ps1', 'msq'], writes=['rstd5'])
        P.op('act', lambda e: e.activation(out=rstd[:], in_=rstd[:], func=AF.Sqrt, bias=eps[:, 0:1]), reads=['rstd5', 'eps5'], writes=['rstd5'])
        P.op('dve', lambda e: e.reciprocal(out=rstd[:], in_=rstd[:]), reads=['rstd5'], writes=['rstd5'])
        for c in range(4):
            xt, xk = xs.get()
            o, ok = ost.get()
            P.op('sp', lambda e, xt=xt, c=c, sl=sl: e.dma_start(out=xt[:], in_=own_get(c, sl.start, 512)), writes=[xk], lane=xk)
            P.op('dve', lambda e, xt=xt: e.tensor_tensor(out=xt[:], in0=xt[:], in1=mean[:], op=ALU.subtract), reads=[xk, 'mean'], writes=[xk])
            P.op('pool', lambda e, xt=xt: e.tensor_tensor(out=xt[:], in0=xt[:], in1=rstd[:], op=ALU.mult), reads=[xk, 'rstd5'], writes=[xk])
            P.op('dve', lambda e, xt=xt, c=c: e.tensor_scalar(out=xt[:], in0=xt[:], scalar1=lgt[:, c:c + 1], scalar2=lbt[:, c:c + 1], op0=ALU.mult, op1=ALU.add),
                 reads=[xk], writes=[xk])
            P.op('act', lambda e, xt=xt, o=o: e.activation(out=o[:], in_=xt[:], func=AF.Silu), reads=[xk], writes=[ok])
            st_dma(P, un[c, :, sl], o[:], ok)
    P.phase_end()
    P.phase_begin()
    emit_outproj(P, C, [un[c] for c in range(4)] + [yr[c] for c in range(4)], woutt, part, S, 8)
    P.phase_end()


def mix1_cols(j):
    CC, RD = 2048, 2048
    cols = []
    for c in range(4):
        ch = j * 512 + c * 128 + np.arange(128)
        cols += [ch, CC + ch]
    base = 2 * CC
    for off in (0, RD, 2 * RD):
        for c in range(4):
            cols.append(base + off + j * 512 + c * 128 + np.arange(128))
    pad = -np.ones(32, np.int64)
    cols.append(np.concatenate([base + 3 * RD + np.arange(96), pad]))
    cols.append(np.concatenate([base + 3 * RD + 96 + np.arange(96), pad]))
    cols.append(base + 3 * RD + 192 + np.arange(128))
    cols.append(base + 3 * RD + 192 + 128 + np.arange(128))
    return np.concatenate(cols)


def take_cols(w, idx):
    out = np.zeros((w.shape[0], len(idx)), np.float32)
    ok = idx >= 0
    out[:, ok] = w[:, idx[ok]]
    return out


def pad_rows(w, n):
    out = np.zeros((n, w.shape[1]), np.float32)
    out[:w.shape[0]] = w
    return out


def mix1a_inputs(x_b, ng, sc, shf, p, j, S):
    D = p['odd_w_in'].shape[0]
    KC = D // 128
    ch = slice(j * 512, (j + 1) * 512)
    mu = p['rwkv_mu']
    pv = lambda v: chunked(v[ch], 4)
    im = dict(x=(fm(x_b, KC) if x_b is not None else None), ng=chunked(ng, KC), sc=chunked(sc, KC), shf=chunked(shf, KC),
              wint1=tile_w(take_cols(p['odd_w_in'], mix1_cols(j)), KC),
              convw=np.ascontiguousarray(p['conv_w'][:, ch].reshape(31, 4, 128).transpose(2, 1, 0)),
              convb=pv(p['conv_b']),
              mu3=np.concatenate([pv(mu[0:2048]), pv(mu[2048:4096]), pv(mu[4096:6144])], axis=1),
              mux=np.stack([np.pad(mu[6144:6240], (0, 32)), np.pad(mu[6240:6336], (0, 32)), mu[6336:6464], mu[6464:6592]], axis=1).astype(np.float32),
              w0=pv(p['rwkv_w0']), a0=pv(p['rwkv_a0']), k_k=pv(p['rwkv_k_k']), k_a=pv(p['rwkv_k_a']), r_k=pv(p['rwkv_r_k'].reshape(-1)),
              lng=pv(p['rwkv_lnx_g']), lnb=pv(p['rwkv_lnx_b']),
              w_up=pad_rows(p['rwkv_w_up'][:, ch], 128), a_up=pad_rows(p['rwkv_a_up'][:, ch], 128),
              g_up0=np.ascontiguousarray(p['rwkv_g_up'][0:128, ch]), g_up1=np.ascontiguousarray(p['rwkv_g_up'][128:256, ch]),
              ident1=np.eye(128, dtype=np.float32))
    rmask = np.ones((128, 512), np.float32)
    rmask[:, 0::64] = 0.0
    im['rmask'] = rmask
    im.update(rwkv_const_arrays())
    return im


def mix1b_inputs(conv_all_fm, own_fm, yr_fm, p, j, D):
    ch = slice(j * 512, (j + 1) * 512)
    rows = np.concatenate([np.arange(j * 512, (j + 1) * 512), 2048 + np.arange(j * 512, (j + 1) * 512)])
    return dict(call=conv_all_fm, own=own_fm, yr=yr_fm, lg=chunked(p['conv_ln_g'][ch], 4), lb=chunked(p['conv_ln_b'][ch], 4),
                woutt=tile_w(p['odd_w_out'][rows, :], 8))


def build_ada(D, NCH):
    nc = bass.Bass("TRN2", target_bir_lowering=False)
    KC = D // 128
    ct = nc.dram_tensor("ct", [128, KC * 2], F32, kind="ExternalInput").ap()
    wt = nc.dram_tensor("wt", [NCH, 128, KC * 128], F32, kind="ExternalInput").ap()
    bt = nc.dram_tensor("bt", [128, NCH], F32, kind="ExternalInput").ap()
    mod = nc.dram_tensor("mod", [128, NCH * 2], F32, kind="ExternalOutput").ap()
    P = Prog(nc)
    C = Common(P, D, 512)
    c_sb = load_vec(P, 'c_sb', ct, KC * 2)
    b_sb = load_vec(P, 'b_sb', bt, NCH)
    P.op('act', lambda e: e.activation(out=c_sb[:], in_=c_sb[:], func=AF.Silu), reads=['c_sb'], writes=['c_sb'])
    res = P.sb('res', [128, NCH * 2], F32)
    slots = [P.sb('aw%d' % i, [128, KC * 128], F32) for i in range(3)]
    for m in range(NCH):
        w, wk = slots[m % 3], 'aw%d' % (m % 3)
        P.op('sp', lambda e, w=w, m=m: e.dma_start(out=w[:], in_=wt[m]), writes=[wk], lane=wk)
        b = m % 8
        for k in range(KC):
            P.op('pe', lambda e, w=w, k=k, b=b: e.matmul(C.psum[b][:, 0:2], w[:, k * 128:(k + 1) * 128], c_sb[:, 2 * k:2 * k + 2], start=(k == 0), stop=(k == KC - 1)),
                 reads=[wk, 'c_sb'], writes=['ps%d' % b])
        P.op('dve', lambda e, m=m, b=b: e.tensor_scalar(out=res[:, 2 * m:2 * m + 2], in0=C.psum[b][:, 0:2], scalar1=b_sb[:, m:m + 1], scalar2=None, op0=ALU.add),
             reads=['ps%d' % b, 'b_sb'], writes=['res'])
    P.op('sp', lambda e: e.dma_start(out=mod, in_=res[:]), reads=['res'], lane='res_st')
    P.finish()
    return nc


NCORE = 8
_progs = {}


def _dbg(name, a):
    import os
    if os.environ.get('KDEBUG'):
        a = np.asarray(a)
        print('[dbg]', name, a.shape, float(np.abs(a).mean()), float(np.abs(a).max()), bool(np.isfinite(a).all()), flush=True)


def _prog(key, builder):
    if key not in _progs:
        _progs[key] = builder()
    return _progs[key]


def _run(nc, in_maps):
    import os
    if os.environ.get('KTRACE'):
        r = run_bass_kernel_spmd(nc, in_maps, core_ids=list(range(NCORE)), trace=True)
        print('[ktrace] exec_time_ns', r.exec_time_ns, flush=True)
        return r.results
    return run_bass_kernel_spmd(nc, in_maps, core_ids=list(range(NCORE))).results


def run_ada(c, w_ada, b_ada):
    depth, D, N6 = w_ada.shape
    B = c.shape[0]
    KC = D // 128
    per = depth * N6 // NCORE
    nch = per // 128
    nc = build_ada(D, nch)
    ct = np.ascontiguousarray(c.T.reshape(KC, 128, B).transpose(1, 0, 2).reshape(128, KC * B))
    ims = []
    for i in range(NCORE):
        g0 = i * per
        layer, col0 = divmod(g0, N6)
        wsl = w_ada[layer][:, col0:col0 + per]
        ims.append(dict(ct=ct, wt=tile_w(wsl, KC), bt=chunked(b_ada[layer][col0:col0 + per], nch)))
    res = _run(nc, ims)
    flat = np.zeros((depth * N6, B), np.float32)
    for i in range(NCORE):
        m = res[i]["mod"].reshape(128, nch, B)
        flat[i * per:(i + 1) * per] = m.transpose(1, 0, 2).reshape(per, B)
    mods = flat.reshape(depth, 6, D, B).transpose(0, 1, 3, 2)
    return np.ascontiguousarray(mods)


def run_reduce(parts, x_fm, ng, gate, D, S, B):
    KC = D // 128
    G = NCORE // B
    TS = S // G
    nc = _prog(('red', D, TS, G), lambda: build_reduce(D, TS, G))
    ims = []
    for i in range(NCORE):
        b, q = divmod(i, G)
        ts = slice(q * TS, (q + 1) * TS)
        ims.append(dict(part=np.ascontiguousarray(np.stack([parts[b * G + g][:, :, ts] for g in range(G)])),
                        x=np.ascontiguousarray(x_fm[b][:, :, ts]), ng=chunked(ng, KC), gate=chunked(gate[b], KC)))
    res = _run(nc, ims)
    out = []
    for b in range(B):
        out.append(np.ascontiguousarray(np.concatenate([res[b * G + q]["xo"] for q in range(G)], axis=2)))
    return out


def kernel_unfused(x, c, w_ada, b_ada, norm_g, w_ffn_in, w_ffn_out, even_w_in, even_w_out, odd_w_in, odd_w_out, conv_w, conv_b,
           conv_ln_g, conv_ln_b, rwkv_mu, rwkv_w0, rwkv_w_up, rwkv_a0, rwkv_a_up, rwkv_g_up, rwkv_k_k, rwkv_k_a, rwkv_r_k,
           rwkv_lnx_g, rwkv_lnx_b):
    f = lambda a: np.asarray(a, dtype=np.float32)
    x, c = f(x), f(c)
    B, S, D = x.shape
    KC = D // 128
    G = NCORE // B
    depth = w_ada.shape[0]
    mods = run_ada(c, f(w_ada), f(b_ada))
    _dbg('mods', mods)
    x_fm = [fm(x[b], KC) for b in range(B)]
    FH = w_ffn_out.shape[1]
    HCt = -(-(FH // 128) // G)
    for layer in range(depth):
        sh_m, sc_m, g_m, sh_f, sc_f, g_f = [mods[layer, i] for i in range(6)]
        jj = layer // 2
        if layer % 2 == 0:
            nc = _prog(('mix0', D, S), lambda: build_mix0(D, S))
            w_in, w_out = f(even_w_in[jj]), f(even_w_out[jj])
            wl = [(tile_w(w_in[:, mix0_cols(j)], KC), tile_w(w_out[mix0_rows(j), :], 12)) for j in range(G)]
            ims = []
            for i in range(NCORE):
                b, j = divmod(i, G)
                im = mix0_inputs_fast(x_fm[b], norm_g[layer, 0], sc_m[b], sh_m[b], wl[j], j, S)
                ims.append(im)
            res = _run(nc, ims)
            parts = [res[i]["part"] for i in range(NCORE)]
            del res, ims, wl
        else:
            p = dict(odd_w_in=f(odd_w_in[jj]), odd_w_out=f(odd_w_out[jj]), conv_w=f(conv_w[jj]), conv_b=f(conv_b[jj]),
                     conv_ln_g=f(conv_ln_g[jj]), conv_ln_b=f(conv_ln_b[jj]), rwkv_mu=f(rwkv_mu[jj]), rwkv_w0=f(rwkv_w0[jj]),
                     rwkv_w_up=f(rwkv_w_up[jj]), rwkv_a0=f(rwkv_a0[jj]), rwkv_a_up=f(rwkv_a_up[jj]), rwkv_g_up=f(rwkv_g_up[jj]),
                     rwkv_k_k=f(rwkv_k_k[jj]), rwkv_k_a=f(rwkv_k_a[jj]), rwkv_r_k=f(rwkv_r_k[jj]), rwkv_lnx_g=f(rwkv_lnx_g[jj]),
                     rwkv_lnx_b=f(rwkv_lnx_b[jj]))
            nca = _prog(('mix1a', D, S), lambda: build_mix1a(D, S))
            base = [mix1a_inputs(None, norm_g[layer, 0], sc_m[0], sh_m[0], p, j, S) for j in range(G)]
            ims = []
            for i in range(NCORE):
                b, j = divmod(i, G)
                im = dict(base[j])
                im.update(x=x_fm[b], sc=chunked(sc_m[b], KC), shf=chunked(sh_m[b], KC))
                ims.append(im)
            res = _run(nca, ims)
            ncb = _prog(('mix1b', D, S), lambda: build_mix1b(D, S))
            ims = []
            for i in range(NCORE):
                b, j = divmod(i, G)
                call = np.ascontiguousarray(np.concatenate([res[b * G + g]["convo"] for g in range(G)], axis=0))
                ims.append(mix1b_inputs(call, res[i]["convo"], res[i]["yout"], p, j, D))
            res = _run(ncb, ims)
            parts = [res[i]["part"] for i in range(NCORE)]
            del res, ims, base
        _dbg('parts_mix', parts[0])
        x_fm = run_reduce(parts, x_fm, norm_g[layer, 1], g_m, D, S, B)
        _dbg('x_after_mix', x_fm[0])
        del parts
        nc = _prog(('ffn', D, S, HCt), lambda: build_ffn(D, S, HCt))
        w1, w2 = f(w_ffn_in[layer]), f(w_ffn_out[layer])
        wl = []
        for j in range(G):
            h0, h1 = j * HCt * 128, min((j + 1) * HCt * 128, FH)
            g_ = np.zeros((D, HCt * 128), np.float32)
            u_ = np.zeros((D, HCt * 128), np.float32)
            g_[:, :h1 - h0] = w1[:, h0:h1]
            u_[:, :h1 - h0] = w1[:, FH + h0:FH + h1]
            w1t = np.empty((2 * HCt, 128, KC * 128), np.float32)
            w1t[0::2] = tile_w(g_, KC)
            w1t[1::2] = tile_w(u_, KC)
            w2p = np.zeros((HCt * 128, D), np.float32)
            w2p[:h1 - h0] = w2[h0:h1]
            wl.append((w1t, tile_w(w2p, HCt)))
        ims = []
        for i in range(NCORE):
            b, j = divmod(i, G)
            ims.append(dict(x=x_fm[b], ng=chunked(norm_g[layer, 2], KC), sc=chunked(sc_f[b], KC), shf=chunked(sh_f[b], KC),
                            w1t=wl[j][0], w2t=wl[j][1]))
        res = _run(nc, ims)
        parts = [res[i]["y"] for i in range(NCORE)]
        del res, ims, wl
        _dbg('parts_ffn', parts[0])
        x_fm = run_reduce(parts, x_fm, norm_g[layer, 3], g_f, D, S, B)
        _dbg('x_after_ffn', x_fm[0])
        del parts
    out = np.stack([x_fm[b].reshape(D, S).T for b in range(B)]).astype(np.float32)
    return np.ascontiguousarray(out)


def mix0_inputs_fast(xfm_b, ng, sc, shf, wl, j, S):
    KC = xfm_b.shape[0]
    cm, sm = rope_np(S, 128)
    cr, sr = rope_np(S, 256)
    rope = np.ascontiguousarray(np.stack([np.concatenate([cm.T, cm.T]), np.concatenate([sm.T, sm.T]), cr.T, sr.T]).astype(np.float32))
    im = dict(x=xfm_b, ng=chunked(ng, KC), sc=chunked(sc, KC), shf=chunked(shf, KC), wint=wl[0], woutt=wl[1], rope=rope)
    im.update(moba_const_arrays(S))
    im.update(ret_const_arrays([2 * j, 2 * j + 1]))
    return im


GRP4 = [[0, 1, 2, 3], [4, 5, 6, 7]]
GRP8 = [[0, 1, 2, 3, 4, 5, 6, 7]]
AG_MAX_BYTES = 1 << 20


def emit_cc(P, kind, groups, src2d, dst2d, reads=(), writes=(), chain=True):
    op = ALU.bypass if kind == 'AllGather' else ALU.add
    ch = ['cc_chain'] if chain else []
    P.op('pool', lambda e: e.collective_compute(kind, op, replica_groups=groups, ins=[src2d], outs=[dst2d]),
         reads=list(reads) + ch, writes=list(writes) + ch, lane='cc', lane_inc=1)


def build_fused(D, S, HCt, depth=2):
    nc = bass.Bass("TRN2", target_bir_lowering=False)
    KC = D // 128
    G = 4
    TS = S // G
    TT = min(1024, TS)
    N6 = 6 * D
    per = depth * N6 // 4
    nch = per // 128
    di = lambda n, s: nc.dram_tensor(n, s, F32, kind="ExternalInput").ap()
    scr = lambda n, s: nc.dram_tensor(n, s, F32, kind="Internal").ap()
    x_full = di("x_full", [KC, 128, S])
    x_own = di("x_own", [KC, 128, TS])
    sel = di("sel", [128, 2])
    ngall = di("ngall", [depth * 4, 128, KC])
    ct, wt_ada, bt_ada = di("ct", [128, KC * 2]), di("wt_ada", [nch, 128, KC * 128]), di("bt_ada", [128, nch])
    T0 = decl_mix0(nc, D, S)
    T1 = decl_mix1a(nc, D, S)
    lg, lb = di("lg", [128, 4]), di("lb", [128, 4])
    woutt1 = di("woutt1", [KC, 128, 8 * 128])
    w1t = [di("w1t_%d" % l, [2 * HCt, 128, KC * 128]) for l in range(depth)]
    w2t = [di("w2t_%d" % l, [KC, 128, HCt * 128]) for l in range(depth)]
    out = nc.dram_tensor("out", [KC, 128, TS], F32, kind="ExternalOutput").ap()
    part_rs = scr("part_rs", [G, KC, 128, TS])
    red = scr("red", [1, KC, 128, TS])
    xo = [scr("xo%d" % i, [KC, 128, TS]) for i in range(2)]
    rp = min(KC * 128, max(1, AG_MAX_BYTES // (TS * 4)))
    npc = KC * 128 // rp
    xg = scr("xg", [npc, G, rp, TS])
    ada_src = scr("ada_src", [128, 2 * nch])
    ada_all = scr("ada_all", [4 * 128, 2 * nch])
    mods_d = scr("mods_d", [depth * 6, 128, KC])
    NS = S // 512
    convo_s = scr("convo_s", [NS, 4 * 128, 512])
    callg = scr("callg", [NS, G * 4 * 128, 512])
    yout = scr("yout", [4, 128, S])

    P = Prog(nc)
    C = Common(P, D, TT)

    def part_put(m, t0, n):
        return part_rs[t0 // TS, m, :, t0 % TS:t0 % TS + n]

    def xg_get(k, t0, n):
        pc, r0 = divmod(k * 128, rp)
        return xg[pc, t0 // TS, r0:r0 + 128, t0 % TS:t0 % TS + n]

    def exchange(x_prev_own, x_new_own, ng_ap, gate_ap, gather=True):
        P.barrier()
        emit_cc(P, 'ReduceScatter', GRP4, part_rs.rearrange("g k p t -> (g k p) t"), red.rearrange("o k p t -> (o k p) t"))
        P.barrier()
        P.phase_begin()
        emit_reduce(P, C, red, x_prev_own, ng_ap, gate_ap, x_new_own, TS, 1)
        P.phase_end()
        if gather:
            src2d = x_new_own.rearrange("k p t -> (k p) t")
            for pc in range(npc):
                emit_cc(P, 'AllGather', GRP4, src2d[pc * rp:(pc + 1) * rp, :], xg[pc].rearrange("g r t -> (g r) t"), chain=False)
            P.barrier()

    P.phase_begin()
    c_sb = load_vec(P, 'c_sb', ct, KC * 2)
    b_sb = load_vec(P, 'b_sb', bt_ada, nch)
    sel_sb = load_vec(P, 'sel_sb', sel, 2)
    P.op('act', lambda e: e.activation(out=c_sb[:], in_=c_sb[:], func=AF.Silu), reads=['c_sb'], writes=['c_sb'])
    res = P.sb('res', [128, 2 * nch], F32)
    slots = [P.sb('aw%d' % i, [128, KC * 128], F32) for i in range(3)]
    for m in range(nch):
        w, wk = slots[m % 3], 'aw%d' % (m % 3)
        P.op('sp', lambda e, w=w, m=m: e.dma_start(out=w[:], in_=wt_ada[m]), writes=[wk], lane=wk)
        b = m % 8
        for k in range(KC):
            P.op('pe', lambda e, w=w, k=k, b=b: e.matmul(C.psum[b][:, 0:2], w[:, k * 128:(k + 1) * 128], c_sb[:, 2 * k:2 * k + 2], start=(k == 0), stop=(k == KC - 1)),
                 reads=[wk, 'c_sb'], writes=['ps%d' % b])
        for bb in range(2):
            P.op('dve', lambda e, m=m, b=b, bb=bb: e.tensor_scalar(out=res[:, bb * nch + m:bb * nch + m + 1], in0=C.psum[b][:, bb:bb + 1], scalar1=b_sb[:, m:m + 1],
                                                                  scalar2=None, op0=ALU.add), reads=['ps%d' % b, 'b_sb'], writes=['res'])
    P.op('sp', lambda e: e.dma_start(out=ada_src, in_=res[:]), reads=['res'], lane='res_st')
    P.barrier()
    emit_cc(P, 'AllGather', GRP4, ada_src, ada_all)
    P.barrier()
    vb = [P.sb('vb%d' % b, [128, depth * 6, KC], F32) for b in range(2)]
    vm_ = P.sb('vmix', [128, depth * 6, KC], F32)
    for v in range(depth * 6):
        gc0 = v * KC
        k0 = 0
        while k0 < KC:
            i, m0 = divmod(gc0 + k0, nch)
            ln = min(KC - k0, nch - m0)
            for b in range(2):
                P.op('sp', lambda e, b=b, v=v, k0=k0, ln=ln, i=i, m0=m0: e.dma_start(out=vb[b][:, v, k0:k0 + ln], in_=ada_all[i * 128:(i + 1) * 128, b * nch + m0:b * nch + m0 + ln]),
                     writes=['vb%d' % b], lane='vb%d_%d' % (b, (v * 2 + (1 if k0 else 0)) % 8))
            k0 += ln
    P.barrier()
    P.op('dve', lambda e: e.tensor_scalar(out=vm_[:], in0=vb[0][:], scalar1=sel_sb[:, 0:1], scalar2=None, op0=ALU.mult), writes=['vmix'])
    P.op('dve', lambda e: e.scalar_tensor_tensor(out=vm_[:], in0=vb[1][:], scalar=sel_sb[:, 1:2], in1=vm_[:], op0=ALU.mult, op1=ALU.add), reads=['vmix'], writes=['vmix'])
    P.op('sp', lambda e: e.dma_start(out=mods_d.rearrange("v p k -> p v k"), in_=vm_[:]), reads=['vmix'], lane='vmix_st')
    P.phase_end()

    cur = 0
    x_prev = x_own
    x_src = x_full
    for layer in range(depth):
        sh_m, sc_m, g_m, sh_f, sc_f, g_f = [mods_d[layer * 6 + i] for i in range(6)]
        if layer % 2 == 0:
            stage_mix0(P, C, nc, T0, x_src, ngall[layer * 4 + 0], sc_m, sh_m, part_put, S)
        else:
            def convo_put(c, t0, n):
                return convo_s[t0 // 512, c * 128:(c + 1) * 128, :]
            stage_mix1a(P, C, nc, T1, x_src, ngall[layer * 4 + 0], sc_m, sh_m, convo_put, yout, S)
            P.barrier()
            for sb_ in range(NS):
                emit_cc(P, 'AllGather', GRP4, convo_s[sb_], callg[sb_], chain=False)
            P.barrier()
            stage_mix1b(P, C, nc, lambda c, t0, n: callg[t0 // 512, c * 128:(c + 1) * 128, :],
                        lambda c, t0, n: convo_s[t0 // 512, c * 128:(c + 1) * 128, :], yout, lg, lb, woutt1, part_put, S)
        exchange(x_prev, xo[cur], ngall[layer * 4 + 1], g_m)
        x_prev, x_src = xo[cur], xg_get
        cur ^= 1
        P.phase_begin()
        emit_ffn(P, C, x_src, ngall[layer * 4 + 2], sc_f, sh_f, w1t[layer], w2t[layer], part_put, S, HCt)
        P.phase_end()
        last = (layer == depth - 1)
        exchange(x_prev, out if last else xo[cur], ngall[layer * 4 + 3], g_f, gather=not last)
        x_prev = xo[cur]
        cur ^= 1
    print('[fused] ops=%d sems=%d counts=%s' % (P.n_ops, len(P.sem), {k: v for k, v in P.cnt.items() if k in ENG_ATTR}), flush=True)
    P.finish()
    return nc


def kernel_fused(x, c, w_ada, b_ada, norm_g, w_ffn_in, w_ffn_out, even_w_in, even_w_out, odd_w_in, odd_w_out, conv_w, conv_b,
                 conv_ln_g, conv_ln_b, rwkv_mu, rwkv_w0, rwkv_w_up, rwkv_a0, rwkv_a_up, rwkv_g_up, rwkv_k_k, rwkv_k_a, rwkv_r_k,
                 rwkv_lnx_g, rwkv_lnx_b):
    f = lambda a: np.asarray(a, dtype=np.float32)
    x, c = f(x), f(c)
    B, S, D = x.shape
    KC = D // 128
    G = NCORE // B
    TS = S // G
    depth = w_ada.shape[0]
    FH = w_ffn_out.shape[1]
    HCt = -(-(FH // 128) // G)
    N6 = 6 * D
    per = depth * N6 // G
    nch = per // 128
    nc = build_fused(D, S, HCt, depth)
    w_ada, b_ada = f(w_ada), f(b_ada)
    x_fm = [fm(x[b], KC) for b in range(B)]
    ct = np.ascontiguousarray(c.T.reshape(KC, 128, B).transpose(1, 0, 2).reshape(128, KC * B))
    ngall = np.ascontiguousarray(np.stack([chunked(f(norm_g[l, i]), KC) for l in range(depth) for i in range(4)]))
    p = dict(odd_w_in=f(odd_w_in[0]), odd_w_out=f(odd_w_out[0]), conv_w=f(conv_w[0]), conv_b=f(conv_b[0]),
             conv_ln_g=f(conv_ln_g[0]), conv_ln_b=f(conv_ln_b[0]), rwkv_mu=f(rwkv_mu[0]), rwkv_w0=f(rwkv_w0[0]),
             rwkv_w_up=f(rwkv_w_up[0]), rwkv_a0=f(rwkv_a0[0]), rwkv_a_up=f(rwkv_a_up[0]), rwkv_g_up=f(rwkv_g_up[0]),
             rwkv_k_k=f(rwkv_k_k[0]), rwkv_k_a=f(rwkv_k_a[0]), rwkv_r_k=f(rwkv_r_k[0]), rwkv_lnx_g=f(rwkv_lnx_g[0]),
             rwkv_lnx_b=f(rwkv_lnx_b[0]))
    w_in0, w_out0 = f(even_w_in[0]), f(even_w_out[0])
    cm, sm = rope_np(S, 128)
    cr, sr = rope_np(S, 256)
    rope = np.ascontiguousarray(np.stack([np.concatenate([cm.T, cm.T]), np.concatenate([sm.T, sm.T]), cr.T, sr.T]).astype(np.float32))
    per_j = []
    for j in range(G):
        d = dict(wint=tile_w(w_in0[:, mix0_cols(j)], KC), woutt=tile_w(w_out0[mix0_rows(j), :], 12), rope=rope)
        d.update(moba_const_arrays(S))
        d.update(ret_const_arrays([2 * j, 2 * j + 1]))
        m1 = mix1a_inputs(None, np.zeros(D, np.float32), np.zeros(D, np.float32), np.zeros(D, np.float32), p, j, S)
        for k_ in ('x', 'ng', 'sc', 'shf'):
            m1.pop(k_)
        d.update(m1)
        mb = mix1b_inputs(None, None, None, p, j, D)
        d.update(lg=mb['lg'], lb=mb['lb'], woutt1=mb['woutt'])
        for l in range(depth):
            w1, w2 = f(w_ffn_in[l]), f(w_ffn_out[l])
            h0, h1 = j * HCt * 128, min((j + 1) * HCt * 128, FH)
            g_ = np.zeros((D, HCt * 128), np.float32)
            u_ = np.zeros((D, HCt * 128), np.float32)
            g_[:, :h1 - h0] = w1[:, h0:h1]
            u_[:, :h1 - h0] = w1[:, FH + h0:FH + h1]
            w1t = np.empty((2 * HCt, 128, KC * 128), np.float32)
            w1t[0::2] = tile_w(g_, KC)
            w1t[1::2] = tile_w(u_, KC)
            w2p = np.zeros((HCt * 128, D), np.float32)
            w2p[:h1 - h0] = w2[h0:h1]
            d['w1t_%d' % l] = w1t
            d['w2t_%d' % l] = tile_w(w2p, HCt)
        per_j.append(d)
    ims = []
    for i in range(NCORE):
        b, j = divmod(i, G)
        im = dict(per_j[j])
        g0 = j * per
        layer, col0 = divmod(g0, N6)
        selv = np.zeros((128, 2), np.float32)
        selv[:, b] = 1.0
        im.update(x_full=x_fm[b], x_own=np.ascontiguousarray(x_fm[b][:, :, j * TS:(j + 1) * TS]), sel=selv, ngall=ngall, ct=ct,
                  wt_ada=tile_w(w_ada[layer][:, col0:col0 + per], KC), bt_ada=chunked(b_ada[layer][col0:col0 + per], nch))
        ims.append(im)
    res = _run(nc, ims)
    out = np.empty((B, S, D), np.float32)
    for i in range(NCORE):
        b, j = divmod(i, G)
        out[b, j * TS:(j + 1) * TS, :] = res[i]["out"].reshape(D, TS).T
    return out


def kernel(**inputs):
    return kernel_fused(**inputs)
```
